# Optimizing a Trainium2 kernel written in Bass

```python
import math
import jax, jax.numpy as jnp
from jax import lax
import numpy as np

D_MODEL = 2048
BATCH = 16
SEQ = 256
DEPTH = 2
DEC_BATCH = 8
DEC_SEQ = 2048
PAST_LEN = 256

GRID_W = 64
N_MIXERS = 2
N_ATTN_LAYERS = (DEPTH + 1) // 2
N_SSM_LAYERS = DEPTH // 2
N_HEADS = 8
HEAD_DIM = 128
V_DIM = 2 * HEAD_DIM
ROPE_THETA = 10000.0
Q_BLOCK = 128
SSM_GROUP = 16
N_GROUPS = D_MODEL // SSM_GROUP
STATE_DIM = 64
D_FF = 5632
NORM_EPS = 1e-6
SUBLN_EPS = 1e-5
DT_MIN = 1e-3
DT_MAX = 1e-1

kernel_name = 'hybrid_diffattn_s5_prefix_step'

F32 = jnp.float32


def rms_norm(x, g, eps=NORM_EPS):
    xf = x.astype(F32)
    y = xf * lax.rsqrt(jnp.mean(xf * xf, axis=-1, keepdims=True) + eps)
    return (y * g.astype(F32)).astype(x.dtype)


def modulation(cond, w_mod, b_mod):
    m = jax.nn.silu(cond) @ w_mod + b_mod
    m = m.reshape(cond.shape[:-1] + (1, 6, D_MODEL))
    return [m[..., j, :] for j in range(6)]


def lambda_init_fn(layer):
    return 0.8 - 0.6 * math.exp(-0.3 * layer)


def axial_rope_tables(length):
    rows = length // GRID_W
    r, col = jnp.meshgrid(jnp.arange(rows), jnp.arange(GRID_W), indexing='ij')
    r = r.reshape(-1).astype(F32)
    col = col.reshape(-1).astype(F32)
    half = HEAD_DIM // 2
    inv = ROPE_THETA ** (-jnp.arange(0, half, 2, dtype=F32) / half)
    ar = r[:, None] * inv
    ac = col[:, None] * inv
    ang = jnp.concatenate([ar, ar, ac, ac], axis=-1)
    return jnp.cos(ang), jnp.sin(ang)


def apply_axial_rope(x, cos, sin):
    q4 = HEAD_DIM // 4
    xf = x.astype(F32)
    xr = xf.reshape(xf.shape[:-1] + (2, 2, q4))
    rot = jnp.stack([-xr[..., 1, :], xr[..., 0, :]], axis=-2).reshape(xf.shape)
    c = cos[None, :, None, None, :]
    s = sin[None, :, None, None, :]
    return (xf * c + rot * s).astype(x.dtype)


def diff_qkv(xm, w_qkv):
    b, l, _ = xm.shape
    q, k, v = jnp.split(xm @ w_qkv, 3, axis=-1)
    return (q.reshape(b, l, N_HEADS, 2, HEAD_DIM),
            k.reshape(b, l, N_HEADS, 2, HEAD_DIM),
            v.reshape(b, l, N_HEADS, V_DIM))


def diff_lambda(lv, lam_init):
    lv = lv.astype(F32)
    return jnp.exp(jnp.sum(lv[0] * lv[1])) - jnp.exp(jnp.sum(lv[2] * lv[3])) + lam_init


def diff_attention(q, k, v, lam):
    b, lq = q.shape[:2]
    nb = lq // Q_BLOCK
    qb = q.reshape(b, nb, Q_BLOCK, N_HEADS, 2, HEAD_DIM).transpose(1, 0, 2, 3, 4, 5)
    scale = HEAD_DIM ** -0.5

    def one_block(qblk):
        s = jnp.einsum('bqhcd,bkhcd->bhcqk', qblk, k).astype(F32) * scale
        p = jax.nn.softmax(s, axis=-1)
        w = p[:, :, 0] - lam * p[:, :, 1]
        return jnp.einsum('bhqk,bkhe->bqhe', w.astype(v.dtype), v)

    o = lax.map(one_block, qb)
    return o.transpose(1, 0, 2, 3, 4).reshape(b, lq, N_HEADS, V_DIM)


def diff_attn_out(o, subln_g, lam_init, w_o):
    b, l = o.shape[:2]
    o = rms_norm(o, subln_g, SUBLN_EPS) * (1.0 - lam_init)
    return o.reshape(b, l, N_HEADS * V_DIM) @ w_o


def cmul(ar, ai, br, bi):
    return ar * br - ai * bi, ar * bi + ai * br


def s5_discretize(a_re, a_im, log_step, b_re, b_im):
    dt = jnp.exp(log_step.astype(F32))[:, None]
    a_re = a_re.astype(F32)
    a_im = a_im.astype(F32)
    mag = jnp.exp(a_re * dt)
    lb_re = mag * jnp.cos(a_im * dt)
    lb_im = mag * jnp.sin(a_im * dt)
    num_re = lb_re - 1.0
    num_im = lb_im
    den = a_re * a_re + a_im * a_im
    f_re = (num_re * a_re + num_im * a_im) / den
    f_im = (num_im * a_re - num_re * a_im) / den
    bb_re, bb_im = cmul(f_re[..., None], f_im[..., None], b_re.astype(F32), b_im.astype(F32))
    return lb_re, lb_im, bb_re, bb_im


def s5_scan(u, lb_re, lb_im, bb_re, bb_im, c_re, c_im, h0, reverse):
    bu_re = jnp.einsum('blgc,gpc->blgp', u, bb_re)
    bu_im = jnp.einsum('blgc,gpc->blgp', u, bb_im)
    if reverse:
        bu_re = jnp.flip(bu_re, axis=1)
        bu_im = jnp.flip(bu_im, axis=1)
    if h0 is not None:
        ir, ii = cmul(lb_re, lb_im, h0[0].astype(F32), h0[1].astype(F32))
        bu_re = bu_re.at[:, 0].add(ir)
        bu_im = bu_im.at[:, 0].add(ii)
    l = u.shape[1]
    a_re = jnp.broadcast_to(lb_re, (1, l) + lb_re.shape)
    a_im = jnp.broadcast_to(lb_im, (1, l) + lb_im.shape)

    def combine(e1, e2):
        a1r, a1i, b1r, b1i = e1
        a2r, a2i, b2r, b2i = e2
        ar, ai = cmul(a2r, a2i, a1r, a1i)
        br, bi = cmul(a2r, a2i, b1r, b1i)
        return (ar, ai, br + b2r, bi + b2i)

    _, _, h_re, h_im = lax.associative_scan(combine, (a_re, a_im, bu_re, bu_im), axis=1)
    final = (h_re[:, -1], h_im[:, -1])
    if reverse:
        h_re = jnp.flip(h_re, axis=1)
        h_im = jnp.flip(h_im, axis=1)
    y = (jnp.einsum('blgp,gcp->blgc', h_re, c_re.astype(F32))
         - jnp.einsum('blgp,gcp->blgc', h_im, c_im.astype(F32)))
    return y, final


def s5_mixer(xm, a_re, a_im, log_step, b_re, b_im, c_re, c_im, d_skip, w_glu, b_glu, h0_re, h0_im):
    b, l, _ = xm.shape
    u = xm.reshape(b, l, N_GROUPS, SSM_GROUP)
    y = None
    fin_re = []
    fin_im = []
    for d in range(2):
        lb_re, lb_im, bb_re, bb_im = s5_discretize(a_re[d], a_im[d], log_step[d], b_re[d], b_im[d])
        h0 = None if h0_re is None else (h0_re[:, d], h0_im[:, d])
        yd, (fr, fi) = s5_scan(u, lb_re, lb_im, bb_re, bb_im, c_re[d], c_im[d], h0, d == 1)
        y = yd if y is None else y + yd
        fin_re.append(fr)
        fin_im.append(fi)
    z = y.reshape(b, l, D_MODEL) + d_skip.astype(F32) * xm.astype(F32)
    g = jax.nn.gelu(z)
    out = g * jax.nn.sigmoid(g @ w_glu.astype(F32) + b_glu.astype(F32))
    return out.astype(xm.dtype), jnp.stack(fin_re, axis=1), jnp.stack(fin_im, axis=1)


def conv_ffn(xm, w_up, conv_w, conv_b, w_down):
    h = xm @ w_up
    hp = jnp.pad(h, ((0, 0), (1, 1), (0, 0)))
    h = hp[:, :-2] * conv_w[0] + hp[:, 1:-1] * conv_w[1] + hp[:, 2:] * conv_w[2] + conv_b
    gate, val = jnp.split(h, 2, axis=-1)
    return (jax.nn.silu(gate) * val) @ w_down


def setup_inputs(seed: int = 0) -> dict:
    key = jax.random.key(seed)
    ks = jax.random.split(key, 32)
    nrm = jax.random.normal
    D = D_MODEL
    NA, NS = N_ATTN_LAYERS, N_SSM_LAYERS
    a_im0 = math.pi * jnp.arange(STATE_DIM, dtype=F32)
    return {
        'x_prompt': nrm(ks[0], (BATCH, SEQ, D), F32),
        'x_sample': nrm(ks[1], (DEC_BATCH, DEC_SEQ, D), F32),
        'cache_k': nrm(ks[2], (DEC_BATCH, NA, PAST_LEN, N_HEADS, 2, HEAD_DIM), F32),
        'cache_v': nrm(ks[3], (DEC_BATCH, NA, PAST_LEN, N_HEADS, V_DIM), F32),
        'state_re': 0.3 * nrm(ks[4], (DEC_BATCH, NS, 2, N_GROUPS, STATE_DIM), F32),
        'state_im': 0.3 * nrm(ks[5], (DEC_BATCH, NS, 2, N_GROUPS, STATE_DIM), F32),
        'c': nrm(ks[6], (DEC_BATCH, D), F32),
        'c_ctx': nrm(ks[7], (D,), F32),
        'w_mod': 0.5 * D ** -0.5 * nrm(ks[8], (DEPTH, D, 6 * D), F32),
        'b_mod': 0.01 * nrm(ks[9], (DEPTH, 6 * D), F32),
        'norm_g': 1.0 + 0.02 * nrm(ks[10], (DEPTH, 2, D), F32),
        'w_qkv': D ** -0.5 * nrm(ks[11], (NA, D, 3 * D), F32),
        'lam_vecs': 0.1 * nrm(ks[12], (NA, 4, HEAD_DIM), F32),
        'subln_g': 1.0 + 0.02 * nrm(ks[13], (NA, V_DIM), F32),
        'w_o': D ** -0.5 * nrm(ks[14], (NA, D, D), F32),
        'ssm_a_re': -0.5 * (1.0 + 0.02 * nrm(ks[15], (NS, 2, N_GROUPS, STATE_DIM), F32)),
        'ssm_a_im': a_im0 + 0.01 * nrm(ks[16], (NS, 2, N_GROUPS, STATE_DIM), F32),
        'ssm_log_step': jax.random.uniform(ks[17], (NS, 2, N_GROUPS), F32, math.log(DT_MIN), math.log(DT_MAX)),
        'ssm_b_re': (2 * SSM_GROUP) ** -0.5 * nrm(ks[18], (NS, 2, N_GROUPS, STATE_DIM, SSM_GROUP), F32),
        'ssm_b_im': (2 * SSM_GROUP) ** -0.5 * nrm(ks[19], (NS, 2, N_GROUPS, STATE_DIM, SSM_GROUP), F32),
        'ssm_c_re': (2 * STATE_DIM) ** -0.5 * nrm(ks[20], (NS, 2, N_GROUPS, SSM_GROUP, STATE_DIM), F32),
        'ssm_c_im': (2 * STATE_DIM) ** -0.5 * nrm(ks[21], (NS, 2, N_GROUPS, SSM_GROUP, STATE_DIM), F32),
        'ssm_d': nrm(ks[22], (NS, D), F32),
        'w_glu': D ** -0.5 * nrm(ks[23], (NS, D, D), F32),
        'b_glu': 0.01 * nrm(ks[24], (NS, D), F32),
        'w_up': D ** -0.5 * nrm(ks[25], (DEPTH, D, 2 * D_FF), F32),
        'conv_w': 3 ** -0.5 * nrm(ks[26], (DEPTH, 3, 2 * D_FF), F32),
        'conv_b': 0.01 * nrm(ks[27], (DEPTH, 2 * D_FF), F32),
        'w_down': D_FF ** -0.5 * nrm(ks[28], (DEPTH, D_FF, D), F32),
        'final_g': 1.0 + 0.02 * nrm(ks[29], (D,), F32),
    }


def reference(x_prompt, x_sample, cache_k, cache_v, state_re, state_im, c, c_ctx,
              w_mod, b_mod, norm_g, w_qkv, lam_vecs, subln_g, w_o,
              ssm_a_re, ssm_a_im, ssm_log_step, ssm_b_re, ssm_b_im, ssm_c_re, ssm_c_im,
              ssm_d, w_glu, b_glu, w_up, conv_w, conv_b, w_down, final_g):
    xp = x_prompt
    xs = x_sample
    cos, sin = axial_rope_tables(xs.shape[1])
    new_k, new_v, new_sre, new_sim = [], [], [], []
    for i in range(DEPTH):
        mix = i % N_MIXERS
        slot = i // N_MIXERS
        sh1p, sc1p, g1p, sh2p, sc2p, g2p = modulation(c_ctx, w_mod[i], b_mod[i])
        sh1s, sc1s, g1s, sh2s, sc2s, g2s = modulation(c, w_mod[i], b_mod[i])
        hp = rms_norm(xp, norm_g[i, 0]) * (1.0 + sc1p) + sh1p
        hs = rms_norm(xs, norm_g[i, 0]) * (1.0 + sc1s) + sh1s
        if mix == 0:
            lam_init = lambda_init_fn(i)
            lam = diff_lambda(lam_vecs[slot], lam_init)
            qp, kp, vp = diff_qkv(hp, w_qkv[slot])
            mp = diff_attn_out(diff_attention(qp, kp, vp, lam), subln_g[slot], lam_init, w_o[slot])
            new_k.append(kp)
            new_v.append(vp)
            qs, ks_, vs = diff_qkv(hs, w_qkv[slot])
            qs = apply_axial_rope(qs, cos, sin)
            ks_ = apply_axial_rope(ks_, cos, sin)
            k_all = jnp.concatenate([ks_, cache_k[:, slot].astype(ks_.dtype)], axis=1)
            v_all = jnp.concatenate([vs, cache_v[:, slot].astype(vs.dtype)], axis=1)
            ms = diff_attn_out(diff_attention(qs, k_all, v_all, lam), subln_g[slot], lam_init, w_o[slot])
        else:
            mp, fre, fim = s5_mixer(hp, ssm_a_re[slot], ssm_a_im[slot], ssm_log_step[slot],
                                    ssm_b_re[slot], ssm_b_im[slot], ssm_c_re[slot], ssm_c_im[slot],
                                    ssm_d[slot], w_glu[slot], b_glu[slot], None, None)
            new_sre.append(fre)
            new_sim.append(fim)
            ms, _, _ = s5_mixer(hs, ssm_a_re[slot], ssm_a_im[slot], ssm_log_step[slot],
                                ssm_b_re[slot], ssm_b_im[slot], ssm_c_re[slot], ssm_c_im[slot],
                                ssm_d[slot], w_glu[slot], b_glu[slot],
                                state_re[:, slot], state_im[:, slot])
        xp = xp + (g1p * mp).astype(xp.dtype)
        xs = xs + (g1s * ms).astype(xs.dtype)
        hp = rms_norm(xp, norm_g[i, 1]) * (1.0 + sc2p) + sh2p
        hs = rms_norm(xs, norm_g[i, 1]) * (1.0 + sc2s) + sh2s
        xp = xp + (g2p * conv_ffn(hp, w_up[i], conv_w[i], conv_b[i], w_down[i])).astype(xp.dtype)
        xs = xs + (g2s * conv_ffn(hs, w_up[i], conv_w[i], conv_b[i], w_down[i])).astype(xs.dtype)
    y_prompt = rms_norm(xp, final_g)
    y_sample = rms_norm(xs, final_g)
    new_cache_k = jnp.stack(new_k, axis=1)
    new_cache_v = jnp.stack(new_v, axis=1)
    new_state_re = jnp.stack(new_sre, axis=1)
    new_state_im = jnp.stack(new_sim, axis=1)
    return (y_prompt, y_sample, new_cache_k, new_cache_v, new_state_re, new_state_im)
```

```python
import math
import os
from contextlib import ExitStack

import numpy as np
import concourse.bass as bass
import concourse.mybir as mybir
from concourse.bass_utils import run_bass_kernel_spmd

F32 = mybir.dt.float32
BF16 = mybir.dt.bfloat16
I32 = mybir.dt.int32
ALU = mybir.AluOpType
AF = mybir.ActivationFunctionType

NCORES = 8
D = 2048
KC = 16
T = 2560
NT = 5
TW = 512
DFF = 5632
JC = 44
NH = 8
SEQS = [(0, 2048), (2048, 256), (2304, 256)]
NORM_EPS = 1e-6
SUBLN_EPS = 1e-5
LAM_INIT0 = 0.8 - 0.6 * math.exp(-0.3 * 0)
ATT_SCALE = 128 ** -0.5
CPAD = [1, 2051, 2309]
CW_TOT = 2566


class Buf:
    __slots__ = ("name", "t", "lw", "rd", "psum")

    def __init__(self, name, t=None, psum=False):
        self.name = name
        self.t = t
        self.lw = None
        self.rd = {}
        self.psum = psum


class MK:
    def __init__(self, debug=False, stop_after=None):
        self.debug = debug
        self.stop_after = stop_after
        self.nc = bass.Bass("TRN2", target_bir_lowering=False)
        nc = self.nc
        self.es = ExitStack()
        self.engs = {"pe": nc.tensor, "act": nc.scalar, "dve": nc.vector, "pool": nc.gpsimd, "sp": nc.sync}
        self.sems = {}
        self.cnt = {}
        for k in self.engs:
            self.sems["e_" + k] = self.es.enter_context(nc.semaphore("e_" + k))
            self.cnt["e_" + k] = 0
        self.ndq = {"sp": 14, "pool": 14}
        self.drr = {"sp": 0, "pool": 0}
        for q, n in self.ndq.items():
            for i in range(n):
                key = f"d_{q}_{i}"
                self.sems[key] = self.es.enter_context(nc.semaphore(key))
                self.cnt[key] = 0
        self.waited = {}
        self.ps = []
        for i in range(8):
            t = self.es.enter_context(nc.psum_tensor(f"ps{i}", [128, 512], F32))
            self.ps.append(Buf(f"ps{i}", t, psum=True))
        self.psrr = 0
        self.dram_in = {}
        self.dram_out = {}

    def cur(self, key):
        return self.cnt[key] * (16 if key.startswith("d_") else 1)

    def wait(self, eng, ev):
        if ev is None:
            return
        key, val = ev
        if val <= 0:
            return
        if eng == "pe" and key == "e_pe":
            return
        if self.waited.get((eng, key), 0) >= val:
            return
        self.engs[eng].wait_ge(self.sems[key], val)
        self.waited[(eng, key)] = val

    def _deps(self, eng, r, w):
        for b in r:
            self.wait(eng, b.lw)
            if b.psum:
                for k, v in b.rd.items():
                    if k != "e_" + eng:
                        self.wait(eng, (k, v))
        for b in w:
            self.wait(eng, b.lw)
            for k, v in b.rd.items():
                self.wait(eng, (k, v))

    def _record(self, ev, r, w):
        for b in r:
            if b.rd.get(ev[0], 0) < ev[1]:
                b.rd[ev[0]] = ev[1]
        for b in w:
            b.lw = ev
            b.rd = {}

    def op(self, eng, fn, r=(), w=(), sig=True):
        self._deps(eng, r, w)
        ins = fn(self.engs[eng])
        key = "e_" + eng
        if sig:
            self.cnt[key] += 1
            ins.then_inc(self.sems[key], 1)
            ev = (key, self.cnt[key])
        else:
            ev = (key, self.cnt[key] + 1)
        self._record(ev, r, w)
        return ins

    def dma(self, q, out, in_, r=(), w=(), **kw):
        self._deps(q, r, w)
        idx = self.drr[q] % self.ndq[q]
        self.drr[q] += 1
        key = f"d_{q}_{idx}"
        self.wait(q, (key, self.cur(key)))
        ins = self.engs[q].dma_start(out=out, in_=in_, **kw)
        ins.then_inc(self.sems[key], 16)
        self.cnt[key] += 1
        ev = (key, self.cur(key))
        self._record(ev, r, w)

    def barrier(self):
        for eng in self.engs:
            for key in self.sems:
                self.wait(eng, (key, self.cur(key)))

    def dbg(self, name, buf, ap, shape, dtype=F32):
        if not self.debug:
            return
        d = self.dout("dbg_" + name, shape, dtype)
        self.dma("sp", d, ap, r=[buf], w=[self.OUTb])

    def nextps(self):
        p = self.ps[self.psrr % 8]
        self.psrr += 1
        return p

    def sb(self, st, name, shape, dtype):
        self.uid = getattr(self, "uid", 0) + 1
        name = f"{name}_{self.uid}"
        t = st.enter_context(self.nc.sbuf_tensor(name, list(shape), dtype))
        return Buf(name, t)

    def din(self, name, shape, dtype=F32):
        h = self.nc.dram_tensor(name, list(shape), dtype, kind="ExternalInput")
        self.dram_in[name] = h
        return h.ap()

    def dout(self, name, shape, dtype=F32):
        h = self.nc.dram_tensor(name, list(shape), dtype, kind="ExternalOutput")
        self.dram_out[name] = h
        return h.ap()

    def dscr(self, name, shape, dtype):
        if self.debug:
            return self.dout(name, shape, dtype)
        return self.nc.dram_tensor(name, list(shape), dtype).ap()

    def load_fm(self, st_name, src2d, R, dst_ap, dstbuf, srcbuf):
        stg = self.fm_stage[self.fm_i % 2]
        self.fm_i += 1
        self.dma("sp", stg.t[0:R, :], src2d, r=[srcbuf], w=[stg])
        ps = self.nextps()
        self.op("pe", lambda e: e.transpose(out=ps.t[:, 0:R], in_=stg.t[0:R, :], identity=self.ident.t[0:R, 0:R]),
                r=[stg, self.ident], w=[ps])
        self.op("dve", lambda e: e.tensor_copy(out=dst_ap, in_=ps.t[:, 0:R]), r=[ps], w=[dstbuf])

    def build(self):
        nc = self.nc
        es = self.es
        self.x_all = self.din("x_all", [T, D])
        self.cond = self.din("cond", [2, D])
        self.cache_k = self.din("cache_k", [256, 2048])
        self.cache_v = self.din("cache_v", [256, 2048])
        self.st_re = self.din("st_re", [2, 128, 64])
        self.st_im = self.din("st_im", [2, 128, 64])
        self.w_mod = self.din("w_mod", [2, D, 6 * D])
        self.b_mod = self.din("b_mod", [2, 6 * D])
        self.norm_g = self.din("norm_g", [2, 2, D])
        self.w_qkv = self.din("w_qkv", [D, 3 * D])
        self.lam_vecs = self.din("lam_vecs", [4, 128])
        self.subln_g = self.din("subln_g", [256])
        self.w_o = self.din("w_o", [D, D])
        self.ssm_a_re = self.din("ssm_a_re", [2, 128, 64])
        self.ssm_a_im = self.din("ssm_a_im", [2, 128, 64])
        self.ssm_log_step = self.din("ssm_log_step", [2, 128])
        self.ssm_b_re = self.din("ssm_b_re", [2, 128, 64, 16])
        self.ssm_b_im = self.din("ssm_b_im", [2, 128, 64, 16])
        self.ssm_c_re = self.din("ssm_c_re", [2, 128, 16, 64])
        self.ssm_c_im = self.din("ssm_c_im", [2, 128, 16, 64])
        self.ssm_d = self.din("ssm_d", [D])
        self.w_glu = self.din("w_glu", [D, D])
        self.b_glu = self.din("b_glu", [D])
        self.w_up = self.din("w_up", [2, D, 2 * DFF])
        self.conv_w = self.din("conv_w", [2, 3, 2 * DFF])
        self.conv_b = self.din("conv_b", [2, 2 * DFF])
        self.w_down = self.din("w_down", [2, DFF, D])
        self.final_g = self.din("final_g", [D])
        self.c_ident = self.din("c_ident", [128, 128])
        self.c_ropeT = self.din("c_ropeT", [128, 128])
        self.c_cos = self.din("c_cos", [128, 2048])
        self.c_sin = self.din("c_sin", [128, 2048])
        self.c_sel = self.din("c_sel", [128, 8 * 240])
        self.c_maskf = self.din("c_maskf", [128, 128])
        self.c_maskb = self.din("c_maskb", [128, 128])

        self.y_all = self.dout("y_all", [T, D])
        self.new_k = self.dout("new_k", [2, 256, 2048])
        self.new_v = self.dout("new_v", [2, 256, 2048])
        self.new_sre = self.dout("new_sre", [256, 128])
        self.new_sim = self.dout("new_sim", [256, 128])

        self.XT = self.dscr("XT", [KC, 128, T], F32)
        self.OT = self.dscr("OT", [KC, 128, T], BF16)
        self.AT = self.dscr("AT", [JC, 128, T], BF16)
        self.YT = self.dscr("YT", [KC, 128, T], F32)
        self.INb = Buf("inputs")
        self.XTb = [Buf(f"XT{i}") for i in range(NT)]
        self.OTb = [Buf(f"OT{i}") for i in range(NT)]
        self.ATb = [Buf(f"AT{i}") for i in range(NT)]
        self.YTb = [Buf(f"YT{i}") for i in range(NT)]
        self.OUTb = Buf("outputs")

        self.ident = self.sb(es, "ident", [128, 128], F32)
        self.onesb = self.sb(es, "onesb", [128, 128], BF16)
        self.onesf = self.sb(es, "onesf", [128, 128], F32)
        self.cst = self.sb(es, "cst", [128, 8], F32)
        self.MOD = self.sb(es, "MOD", [128, 2 * 96 * 2], F32)
        self.AV = self.sb(es, "AV", [128, 2 * 2 * 16 * 2], F32)
        self.AFN = self.sb(es, "AFN", [128, 16], F32)
        self.RSTD = None
        self.RSD = self.dscr("RSD", [128, T], F32)
        self.RSDb = Buf("RSD")
        self.HT = None
        self.HTb = [Buf(f"HT{i}") for i in range(NT)]
        self.ht_scope = None
        self.fm_stage = [self.sb(es, f"fmst{i}", [128, 128], F32) for i in range(2)]
        self.fm_i = 0

        self.dma("sp", self.ident.t[:], self.c_ident, r=[self.INb], w=[self.ident])
        self.op("dve", lambda e: e.memset(self.onesb.t[:], 1.0), w=[self.onesb])
        self.op("dve", lambda e: e.memset(self.onesf.t[:], 1.0), w=[self.onesf])
        self.op("dve", lambda e: e.memset(self.cst.t[:, 0:1], NORM_EPS), w=[self.cst])
        self.op("dve", lambda e: e.memset(self.cst.t[:, 1:2], SUBLN_EPS), w=[self.cst])
        self.op("dve", lambda e: e.memset(self.cst.t[:, 2:3], 0.0), w=[self.cst])
        self.op("dve", lambda e: e.memset(self.cst.t[:, 3:4], 1.0), w=[self.cst])

        phases = [
            ("mod", self.phase_mod),
            ("loadx", lambda: (self.ht_open(), self.phase_loadx())),
            ("attn", lambda: (self.phase_attn(), self.ht_close())),
            ("wo", lambda: self.phase_proj(self.OT, self.OTb, KC, self.w_o, 0, 0)),
            ("norm02", lambda: (self.ht_open(), self.phase_norm(0, 1))),
            ("up0", lambda: (self.phase_up(0), self.ht_close())),
            ("down0", lambda: self.phase_proj(self.AT, self.ATb, JC, self.w_down[0], 0, 1)),
            ("norm11", lambda: (self.ht_open(), self.phase_norm(1, 0))),
            ("s5", lambda: (self.phase_s5(), self.ht_close())),
            ("glu", self.phase_glu),
            ("norm12", lambda: (self.ht_open(), self.phase_norm(1, 1))),
            ("up1", lambda: (self.phase_up(1), self.ht_close())),
            ("down1", lambda: self.phase_proj(self.AT, self.ATb, JC, self.w_down[1], 1, 1)),
            ("final", self.phase_final),
        ]
        for name, fn in phases:
            if name in os.environ.get('MK_SKIP', '').split(','):
                continue
            fn()
            self.barrier()
            if self.stop_after == name:
                if self.ht_scope is not None:
                    self.ht_close()
                break
        self.barrier()
        return nc

    def ht_open(self):
        self.ht_scope = ExitStack()
        self.HT = self.sb(self.ht_scope, "HT", [128, KC * T], BF16)
        self.HTb = [Buf(f"HT{i}") for i in range(NT)]

    def ht_close(self):
        self.barrier()
        self.ht_scope.close()
        self.ht_scope = None
        self.HT = None

    def modv(self, l, j, kc, n):
        c = ((l * 96 + j * 16 + kc) * 2 + n)
        return self.MOD.t[:, c:c + 1]

    def av(self, l, s, kc, n):
        c = (((l * 2 + s) * 16 + kc) * 2 + n)
        return self.AV.t[:, c:c + 1]

    def phase_mod(self):
        with ExitStack() as ph:
            condT = self.sb(ph, "condT", [128, 32], F32)
            scT = self.sb(ph, "scT", [128, 32], BF16)
            bmT = self.sb(ph, "bmT", [128, 192], F32)
            gT = self.sb(ph, "gT", [128, 64], F32)
            wblk = [self.sb(ph, f"wmod{i}", [128, 16 * 512], BF16) for i in range(2)]
            self.load_fm("cond", self.cond.rearrange("n (k p) -> (n k) p", p=128), 32, condT.t[:, :], condT, self.INb)
            self.op("act", lambda e: e.activation(out=scT.t[:, :], in_=condT.t[:, :], func=AF.Silu), r=[condT], w=[scT])
            for l in range(2):
                self.load_fm("bm", self.b_mod[l].rearrange("(m p) -> m p", p=128), 96, bmT.t[:, l * 96:(l + 1) * 96], bmT, self.INb)
            self.load_fm("ng", self.norm_g.rearrange("l s (k p) -> (l s k) p", p=128), 64, gT.t[:, :], gT, self.INb)
            self.load_fm("fg", self.final_g.rearrange("(k p) -> k p", p=128), 16, self.AFN.t[:, :], self.AFN, self.INb)
            i = 0
            for l in range(2):
                for blk in range(24):
                    wb = wblk[i % 2]
                    i += 1
                    self.dma("pool", wb.t[:, :].rearrange("p (k m) -> p k m", k=16),
                             self.w_mod[l][:, blk * 512:(blk + 1) * 512].rearrange("(k p) m -> p k m", p=128),
                             r=[self.INb], w=[wb])
                    ps = self.nextps()
                    for m4 in range(4):
                        for kc in range(16):
                            self.op("pe", lambda e, m4=m4, kc=kc: e.matmul(
                                ps.t[:, m4 * 2:(m4 + 1) * 2], lhsT=wb.t[:, kc * 512 + m4 * 128: kc * 512 + (m4 + 1) * 128],
                                rhs=scT.t[:, kc:32:16], start=(kc == 0), stop=(kc == 15)),
                                r=[wb, scT], w=[ps], sig=(kc == 15))
                    for n in range(2):
                        base = (l * 96 + blk * 4) * 2 + n
                        self.op("dve", lambda e, n=n, base=base: e.tensor_tensor(
                            out=self.MOD.t[:, base:base + 7:2], in0=ps.t[:, n:8:2],
                            in1=bmT.t[:, l * 96 + blk * 4: l * 96 + blk * 4 + 4], op=ALU.add),
                            r=[ps, bmT], w=[self.MOD])
            for l in range(2):
                for s in range(2):
                    for n in range(2):
                        j = 1 + 3 * s
                        mb = (l * 96 + j * 16) * 2 + n
                        ab = ((l * 2 + s) * 16) * 2 + n
                        self.op("dve", lambda e, mb=mb, ab=ab, l=l, s=s: e.scalar_tensor_tensor(
                            out=self.AV.t[:, ab:ab + 31:2], in0=self.MOD.t[:, mb:mb + 31:2], scalar=1.0,
                            in1=gT.t[:, (l * 2 + s) * 16:(l * 2 + s + 1) * 16], op0=ALU.add, op1=ALU.mult),
                            r=[self.MOD, gT], w=[self.AV])
            self.dbg("MOD", self.MOD, self.MOD.t[:, :], [128, 384])
            self.dbg("AV", self.AV, self.AV.t[:, :], [128, 128])
            self.barrier()

    def norm_tile(self, xt, tt, l, s, sq, rs, tmp, final_out=None):
        n = 0 if tt < 4 else 1
        tok = slice(tt * TW, (tt + 1) * TW)
        self.op("act", lambda e: e.activation(out=sq.t[:, :], in_=xt.t[:, :], func=AF.Square), r=[xt], w=[sq])
        ps = self.nextps()
        for kc in range(KC):
            self.op("pe", lambda e, kc=kc: e.matmul(ps.t[:, :], lhsT=self.onesb.t[:, :], rhs=sq.t[:, kc * TW:(kc + 1) * TW],
                                                  start=(kc == 0), stop=(kc == KC - 1)),
                    r=[sq, self.onesb], w=[ps], sig=(kc == KC - 1))
        self.op("act", lambda e: e.activation(out=rs.t[:, :], in_=ps.t[:, :], func=AF.Sqrt, scale=1.0 / D, bias=self.cst.t[:, 0:1]),
                r=[ps, self.cst], w=[rs])
        self.op("dve", lambda e: e.reciprocal(out=self.RSTD.t[:, tok], in_=rs.t[:, :]), r=[rs], w=[self.RSTD])
        for kc in range(KC):
            if final_out is None:
                tb = tmp[kc % 2]
                a = self.av(l, s, kc, n)
                b = self.modv(l, 3 * s, kc, n)
                self.op("dve", lambda e, kc=kc, tb=tb, a=a: e.scalar_tensor_tensor(
                    out=tb.t[:, :], in0=xt.t[:, kc * TW:(kc + 1) * TW], scalar=a, in1=self.RSTD.t[:, tok],
                    op0=ALU.mult, op1=ALU.mult), r=[xt, self.RSTD, self.AV], w=[tb])
                self.op("act", lambda e, kc=kc, tb=tb, b=b: e.activation(
                    out=self.HT.t[:, kc * T + tt * TW: kc * T + (tt + 1) * TW], in_=tb.t[:, :], func=AF.Identity, bias=b, scale=1.0),
                    r=[tb, self.MOD], w=[self.HTb[tt]])
            else:
                self.op("dve", lambda e, kc=kc: e.scalar_tensor_tensor(
                    out=final_out.t[:, kc * TW:(kc + 1) * TW], in0=xt.t[:, kc * TW:(kc + 1) * TW], scalar=self.AFN.t[:, kc:kc + 1],
                    in1=self.RSTD.t[:, tok], op0=ALU.mult, op1=ALU.mult), r=[xt, self.RSTD, self.AFN], w=[final_out])

    def xt_dram(self, tt, k0=0, k1=KC):
        return self.XT[k0:k1, :, tt * TW:(tt + 1) * TW].rearrange("k p t -> p k t")

    def phase_loadx(self):
        with ExitStack() as ph:
            xin = [self.sb(ph, f"xin{i}", [128, D], F32) for i in range(2)]
            xtile = [self.sb(ph, f"xtile{i}", [128, KC * TW], F32) for i in range(2)]
            self.RSTD = self.sb(ph, "RSTD", [128, T], F32)
            sq = self.sb(ph, "sq", [128, KC * TW], BF16)
            rs = self.sb(ph, "rs", [128, TW], F32)
            tmp = [self.sb(ph, f"ntmp{i}", [128, TW], F32) for i in range(2)]
            for tt in range(NT):
                xt = xtile[tt % 2]
                for b4 in range(4):
                    tb = tt * 4 + b4
                    xi = xin[tb % 2]
                    self.dma("sp", xi.t[:, :], self.x_all[tb * 128:(tb + 1) * 128, :], r=[self.INb], w=[xi])
                    for kq in range(4):
                        ps = self.nextps()
                        for k4 in range(4):
                            kc = kq * 4 + k4
                            self.op("pe", lambda e, kc=kc, k4=k4: e.transpose(
                                out=ps.t[:, k4 * 128:(k4 + 1) * 128], in_=xi.t[:, kc * 128:(kc + 1) * 128], identity=self.ident.t[:, :]),
                                r=[xi, self.ident], w=[ps], sig=(k4 == 3))
                        for k4 in range(4):
                            kc = kq * 4 + k4
                            eng = "act" if k4 % 2 == 0 else "dve"
                            if eng == "act":
                                self.op("act", lambda e, kc=kc, k4=k4: e.activation(
                                    out=xt.t[:, kc * TW + b4 * 128: kc * TW + (b4 + 1) * 128], in_=ps.t[:, k4 * 128:(k4 + 1) * 128],
                                    func=AF.Identity), r=[ps], w=[xt])
                            else:
                                self.op("dve", lambda e, kc=kc, k4=k4: e.tensor_copy(
                                    out=xt.t[:, kc * TW + b4 * 128: kc * TW + (b4 + 1) * 128], in_=ps.t[:, k4 * 128:(k4 + 1) * 128]),
                                    r=[ps], w=[xt])
                self.dma("sp", self.xt_dram(tt), xt.t[:, :].rearrange("p (k t) -> p k t", k=KC), r=[xt], w=[self.XTb[tt]])
                self.norm_tile(xt, tt, 0, 0, sq, rs, tmp)
            self.dbg("HT0", self.HTb[4], self.HT.t[:, 0:2 * T], [128, 2 * T], BF16)
            self.barrier()

    def phase_norm(self, l, s):
        with ExitStack() as ph:
            xtile = [self.sb(ph, f"xtile{i}", [128, KC * TW], F32) for i in range(2)]
            self.RSTD = self.sb(ph, "RSTD", [128, T], F32)
            sq = self.sb(ph, "sq", [128, KC * TW], BF16)
            rs = self.sb(ph, "rs", [128, TW], F32)
            tmp = [self.sb(ph, f"ntmp{i}", [128, TW], F32) for i in range(2)]
            for tt in range(NT):
                xt = xtile[tt % 2]
                self.dma("sp", xt.t[:, :].rearrange("p (k t) -> p k t", k=KC), self.xt_dram(tt), r=[self.XTb[tt]], w=[xt])
                self.norm_tile(xt, tt, l, s, sq, rs, tmp)
            if l == 1 and s == 0:
                self.dma("sp", self.RSD, self.RSTD.t[:, :], r=[self.RSTD], w=[self.RSDb])
            self.barrier()

    def phase_attn(self):
        HT = self.HT
        with ExitStack() as ph:
            Wp = [self.sb(ph, f"Wh{i}", [128, KC * 256], BF16) for i in range(3)]
            QT = self.sb(ph, "QT", [128, 2 * T], BF16)
            KT = self.sb(ph, "KT", [128, 2 * 2816], BF16)
            VS = 264
            Vx = self.sb(ph, "Vx", [128, 22 * VS], BF16)
            cosT = self.sb(ph, "cosT", [128, 2048], F32)
            sinT = self.sb(ph, "sinT", [128, 2048], F32)
            ropeT = self.sb(ph, "ropeT", [128, 128], BF16)
            xqb = [self.sb(ph, f"xqb{i}", [128, TW], BF16) for i in range(2)]
            lamb = self.sb(ph, "lamb", [128, 4], BF16)
            CKh = self.sb(ph, "CKh", [128, 2 * 256], F32)
            xq = [self.sb(ph, f"xq{i}", [128, TW], F32) for i in range(2)]
            t1 = [self.sb(ph, f"t1{i}", [128, TW], F32) for i in range(1)]
            t2 = [self.sb(ph, f"t2{i}", [128, TW], F32) for i in range(1)]
            PTb = [self.sb(ph, f"PT{i}", [128, TW], BF16) for i in range(4)]
            r12 = [self.sb(ph, f"r12{i}", [128, TW], F32) for i in range(2)]
            Dh = [self.sb(ph, f"Dh{i}", [128, TW], F32) for i in range(2)]
            o1 = self.sb(ph, "o1", [128, TW], F32)
            sqd = [self.sb(ph, f"sqd{i}", [128, TW], BF16) for i in range(2)]
            rsd = self.sb(ph, "rsd", [128, TW], F32)
            ost = [self.sb(ph, f"ost{i}", [128, TW], BF16) for i in range(2)]
            kvst = [self.sb(ph, f"kvst{i}", [128, 256], F32) for i in range(2)]
            lamt = self.sb(ph, "lamt", [128, 8], F32)
            gsub = self.sb(ph, "gsub", [128, 2], F32)

            self.dma("sp", cosT.t[:, :], self.c_cos, r=[self.INb], w=[cosT])
            self.dma("sp", sinT.t[:, :], self.c_sin, r=[self.INb], w=[sinT])
            self.dma("pool", ropeT.t[:, :], self.c_ropeT, r=[self.INb], w=[ropeT])
            for blk in range(22):
                self.op("dve", lambda e, blk=blk: e.memset(Vx.t[:, blk * VS + 256: blk * VS + 257], 1.0), w=[Vx])
            stage = int(os.environ.get('ATT_STAGE', 9))
            if stage < 1:
                return
            self.load_fm("lv", self.lam_vecs, 4, lamt.t[:, 0:4], lamt, self.INb)
            self.op("dve", lambda e: e.tensor_tensor(out=lamt.t[:, 4:6], in0=lamt.t[:, 0:4:2], in1=lamt.t[:, 1:4:2], op=ALU.mult),
                    r=[lamt], w=[lamt])
            self.op("dve", lambda e: e.tensor_copy(out=lamb.t[:, 0:2], in_=lamt.t[:, 4:6]), r=[lamt], w=[lamb])
            self.op("dve", lambda e: e.tensor_tensor(out=lamb.t[:, 2:4], in0=lamt.t[:, 4:6], in1=lamb.t[:, 0:2], op=ALU.subtract), r=[lamt, lamb], w=[lamb])
            psl = self.nextps()
            self.op("pe", lambda e: e.matmul(psl.t[:, 0:4], lhsT=self.onesb.t[:, :], rhs=lamb.t[:, 0:4], start=True, stop=True),
                    r=[lamb, self.onesb], w=[psl])
            self.op("dve", lambda e: e.tensor_copy(out=lamt.t[:, 0:4], in_=psl.t[:, 0:4]), r=[psl], w=[lamt])
            self.op("dve", lambda e: e.tensor_tensor(out=lamt.t[:, 4:6], in0=lamt.t[:, 0:2], in1=lamt.t[:, 2:4], op=ALU.add), r=[lamt], w=[lamt])
            self.op("act", lambda e: e.activation(out=lamt.t[:, 6:8], in_=lamt.t[:, 4:6], func=AF.Exp), r=[lamt], w=[lamt])
            self.op("dve", lambda e: e.tensor_tensor(out=lamt.t[:, 4:5], in0=lamt.t[:, 6:7], in1=lamt.t[:, 7:8], op=ALU.subtract),
                    r=[lamt], w=[lamt])
            self.op("dve", lambda e: e.tensor_scalar(out=lamt.t[:, 5:6], in0=lamt.t[:, 4:5], scalar1=LAM_INIT0, scalar2=1.0,
                                                     op0=ALU.add, op1=ALU.mult), r=[lamt], w=[lamt])
            LAM = lamt.t[:, 5:6]
            if stage < 2:
                return
            self.load_fm("sg", self.subln_g.rearrange("(k p) -> k p", p=128), 2, gsub.t[:, 0:2], gsub, self.INb)
            self.op("dve", lambda e: e.tensor_scalar(out=gsub.t[:, 0:2], in0=gsub.t[:, 0:2], scalar1=(1.0 - LAM_INIT0), scalar2=1.0,
                                                     op0=ALU.mult, op1=ALU.mult), r=[gsub], w=[gsub])
            if stage < 3:
                return
            kvi = 0
            osti = 0
            pti = 0
            skip = os.environ.get('ATT_SKIP', '').split(',')
            def load_head_w(hh):
                for part in range(3):
                    self.dma("pool", Wp[part].t[:, :].rearrange("p (k m) -> p k m", k=KC),
                             self.w_qkv[:, part * 2048 + hh * 256: part * 2048 + (hh + 1) * 256].rearrange("(k p) m -> p k m", p=128),
                             r=[self.INb], w=[Wp[part]])

            NHEADS = int(os.environ.get('ATT_HEADS', NH))
            load_head_w(0)
            for h in range(NHEADS):
                for b in range(2):
                    self.dma("sp", CKh.t[:, b * 256:(b + 1) * 256], self.cache_k[b * 128:(b + 1) * 128, h * 256:(h + 1) * 256],
                             r=[self.INb], w=[CKh])
                for b in range(2):
                    self.dma("pool", Vx.t[:, (16 + b) * VS: (16 + b) * VS + 256],
                             self.cache_v[b * 128:(b + 1) * 128, h * 256:(h + 1) * 256], r=[self.INb], w=[Vx])
                ri = 0
                pending = []
                for which in range(0 if 'qk' in skip else 2):
                    for c in range(2):
                        for tt in [int(x) for x in os.environ.get('ATT_TT', '0,1,2,3,4').split(',')]:
                            ps = self.nextps()
                            for kc in range(KC):
                                self.op("pe", lambda e, kc=kc: e.matmul(
                                    ps.t[:, :], lhsT=Wp[which].t[:, kc * 256 + c * 128: kc * 256 + (c + 1) * 128],
                                    rhs=HT.t[:, kc * T + tt * TW: kc * T + (tt + 1) * TW], start=(kc == 0), stop=(kc == KC - 1)),
                                    r=[Wp[which], self.HTb[tt]], w=[ps], sig=(kc == KC - 1))
                            while pending:
                                pending.pop(0)()
                            if which == 0:
                                dst = QT.t[:, c * T + tt * TW: c * T + (tt + 1) * TW]
                                dstb = QT
                            else:
                                off = tt * TW if tt < 4 else 2304
                                dst = KT.t[:, c * 2816 + off: c * 2816 + off + TW]
                                dstb = KT
                            if tt < 4 and os.environ.get('ATT_NOROPE', '0') == '0':
                                xb = xq[ri % 2]
                                a1 = t1[0]
                                a2 = t2[0]
                                ri += 1
                                self.op("act", lambda e, xb=xb: e.activation(out=xb.t[:, :], in_=ps.t[:, :], func=AF.Identity), r=[ps], w=[xb])
                                xbb = xqb[ri % 2]
                                self.op("dve", lambda e, xbb=xbb: e.tensor_copy(out=xbb.t[:, :], in_=ps.t[:, :]), r=[ps], w=[xbb])
                                def rope_tail(xb=xb, xbb=xbb, a1=a1, a2=a2, dst=dst, dstb=dstb, tt=tt):
                                    ps2 = self.nextps()
                                    self.op("pe", lambda e: e.matmul(ps2.t[:, :], lhsT=ropeT.t[:, :], rhs=xbb.t[:, :], start=True, stop=True),
                                            r=[xbb, ropeT], w=[ps2])
                                    self.op("dve", lambda e: e.tensor_tensor(out=a1.t[:, :], in0=xb.t[:, :], in1=cosT.t[:, tt * TW:(tt + 1) * TW], op=ALU.mult),
                                            r=[xb, cosT], w=[a1])
                                    self.op("dve", lambda e: e.tensor_tensor(out=a2.t[:, :], in0=ps2.t[:, :], in1=sinT.t[:, tt * TW:(tt + 1) * TW], op=ALU.mult),
                                            r=[ps2, sinT], w=[a2])
                                    self.op("pool", lambda e: e.tensor_tensor(out=dst, in0=a1.t[:, :], in1=a2.t[:, :], op=ALU.add),
                                            r=[a1, a2], w=[dstb])
                                pending.append(rope_tail)
                            else:
                                self.op("act", lambda e, dst=dst: e.activation(out=dst, in_=ps.t[:, :], func=AF.Identity), r=[ps], w=[dstb])
                while pending:
                    pending.pop(0)()
                for c in range(0 if 'ck' in skip else 2):
                    ps = self.nextps()
                    for b in range(2):
                        self.op("pe", lambda e, b=b, c=c: e.transpose(out=ps.t[:, b * 128:(b + 1) * 128],
                                                                      in_=CKh.t[:, b * 256 + c * 128: b * 256 + (c + 1) * 128], identity=self.ident.t[:, :]),
                                r=[CKh, self.ident], w=[ps], sig=(b == 1))
                    self.op("act", lambda e, c=c, ps=ps: e.activation(out=KT.t[:, c * 2816 + 2048: c * 2816 + 2304], in_=ps.t[:, 0:256], func=AF.Identity),
                            r=[ps], w=[KT])
                for tb in range(0 if 'v' in skip else 20):
                    blk = tb if tb < 16 else tb + 2
                    ps = self.nextps()
                    for kc in range(KC):
                        self.op("pe", lambda e, kc=kc: e.matmul(
                            ps.t[:, 0:256], lhsT=HT.t[:, kc * T + tb * 128: kc * T + (tb + 1) * 128],
                            rhs=Wp[2].t[:, kc * 256: kc * 256 + 256], start=(kc == 0), stop=(kc == KC - 1)),
                            r=[Wp[2], self.HTb[tb // 4]], w=[ps], sig=(kc == KC - 1))
                    self.op("act", lambda e, blk=blk, ps=ps: e.activation(out=Vx.t[:, blk * VS: blk * VS + 256], in_=ps.t[:, 0:256], func=AF.Identity),
                            r=[ps], w=[Vx])
                    if tb >= 16:
                        seq = (tb - 16) // 2
                        row0 = ((tb - 16) % 2) * 128
                        st = kvst[kvi % 2]
                        kvi += 1
                        self.op("dve", lambda e, st=st, ps=ps: e.tensor_copy(out=st.t[:, :], in_=ps.t[:, 0:256]), r=[ps], w=[st])
                        self.dma("sp", self.new_v[seq, row0:row0 + 128, h * 256:(h + 1) * 256], st.t[:, :], r=[st], w=[self.OUTb])
                        ps = self.nextps()
                        for kc in range(KC):
                            self.op("pe", lambda e, kc=kc: e.matmul(
                                ps.t[:, 0:256], lhsT=HT.t[:, kc * T + tb * 128: kc * T + (tb + 1) * 128],
                                rhs=Wp[1].t[:, kc * 256: kc * 256 + 256], start=(kc == 0), stop=(kc == KC - 1)),
                                r=[Wp[1], self.HTb[4]], w=[ps], sig=(kc == KC - 1))
                        st = kvst[kvi % 2]
                        kvi += 1
                        self.op("dve", lambda e, st=st, ps=ps: e.tensor_copy(out=st.t[:, :], in_=ps.t[:, 0:256]), r=[ps], w=[st])
                        self.dma("sp", self.new_k[seq, row0:row0 + 128, h * 256:(h + 1) * 256], st.t[:, :], r=[st], w=[self.OUTb])
                if h + 1 < NHEADS:
                    load_head_w(h + 1)
                jobs = [(qt * TW, TW, list(range(18)), qt) for qt in range(4)]
                jobs.append((2048, 256, [18, 19], 4))
                jobs.append((2304, 256, [20, 21], 4))
                if os.environ.get('ATT_CORE', '1') == '0':
                    jobs = []
                acc_o = [[self.ps[0], self.ps[1]], [self.ps[2], self.ps[3]]]
                acc_s = [self.ps[4], self.ps[5]]
                sps = [self.ps[6], self.ps[7]]
                si = 0
                for (q0, N, blks, tt) in jobs:
                    steps = [(c, bi, blk) for c in range(2) for bi, blk in enumerate(blks)]
                    Sof = {}

                    def emit_S(k):
                        nonlocal si
                        c, bi, blk = steps[k]
                        if blk < 16:
                            koff = blk * 128
                        elif blk < 18:
                            koff = 2048 + (blk - 16) * 128
                        else:
                            koff = 2304 + (blk - 18) * 128
                        S = sps[si % 2]
                        si += 1
                        Sof[k] = S
                        self.op("pe", lambda e: e.matmul(
                            S.t[:, 0:N], lhsT=KT.t[:, c * 2816 + koff: c * 2816 + koff + 128],
                            rhs=QT.t[:, c * T + q0: c * T + q0 + N], start=True, stop=True), r=[KT, QT], w=[S])

                    emit_S(0)
                    if len(steps) > 1:
                        emit_S(1)
                    for k, (c, bi, blk) in enumerate(steps):
                        S = Sof[k]
                        P = PTb[pti % 4]
                        pti += 1
                        self.op("act", lambda e: e.activation(out=P.t[:, 0:N], in_=S.t[:, 0:N], func=AF.Exp, scale=ATT_SCALE),
                                r=[S], w=[P])
                        first = (bi == 0)
                        last = (bi == len(blks) - 1)
                        for half in range(2):
                            self.op("pe", lambda e, half=half: e.matmul(
                                acc_o[c][half].t[:, 0:N], lhsT=Vx.t[:, blk * VS + half * 128: blk * VS + (half + 1) * 128],
                                rhs=P.t[:, 0:N], start=first, stop=last), r=[Vx, P], w=[acc_o[c][half]], sig=False)
                        self.op("pe", lambda e: e.matmul(acc_s[c].t[:, 0:N], lhsT=self.onesb.t[:, :], rhs=P.t[:, 0:N],
                                                         start=first, stop=last), r=[self.onesb, P], w=[acc_s[c]] + acc_o[c])
                        if k + 2 < len(steps):
                            emit_S(k + 2)
                    self.op("dve", lambda e: e.reciprocal(out=r12[0].t[:, 0:N], in_=acc_s[0].t[:, 0:N]), r=[acc_s[0]], w=[r12[0]])
                    self.op("dve", lambda e: e.reciprocal(out=r12[1].t[:, 0:N], in_=acc_s[1].t[:, 0:N]), r=[acc_s[1]], w=[r12[1]])
                    self.op("dve", lambda e: e.tensor_scalar(out=r12[1].t[:, 0:N], in0=r12[1].t[:, 0:N], scalar1=LAM, scalar2=1.0,
                                                             op0=ALU.mult, op1=ALU.mult), r=[r12[1], lamt], w=[r12[1]])
                    for half in range(2):
                        self.op("dve", lambda e, half=half: e.tensor_tensor(out=o1.t[:, 0:N], in0=acc_o[0][half].t[:, 0:N], in1=r12[0].t[:, 0:N], op=ALU.mult),
                                r=[acc_o[0][half], r12[0]], w=[o1])
                        self.op("dve", lambda e, half=half: e.tensor_tensor(out=Dh[half].t[:, 0:N], in0=acc_o[1][half].t[:, 0:N], in1=r12[1].t[:, 0:N], op=ALU.mult),
                                r=[acc_o[1][half], r12[1]], w=[Dh[half]])
                        self.op("pool", lambda e, half=half: e.tensor_tensor(out=Dh[half].t[:, 0:N], in0=o1.t[:, 0:N], in1=Dh[half].t[:, 0:N], op=ALU.subtract),
                                r=[o1, Dh[half]], w=[Dh[half]])
                        self.op("act", lambda e, half=half: e.activation(out=sqd[half].t[:, 0:N], in_=Dh[half].t[:, 0:N], func=AF.Square),
                                r=[Dh[half]], w=[sqd[half]])
                    S = sps[si % 2]
                    si += 1
                    for half in range(2):
                        self.op("pe", lambda e, half=half, S=S: e.matmul(S.t[:, 0:N], lhsT=self.onesb.t[:, :], rhs=sqd[half].t[:, 0:N],
                                                                     start=(half == 0), stop=(half == 1)), r=[sqd[half], self.onesb], w=[S], sig=(half == 1))
                    self.op("act", lambda e, S=S: e.activation(out=rsd.t[:, 0:N], in_=S.t[:, 0:N], func=AF.Sqrt, scale=1.0 / 256, bias=self.cst.t[:, 1:2]),
                            r=[S, self.cst], w=[rsd])
                    self.op("dve", lambda e: e.reciprocal(out=rsd.t[:, 0:N], in_=rsd.t[:, 0:N]), r=[rsd], w=[rsd])
                    for half in range(2):
                        ob = ost[osti % 2]
                        osti += 1
                        self.op("dve", lambda e, half=half, ob=ob: e.scalar_tensor_tensor(
                            out=ob.t[:, 0:N], in0=Dh[half].t[:, 0:N], scalar=gsub.t[:, half:half + 1], in1=rsd.t[:, 0:N],
                            op0=ALU.mult, op1=ALU.mult), r=[Dh[half], gsub, rsd], w=[ob])
                        self.dma("sp", self.OT[h * 2 + half, :, q0:q0 + N], ob.t[:, 0:N], r=[ob], w=[self.OTb[tt]])
            self.barrier()

    def phase_proj(self, IN, INb, KCin, W, l, s):
        gj = 2 + 3 * s
        with ExitStack() as ph:
            intile = [self.sb(ph, f"pin{i}", [128, KCin * TW], BF16) for i in range(2)]
            wblk = [self.sb(ph, f"pw{i}", [128, KCin * 512], BF16) for i in range(2)]
            xpc = [self.sb(ph, f"pxp{i}", [128, 4 * TW], F32) for i in range(2)]
            wi = 0
            xi = 0
            for tt in range(NT):
                n = 0 if tt < 4 else 1
                it = intile[tt % 2]
                self.dma("sp", it.t[:, :].rearrange("p (k t) -> p k t", k=KCin),
                         IN[:, :, tt * TW:(tt + 1) * TW].rearrange("k p t -> p k t"), r=[INb[tt]], w=[it])
                for mcb in range(4):
                    wb = wblk[wi % 2]
                    wi += 1
                    self.dma("pool", wb.t[:, :].rearrange("p (k m) -> p k m", k=KCin),
                             W[:, mcb * 512:(mcb + 1) * 512].rearrange("(k p) m -> p k m", p=128), r=[self.INb], w=[wb])
                    xp = xpc[xi % 2]
                    xi += 1
                    self.dma("sp", xp.t[:, :].rearrange("p (k t) -> p k t", k=4), self.xt_dram(tt, mcb * 4, mcb * 4 + 4),
                             r=[self.XTb[tt]], w=[xp])
                    for m2 in range(4):
                        mc = mcb * 4 + m2
                        ps = self.nextps()
                        for kc in range(KCin):
                            self.op("pe", lambda e, kc=kc, m2=m2: e.matmul(
                                ps.t[:, :], lhsT=wb.t[:, kc * 512 + m2 * 128: kc * 512 + (m2 + 1) * 128],
                                rhs=it.t[:, kc * TW:(kc + 1) * TW], start=(kc == 0), stop=(kc == KCin - 1)),
                                r=[wb, it], w=[ps], sig=(kc == KCin - 1))
                        g = self.modv(l, gj, mc, n)
                        self.op("dve", lambda e, m2=m2, g=g, ps=ps: e.scalar_tensor_tensor(
                            out=xp.t[:, m2 * TW:(m2 + 1) * TW], in0=ps.t[:, :], scalar=g, in1=xp.t[:, m2 * TW:(m2 + 1) * TW],
                            op0=ALU.mult, op1=ALU.add), r=[ps, xp, self.MOD], w=[xp])
                    self.dma("sp", self.xt_dram(tt, mcb * 4, mcb * 4 + 4), xp.t[:, :].rearrange("p (k t) -> p k t", k=4),
                             r=[xp], w=[self.XTb[tt]])
            self.barrier()

    def phase_up(self, l):
        HT = self.HT
        with ExitStack() as ph:
            wg = [self.sb(ph, f"wg{i}", [128, KC * 512], BF16) for i in range(2)]
            wv = [self.sb(ph, f"wv{i}", [128, KC * 512], BF16) for i in range(2)]
            UGs = [self.sb(ph, f"UG{i}", [128, CW_TOT], F32) for i in range(1)]
            UVs = [self.sb(ph, f"UV{i}", [128, CW_TOT], F32) for i in range(1)]
            CG = self.sb(ph, "CG", [128, CW_TOT], F32)
            CV = self.sb(ph, "CV", [128, CW_TOT], F32)
            AO = [self.sb(ph, f"AO{i}", [128, CW_TOT], BF16) for i in range(2)]
            CWt = self.sb(ph, "CWt", [128, 3 * 88], F32)
            CBt = self.sb(ph, "CBt", [128, 88], F32)
            for k in range(3):
                self.load_fm("cw", self.conv_w[l, k].rearrange("(m p) -> m p", p=128), 88, CWt.t[:, k * 88:(k + 1) * 88], CWt, self.INb)
            self.load_fm("cb", self.conv_b[l].rearrange("(m p) -> m p", p=128), 88, CBt.t[:, :], CBt, self.INb)
            for U_ in UGs + UVs:
                self.op("dve", lambda e, U_=U_: e.memset(U_.t[:, :], 0.0), w=[U_])
            segs = [(0, 2048, CPAD[0]), (2048, 256, CPAD[1]), (2304, 256, CPAD[2])]
            ai = 0
            for jb in range(11):
                g = wg[jb % 2]
                v = wv[jb % 2]
                self.dma("pool", g.t[:, :].rearrange("p (k m) -> p k m", k=KC),
                         self.w_up[l][:, jb * 512:(jb + 1) * 512].rearrange("(k p) m -> p k m", p=128), r=[self.INb], w=[g])
                self.dma("pool", v.t[:, :].rearrange("p (k m) -> p k m", k=KC),
                         self.w_up[l][:, DFF + jb * 512: DFF + (jb + 1) * 512].rearrange("(k p) m -> p k m", p=128), r=[self.INb], w=[v])
                for j2 in range(4):
                    j = jb * 4 + j2
                    UG, UV = UGs[0], UVs[0]
                    for (wt, U, cidx) in ((g, UG, j), (v, UV, JC + j)):
                        for tt in range(NT):
                            ps = self.nextps()
                            for kc in range(KC):
                                self.op("pe", lambda e, kc=kc, wt=wt: e.matmul(
                                    ps.t[:, :], lhsT=wt.t[:, kc * 512 + j2 * 128: kc * 512 + (j2 + 1) * 128],
                                    rhs=HT.t[:, kc * T + tt * TW: kc * T + (tt + 1) * TW], start=(kc == 0), stop=(kc == KC - 1)),
                                    r=[wt, self.HTb[tt]], w=[ps], sig=(kc == KC - 1))
                            if tt < 4:
                                self.op("act", lambda e, U=U, ps=ps: e.activation(out=U.t[:, 1 + tt * TW: 1 + (tt + 1) * TW], in_=ps.t[:, :], func=AF.Identity),
                                        r=[ps], w=[U])
                            else:
                                self.op("act", lambda e, U=U, ps=ps: e.activation(out=U.t[:, CPAD[1]:CPAD[1] + 256], in_=ps.t[:, 0:256], func=AF.Identity),
                                        r=[ps], w=[U])
                                self.op("act", lambda e, U=U, ps=ps: e.activation(out=U.t[:, CPAD[2]:CPAD[2] + 256], in_=ps.t[:, 256:512], func=AF.Identity),
                                        r=[ps], w=[U])
                    L = CW_TOT - 2
                    for (U, C, cidx) in ((UG, CG, j), (UV, CV, JC + j)):
                        self.op("act", lambda e, U=U, C=C, cidx=cidx: e.activation(
                            out=C.t[:, 1:1 + L], in_=U.t[:, 1:1 + L], func=AF.Identity,
                            scale=CWt.t[:, 88 + cidx: 88 + cidx + 1], bias=CBt.t[:, cidx:cidx + 1]), r=[U, CWt, CBt], w=[C])
                        self.op("dve", lambda e, U=U, C=C, cidx=cidx: e.scalar_tensor_tensor(
                            out=C.t[:, 1:1 + L], in0=U.t[:, 0:L], scalar=CWt.t[:, cidx:cidx + 1], in1=C.t[:, 1:1 + L],
                            op0=ALU.mult, op1=ALU.add), r=[U, C, CWt], w=[C])
                        self.op("dve", lambda e, U=U, C=C, cidx=cidx: e.scalar_tensor_tensor(
                            out=C.t[:, 1:1 + L], in0=U.t[:, 2:2 + L], scalar=CWt.t[:, 176 + cidx: 176 + cidx + 1], in1=C.t[:, 1:1 + L],
                            op0=ALU.mult, op1=ALU.add), r=[U, C, CWt], w=[C])
                    self.op("act", lambda e: e.activation(out=CG.t[:, 1:1 + L], in_=CG.t[:, 1:1 + L], func=AF.Silu), r=[CG], w=[CG])
                    ao = AO[ai % 2]
                    ai += 1
                    self.op("dve", lambda e, ao=ao: e.tensor_tensor(out=ao.t[:, 1:1 + L], in0=CG.t[:, 1:1 + L], in1=CV.t[:, 1:1 + L], op=ALU.mult),
                            r=[CG, CV], w=[ao])
                    for (t0, ln, off) in segs:
                        self.dma("sp", self.AT[j, :, t0:t0 + ln], ao.t[:, off:off + ln], r=[ao], w=self.ATb)
            self.barrier()

    def phase_final(self):
        with ExitStack() as ph:
            xtile = [self.sb(ph, f"xtile{i}", [128, KC * TW], F32) for i in range(2)]
            yt = [self.sb(ph, f"yt{i}", [128, KC * TW], F32) for i in range(2)]
            self.RSTD = self.sb(ph, "RSTD", [128, T], F32)
            sq = self.sb(ph, "sq", [128, KC * TW], BF16)
            rs = self.sb(ph, "rs", [128, TW], F32)
            yo = [self.sb(ph, f"yo{i}", [128, D], F32) for i in range(2)]
            oi = 0
            for tt in range(NT):
                xt = xtile[tt % 2]
                y = yt[tt % 2]
                self.dma("sp", xt.t[:, :].rearrange("p (k t) -> p k t", k=KC), self.xt_dram(tt), r=[self.XTb[tt]], w=[xt])
                self.norm_tile(xt, tt, 0, 0, sq, rs, None, final_out=y)
                for b4 in range(4):
                    o = yo[oi % 2]
                    oi += 1
                    for kq in range(4):
                        ps = self.nextps()
                        for k4 in range(4):
                            kc = kq * 4 + k4
                            self.op("pe", lambda e, kc=kc, k4=k4: e.transpose(
                                out=ps.t[:, k4 * 128:(k4 + 1) * 128], in_=y.t[:, kc * TW + b4 * 128: kc * TW + (b4 + 1) * 128],
                                identity=self.ident.t[:, :]), r=[y, self.ident], w=[ps], sig=(k4 == 3))
                        if kq % 2 == 0:
                            self.op("act", lambda e, kq=kq, ps=ps: e.activation(out=o.t[:, kq * 512:(kq + 1) * 512], in_=ps.t[:, :], func=AF.Identity),
                                    r=[ps], w=[o])
                        else:
                            self.op("dve", lambda e, kq=kq, ps=ps: e.tensor_copy(out=o.t[:, kq * 512:(kq + 1) * 512], in_=ps.t[:, :]), r=[ps], w=[o])
                    tb = tt * 4 + b4
                    self.dma("sp", self.y_all[tb * 128:(tb + 1) * 128, :], o.t[:, :], r=[o], w=[self.OUTb])
            self.barrier()

    def bc(self, ap2d, n):
        a = ap2d.ap
        return bass.AP(ap2d.tensor, ap2d.offset, [list(a[0]), list(a[1]), [0, n]])

    def colbc(self, ap_col, n):
        a = ap_col.ap
        return bass.AP(ap_col.tensor, ap_col.offset, [list(a[0]), [0, n]])

    def rev(self, ap2d):
        a = ap2d.ap
        n = a[1][1]
        return bass.AP(ap2d.tensor, ap2d.offset + (n - 1) * a[1][0], [list(a[0]), [-a[1][0], n]])

    def phase_s5_tok(self):
        HT = self.HT
        TWO_PI = 2.0 * math.pi
        with ExitStack() as ph:
            def T128(name):
                return self.sb(ph, name, [128, 128], F32)
            are, aim, LS, dt, mag, ang, sinr, cosr = [T128(n) for n in ("are", "aim", "LS", "dt", "mag", "ang", "sinr", "cosr")]
            lbre, lbim, fre, fim, tA, tB, tC = [T128(n) for n in ("lbre", "lbim", "fre", "fim", "tA", "tB", "tC")]
            H0re, H0im = T128("H0re"), T128("H0im")
            ki = self.sb(ph, "ki", [128, 128], I32)
            lsr = self.sb(ph, "lsr", [128, 2], F32)
            lsx = self.sb(ph, "lsx", [128, 128], F32)
            NK = 10
            CWre = self.sb(ph, "CWre", [128, NK * 128], F32)
            CWim = self.sb(ph, "CWim", [128, NK * 128], F32)
            FSre = self.sb(ph, "FSre", [128, 256], F32)
            FSim = self.sb(ph, "FSim", [128, 256], F32)
            fso = self.sb(ph, "fso", [128, 128], F32)
            TBre = [self.sb(ph, f"TBre{i}", [128, 4 * 512], F32) for i in range(1)]
            TBim = [self.sb(ph, f"TBim{i}", [128, 4 * 512], F32) for i in range(1)]
            tt1 = self.sb(ph, "tt1", [128, 4 * 256], F32)
            tt2 = self.sb(ph, "tt2", [128, 4 * 256], F32)
            YK = self.sb(ph, "YK", [128, T], F32)
            SBre = self.sb(ph, "SBre", [128, 8 * 16], F32)
            SBim = self.sb(ph, "SBim", [128, 8 * 16], F32)
            BBre = self.sb(ph, "BBre", [128, 8 * 16], F32)
            BBim = self.sb(ph, "BBim", [128, 8 * 16], F32)
            SCre = self.sb(ph, "SCre", [32, 8 * 64], F32)
            SCim = self.sb(ph, "SCim", [32, 8 * 64], F32)
            SC2re = self.sb(ph, "SC2re", [32, 8 * 128], F32)
            SC2im = self.sb(ph, "SC2im", [32, 8 * 128], F32)
            STre = [self.sb(ph, f"STre{i}", [128, 128], F32) for i in range(4)]
            STim = [self.sb(ph, f"STim{i}", [128, 128], F32) for i in range(4)]
            WBre = [self.sb(ph, f"WBre{i}", [128, 128], BF16) for i in range(2)]
            WBim = [self.sb(ph, f"WBim{i}", [128, 128], BF16) for i in range(2)]
            WCre = [self.sb(ph, f"WCre{i}", [128, 128], BF16) for i in range(8)]
            WCin = [self.sb(ph, f"WCin{i}", [128, 128], BF16) for i in range(8)]
            m1, m2, m3, m4 = [self.sb(ph, f"m{i}", [128, TW], F32) for i in range(4)]
            zre, zim, hsre, hsim = [self.sb(ph, n, [128, TW], F32) for n in ("zre", "zim", "hsre", "hsim")]
            HBre = [self.sb(ph, f"HBre{i}", [128, TW], BF16) for i in range(2)]
            HBim = [self.sb(ph, f"HBim{i}", [128, TW], BF16) for i in range(2)]
            ini = self.sb(ph, "ini", [128, 8], F32)
            HPre = [self.sb(ph, f"HPre{i}", [128, TW], BF16) for i in range(2)]
            HPim = [self.sb(ph, f"HPim{i}", [128, TW], BF16) for i in range(2)]
            hpi = 0

            V = lambda b: b.t[:, :]
            dve = lambda fn, r, w: self.op("dve", fn, r=r, w=w)
            TT = lambda o, a, b, op, r, w, eng="dve": self.op(eng, lambda e: e.tensor_tensor(out=o, in0=a, in1=b, op=op), r=r, w=w)

            self.load_fm("are", self.ssm_a_re.rearrange("d (pr g2) p -> (d pr) (g2 p)", g2=2), 128, V(are), are, self.INb)
            self.load_fm("aim", self.ssm_a_im.rearrange("d (pr g2) p -> (d pr) (g2 p)", g2=2), 128, V(aim), aim, self.INb)
            self.load_fm("h0r", self.st_re.rearrange("d (pr g2) p -> (d pr) (g2 p)", g2=2), 128, V(H0re), H0re, self.INb)
            self.load_fm("h0i", self.st_im.rearrange("d (pr g2) p -> (d pr) (g2 p)", g2=2), 128, V(H0im), H0im, self.INb)
            self.dma("sp", lsr.t[:, :], self.ssm_log_step.rearrange("d (pr g2) -> (d pr) g2", g2=2), r=[self.INb], w=[lsr])
            for g2 in range(2):
                dve(lambda e, g2=g2: e.tensor_scalar(out=lsx.t[:, g2 * 64:(g2 + 1) * 64], in0=self.onesf.t[:, 0:64],
                                                     scalar1=lsr.t[:, g2:g2 + 1], scalar2=1.0, op0=ALU.mult, op1=ALU.mult),
                    [lsr, self.onesf], [lsx])
            ps = self.nextps()
            self.op("pe", lambda e: e.transpose(out=ps.t[:, 0:128], in_=lsx.t[:, :], identity=self.ident.t[:, :]), r=[lsx, self.ident], w=[ps])
            self.op("act", lambda e: e.activation(out=V(dt), in_=ps.t[:, 0:128], func=AF.Exp), r=[ps], w=[dt])
            TT(V(tA), V(are), V(dt), ALU.mult, [are, dt], [tA])
            self.op("act", lambda e: e.activation(out=V(mag), in_=V(tA), func=AF.Exp), r=[tA], w=[mag])
            TT(V(ang), V(aim), V(dt), ALU.mult, [aim, dt], [ang])
            dve(lambda e: e.tensor_scalar(out=V(tB), in0=V(ang), scalar1=1.0 / TWO_PI, scalar2=1.0, op0=ALU.mult, op1=ALU.mult), [ang], [tB])
            dve(lambda e: e.tensor_copy(out=ki.t[:, :], in_=V(tB)), [tB], [ki])
            dve(lambda e: e.tensor_copy(out=V(tB), in_=ki.t[:, :]), [ki], [tB])
            dve(lambda e: e.scalar_tensor_tensor(out=V(tC), in0=V(tB), scalar=-TWO_PI, in1=V(ang), op0=ALU.mult, op1=ALU.add), [tB, ang], [tC])
            dve(lambda e: e.tensor_scalar(out=V(tC), in0=V(tC), scalar1=3.141592, scalar2=-3.141592, op0=ALU.min, op1=ALU.max), [tC], [tC])
            self.op("act", lambda e: e.activation(out=V(sinr), in_=V(tC), func=AF.Sin), r=[tC], w=[sinr])
            self.op("act", lambda e: e.activation(out=V(tA), in_=V(tC), func=AF.Sin, scale=0.5), r=[tC], w=[tA])
            TT(V(tA), V(tA), V(tA), ALU.mult, [tA], [tA])
            dve(lambda e: e.tensor_scalar(out=V(cosr), in0=V(tA), scalar1=-2.0, scalar2=1.0, op0=ALU.mult, op1=ALU.add), [tA], [cosr])
            TT(V(lbre), V(mag), V(cosr), ALU.mult, [mag, cosr], [lbre])
            TT(V(lbim), V(mag), V(sinr), ALU.mult, [mag, sinr], [lbim])
            dve(lambda e: e.tensor_scalar(out=V(tA), in0=V(lbre), scalar1=-1.0, scalar2=1.0, op0=ALU.add, op1=ALU.mult), [lbre], [tA])
            TT(V(tB), V(are), V(are), ALU.mult, [are], [tB])
            TT(V(tC), V(aim), V(aim), ALU.mult, [aim], [tC])
            TT(V(tB), V(tB), V(tC), ALU.add, [tB, tC], [tB])
            dve(lambda e: e.reciprocal(out=V(tB), in_=V(tB)), [tB], [tB])
            TT(V(fre), V(tA), V(are), ALU.mult, [tA, are], [fre])
            TT(V(tC), V(lbim), V(aim), ALU.mult, [lbim, aim], [tC])
            TT(V(fre), V(fre), V(tC), ALU.add, [fre, tC], [fre])
            TT(V(fre), V(fre), V(tB), ALU.mult, [fre, tB], [fre])
            TT(V(fim), V(lbim), V(are), ALU.mult, [lbim, are], [fim])
            TT(V(tC), V(tA), V(aim), ALU.mult, [tA, aim], [tC])
            TT(V(fim), V(fim), V(tC), ALU.subtract, [fim, tC], [fim])
            TT(V(fim), V(fim), V(tB), ALU.mult, [fim, tB], [fim])
            dve(lambda e: e.tensor_copy(out=CWre.t[:, 0:128], in_=V(cosr)), [cosr], [CWre])
            dve(lambda e: e.tensor_scalar(out=CWim.t[:, 0:128], in0=V(sinr), scalar1=-1.0, scalar2=1.0, op0=ALU.mult, op1=ALU.mult), [sinr], [CWim])
            for k in range(NK - 1):
                a = CWre.t[:, k * 128:(k + 1) * 128]
                b = CWim.t[:, k * 128:(k + 1) * 128]
                TT(V(tA), a, a, ALU.mult, [CWre], [tA])
                TT(V(tB), b, b, ALU.mult, [CWim], [tB])
                TT(CWre.t[:, (k + 1) * 128:(k + 2) * 128], V(tA), V(tB), ALU.subtract, [tA, tB], [CWre])
                TT(V(tC), a, b, ALU.mult, [CWre, CWim], [tC])
                dve(lambda e, k=k: e.tensor_scalar(out=CWim.t[:, (k + 1) * 128:(k + 2) * 128], in0=V(tC), scalar1=2.0, scalar2=1.0,
                                                   op0=ALU.mult, op1=ALU.mult), [tC], [CWim])
            for st in STre + STim:
                dve(lambda e, st=st: e.memset(st.t[:, :], 0.0), [], [st])
            for wc in WCre + WCin:
                dve(lambda e, wc=wc: e.memset(wc.t[:, :], 0.0), [], [wc])

            seq_chunks = {0: [(0, 512), (512, 512), (1024, 512), (1536, 512)], 1: [(2048, 256)], 2: [(2304, 256)]}
            wbi = 0
            hbi = 0
            tbi = 0
            psy = [self.ps[i] for i in range(5)]
            psb = [self.ps[5], self.ps[6]]
            pst = self.ps[7]
            for kc in range(KC):
                for (SB_, src) in ((SBre, self.ssm_b_re), (SBim, self.ssm_b_im)):
                    for d in range(2):
                        self.dma("sp", SB_.t[:, d * 64:(d + 1) * 64].rearrange("p (c k) -> p c k", k=16),
                                 src[d, 8 * kc:8 * kc + 8].rearrange("(pq g2) p ci -> (g2 p) pq ci", g2=2), r=[self.INb], w=[SB_],
                                 allow_slow_non_contiguous=False)
                for (SC_, src) in ((SCre, self.ssm_c_re), (SCim, self.ssm_c_im)):
                    for d in range(2):
                        self.dma("sp", SC_.t[:, d * 256:(d + 1) * 256].rearrange("p (c k) -> p c k", k=64),
                                 src[d, 8 * kc:8 * kc + 8].rearrange("(pq g2) co p -> (g2 co) pq p", g2=2), r=[self.INb], w=[SC_])
                for (SC_, SC2_) in ((SCre, SC2re), (SCim, SC2im)):
                    for rep in range(2):
                        dve(lambda e, rep=rep, SC_=SC_, SC2_=SC2_: e.tensor_copy(
                            out=SC2_.t[:, :].rearrange("p (c r k) -> p c r k", r=2, k=64)[:, :, rep, :],
                            in_=SC_.t[:, :].rearrange("p (c k) -> p c k", k=64)), [SC_], [SC2_])
                for d in range(2):
                    c0 = d * 64 + 4 * kc
                    fr = self.bc(fre.t[:, c0:c0 + 4], 16)
                    fi = self.bc(fim.t[:, c0:c0 + 4], 16)
                    sl = slice(d * 64, (d + 1) * 64)
                    v3 = lambda b: b.t[:, sl].rearrange("p (c k) -> p c k", k=16)
                    t3 = tt1.t[:, 0:64].rearrange("p (c k) -> p c k", k=16)
                    TT(v3(BBre), v3(SBre), fr, ALU.mult, [SBre, fre], [BBre])
                    TT(t3, v3(SBim), fi, ALU.mult, [SBim, fim], [tt1])
                    TT(v3(BBre), v3(BBre), t3, ALU.subtract, [BBre, tt1], [BBre])
                    TT(v3(BBim), v3(SBre), fi, ALU.mult, [SBre, fim], [BBim])
                    TT(t3, v3(SBim), fr, ALU.mult, [SBim, fre], [tt1])
                    TT(v3(BBim), v3(BBim), t3, ALU.add, [BBim, tt1], [BBim])
                for d in range(2):
                    tre = TBre[0]
                    tim = TBim[0]
                    tbi += 1
                    c0 = d * 64 + 4 * kc
                    tre3 = tre.t[:, :].rearrange("p (c j) -> p c j", j=512)
                    tim3 = tim.t[:, :].rearrange("p (c j) -> p c j", j=512)
                    dve(lambda e: e.memset(tre3[:, :, 0:1], 1.0), [], [tre])
                    dve(lambda e: e.memset(tim3[:, :, 0:1], 0.0), [], [tim])
                    for k in range(9):
                        s = 1 << k
                        wr = self.bc(CWre.t[:, k * 128 + c0: k * 128 + c0 + 4], s)
                        wi = self.bc(CWim.t[:, k * 128 + c0: k * 128 + c0 + 4], s)
                        a = tre3[:, :, 0:s]
                        b = tim3[:, :, 0:s]
                        u1 = tt1.t[:, 0:4 * s].rearrange("p (c j) -> p c j", j=s)
                        u2 = tt2.t[:, 0:4 * s].rearrange("p (c j) -> p c j", j=s)
                        TT(u1, a, wr, ALU.mult, [tre, CWre], [tt1])
                        TT(u2, b, wi, ALU.mult, [tim, CWim], [tt2])
                        TT(tre3[:, :, s:2 * s], u1, u2, ALU.subtract, [tt1, tt2], [tre])
                        TT(u1, a, wi, ALU.mult, [tre, CWim], [tt1])
                        TT(u2, b, wr, ALU.mult, [tim, CWre], [tt2])
                        TT(tim3[:, :, s:2 * s], u1, u2, ALU.add, [tt1, tt2], [tim])
                    for pq in range(4):
                        col = c0 + pq
                        cc = d * 4 + pq
                        for (ST_, BB_, WB_) in ((STre[pq], BBre, WBre[wbi % 2]), (STim[pq], BBim, WBim[wbi % 2])):
                            for g2 in range(2):
                                dve(lambda e, g2=g2, ST_=ST_, BB_=BB_: e.tensor_copy(
                                    out=ST_.t[g2 * 64:(g2 + 1) * 64, 32 * pq + 16 * g2: 32 * pq + 16 * g2 + 16],
                                    in_=BB_.t[g2 * 64:(g2 + 1) * 64, cc * 16:(cc + 1) * 16]), [BB_], [ST_])
                            self.op("pe", lambda e, ST_=ST_: e.transpose(out=pst.t[:, 0:128], in_=ST_.t[:, :], identity=self.ident.t[:, :]),
                                    r=[ST_, self.ident], w=[pst])
                            self.op("act", lambda e, WB_=WB_: e.activation(out=WB_.t[:, :], in_=pst.t[:, 0:128], func=AF.Identity), r=[pst], w=[WB_])
                        wbre, wbim = WBre[wbi % 2], WBim[wbi % 2]
                        wbi += 1
                        wcre, wcin = WCre[cc], WCin[cc]
                        for (SC_, WC_, sgn) in ((SC2re, wcre, 1.0), (SC2im, wcin, -1.0)):
                            dup = SC_.t[:, cc * 128:(cc + 1) * 128]
                            self.op("pe", lambda e, dup=dup: e.transpose(out=pst.t[:, 0:32], in_=dup, identity=self.ident.t[0:32, 0:32]),
                                    r=[SC_, self.ident], w=[pst])
                            for g2 in range(2):
                                self.op("act", lambda e, g2=g2, WC_=WC_, sgn=sgn: e.activation(
                                    out=WC_.t[g2 * 64:(g2 + 1) * 64, 32 * pq + 16 * g2: 32 * pq + 16 * g2 + 16],
                                    in_=pst.t[g2 * 64:(g2 + 1) * 64, 16 * g2: 16 * g2 + 16], func=AF.Identity, scale=sgn), r=[pst], w=[WC_])
                        rho = mag.t[:, col:col + 1]
                        wre_c = cosr.t[:, col:col + 1]
                        wim_c = sinr.t[:, col:col + 1]
                        for sq_ in range(3):
                            chunks = seq_chunks[sq_] if d == 0 else list(reversed(seq_chunks[sq_]))
                            for ci_, (t0, N) in enumerate(chunks):
                                tile_i = t0 // TW
                                off = t0 - tile_i * TW
                                Tr = tre.t[:, pq * 512: pq * 512 + N]
                                Ti = tim.t[:, pq * 512: pq * 512 + N]
                                self.op("pe", lambda e: e.matmul(psb[0].t[:, 0:N], lhsT=wbre.t[:, :], rhs=HT.t[:, kc * T + t0: kc * T + t0 + N],
                                                                 start=True, stop=True), r=[wbre, self.HTb[tile_i]], w=[psb[0]])
                                self.op("pe", lambda e: e.matmul(psb[1].t[:, 0:N], lhsT=wbim.t[:, :], rhs=HT.t[:, kc * T + t0: kc * T + t0 + N],
                                                                 start=True, stop=True), r=[wbim, self.HTb[tile_i]], w=[psb[1]])
                                bre = psb[0].t[:, 0:N] if d == 0 else self.rev(psb[0].t[:, 0:N])
                                bim = psb[1].t[:, 0:N] if d == 0 else self.rev(psb[1].t[:, 0:N])
                                TT(m1.t[:, 0:N], bre, Tr, ALU.mult, [psb[0], tre], [m1])
                                TT(m4.t[:, 0:N], bre, Ti, ALU.mult, [psb[0], tim], [m4])
                                TT(m2.t[:, 0:N], bim, Ti, ALU.mult, [psb[1], tim], [m2])
                                TT(m3.t[:, 0:N], bim, Tr, ALU.mult, [psb[1], tre], [m3])
                                TT(zre.t[:, 0:N], m1.t[:, 0:N], m2.t[:, 0:N], ALU.subtract, [m1, m2], [zre], eng="pool")
                                TT(zim.t[:, 0:N], m3.t[:, 0:N], m4.t[:, 0:N], ALU.add, [m3, m4], [zim], eng="pool")
                                if ci_ == 0:
                                    if sq_ == 0:
                                        h0r = H0re.t[:, col:col + 1]
                                        h0i = H0im.t[:, col:col + 1]
                                        TT(ini.t[:, 2:3], h0r, wre_c, ALU.mult, [H0re, cosr], [ini])
                                        TT(ini.t[:, 3:4], h0i, wim_c, ALU.mult, [H0im, sinr], [ini])
                                        TT(ini.t[:, 0:1], ini.t[:, 2:3], ini.t[:, 3:4], ALU.subtract, [ini], [ini])
                                        TT(ini.t[:, 2:3], h0r, wim_c, ALU.mult, [H0re, sinr], [ini])
                                        TT(ini.t[:, 3:4], h0i, wre_c, ALU.mult, [H0im, cosr], [ini])
                                        TT(ini.t[:, 1:2], ini.t[:, 2:3], ini.t[:, 3:4], ALU.add, [ini], [ini])
                                    else:
                                        dve(lambda e: e.memset(ini.t[:, 0:2], 0.0), [], [ini])
                                else:
                                    c9r = CWre.t[:, 9 * 128 + col: 9 * 128 + col + 1]
                                    c9i = CWim.t[:, 9 * 128 + col: 9 * 128 + col + 1]
                                    lr = hsre.t[:, 511:512]
                                    li = hsim.t[:, 511:512]
                                    TT(ini.t[:, 2:3], lr, c9r, ALU.mult, [hsre, CWre], [ini])
                                    TT(ini.t[:, 3:4], li, c9i, ALU.mult, [hsim, CWim], [ini])
                                    TT(ini.t[:, 4:5], li, c9r, ALU.mult, [hsim, CWre], [ini])
                                    TT(ini.t[:, 5:6], lr, c9i, ALU.mult, [hsre, CWim], [ini])
                                    TT(ini.t[:, 0:1], ini.t[:, 2:3], ini.t[:, 3:4], ALU.add, [ini], [ini])
                                    TT(ini.t[:, 1:2], ini.t[:, 4:5], ini.t[:, 5:6], ALU.subtract, [ini], [ini])
                                dve(lambda e: e.tensor_tensor_scan(out=hsre.t[:, 0:N], data0=self.colbc(rho, N), data1=zre.t[:, 0:N],
                                                                   initial=ini.t[:, 0:1], op0=ALU.mult, op1=ALU.add), [mag, zre, ini], [hsre])
                                dve(lambda e: e.tensor_tensor_scan(out=hsim.t[:, 0:N], data0=self.colbc(rho, N), data1=zim.t[:, 0:N],
                                                                   initial=ini.t[:, 1:2], op0=ALU.mult, op1=ALU.add), [mag, zim, ini], [hsim])
                                if sq_ == 0:
                                    hbre, hbim = HBre[hbi % 2], HBim[hbi % 2]
                                    hbi += 1
                                    ho = 0
                                else:
                                    hbre, hbim = HPre[hpi % 2], HPim[hpi % 2]
                                    ho = (sq_ - 1) * 256
                                    if sq_ == 2:
                                        hpi += 1
                                TT(m1.t[:, 0:N], hsre.t[:, 0:N], Tr, ALU.mult, [hsre, tre], [m1])
                                TT(m2.t[:, 0:N], hsim.t[:, 0:N], Ti, ALU.mult, [hsim, tim], [m2])
                                TT(m3.t[:, 0:N], hsim.t[:, 0:N], Tr, ALU.mult, [hsim, tre], [m3])
                                TT(m4.t[:, 0:N], hsre.t[:, 0:N], Ti, ALU.mult, [hsre, tim], [m4])
                                ore = hbre.t[:, ho:ho + N] if d == 0 else self.rev(hbre.t[:, ho:ho + N])
                                oim = hbim.t[:, ho:ho + N] if d == 0 else self.rev(hbim.t[:, ho:ho + N])
                                TT(ore, m1.t[:, 0:N], m2.t[:, 0:N], ALU.add, [m1, m2], [hbre], eng="pool")
                                TT(oim, m3.t[:, 0:N], m4.t[:, 0:N], ALU.subtract, [m3, m4], [hbim], eng="pool")
                                if sq_ > 0:
                                    fcol = ((sq_ - 1) * 2 + d) * 64 + (4 * kc + pq)
                                    TT(FSre.t[:, fcol:fcol + 1], m1.t[:, N - 1:N], m2.t[:, N - 1:N], ALU.add, [m1, m2], [FSre])
                                    TT(FSim.t[:, fcol:fcol + 1], m3.t[:, N - 1:N], m4.t[:, N - 1:N], ALU.subtract, [m3, m4], [FSim])
                                if sq_ == 1:
                                    continue
                                NN = N if sq_ == 0 else 512
                                yv = psy[tile_i].t[:, 0:NN]
                                self.op("pe", lambda e: e.matmul(yv, lhsT=wcre.t[:, :], rhs=hbre.t[:, 0:NN], start=(pq == 0), stop=False),
                                        r=[wcre, hbre], w=[psy[tile_i]], sig=False)
                                self.op("pe", lambda e: e.matmul(yv, lhsT=wcin.t[:, :], rhs=hbim.t[:, 0:NN], start=False, stop=(pq == 3)),
                                        r=[wcin, hbim], w=[psy[tile_i]])
                    for ti in range(NT):
                        if d == 0:
                            self.op("act", lambda e, ti=ti: e.activation(out=YK.t[:, ti * TW:(ti + 1) * TW], in_=psy[ti].t[:, :], func=AF.Identity),
                                    r=[psy[ti]], w=[YK])
                        else:
                            TT(YK.t[:, ti * TW:(ti + 1) * TW], psy[ti].t[:, :], YK.t[:, ti * TW:(ti + 1) * TW], ALU.add, [psy[ti], YK], [YK])
                self.dma("sp", self.YT[kc], YK.t[:, :], r=[YK], w=self.YTb)
            for (FS, dst) in ((FSre, self.new_sre), (FSim, self.new_sim)):
                for hlf in range(2):
                    self.op("pe", lambda e, hlf=hlf, FS=FS: e.transpose(out=pst.t[:, 0:128], in_=FS.t[:, hlf * 128:(hlf + 1) * 128], identity=self.ident.t[:, :]),
                            r=[FS, self.ident], w=[pst])
                    self.op("act", lambda e: e.activation(out=fso.t[:, :], in_=pst.t[:, 0:128], func=AF.Identity), r=[pst], w=[fso])
                    self.dma("sp", dst[hlf * 128:(hlf + 1) * 128, :], fso.t[:, :], r=[fso], w=[self.OUTb])
            self.barrier()

    def ap4(self, buf, col0, dims, rows=None):
        base = buf.t[:, col0:col0 + 1] if rows is None else buf.t[rows[0]:rows[1], col0:col0 + 1]
        return bass.AP(base.tensor, base.offset, [list(base.ap[0])] + [list(d) for d in dims])

    def phase_s5(self):
        HT = self.HT
        TWO_PI = 2.0 * math.pi
        NCH = 320
        SEQC = [(0, 256), (256, 32), (288, 32)]
        HBASE = [0, 257, 290]
        HTOT = 323
        with ExitStack() as ph:
            def T128(name, st=ph):
                return self.sb(st, name, [128, 128], F32)
            lbre, lbim, fre, fim = [T128(n) for n in ("lbre", "lbim", "fre", "fim")]
            H0re, H0im, ilre, ilim, mag8 = [T128(n) for n in ("H0re", "H0im", "ilre", "ilim", "mag8")]
            maskF, maskB = T128("maskF"), T128("maskB")
            I8re, I8im = T128("I8re"), T128("I8im")
            H0bre = self.sb(ph, "H0bre", [128, 128], BF16)
            H0bim = self.sb(ph, "H0bim", [128, 128], BF16)
            RHm = [self.sb(ph, f"RHm{i}", [128, NCH], F32) for i in range(2)]
            NK = 11
            CWre = self.sb(ph, "CWre", [128, NK * 128], F32)
            CWim = self.sb(ph, "CWim", [128, NK * 128], F32)
            FSre = self.sb(ph, "FSre", [128, 256], F32)
            FSim = self.sb(ph, "FSim", [128, 256], F32)
            fso = self.sb(ph, "fso", [128, 128], F32)
            Sel = self.sb(ph, "Sel", [128, 8 * 240], BF16)
            selw = lambda a, b: Sel.t[:, a * 240 + 112 - 16 * b: a * 240 + 112 - 16 * b + 128]
            YK = self.sb(ph, "YK", [128, T], F32)
            pre = ExitStack()
            are, aim, dt, mag, ang, sinr, cosr, tA, tB, tC = [T128(n, pre) for n in ("are", "aim", "dt", "mag", "ang", "sinr", "cosr", "tA", "tB", "tC")]
            ki = self.sb(pre, "ki", [128, 128], I32)
            lsr = self.sb(pre, "lsr", [128, 2], F32)
            lsx = self.sb(pre, "lsx", [128, 128], F32)
            V = lambda b: b.t[:, :]
            dve = lambda fn, r, w: self.op("dve", fn, r=r, w=w)
            TT = lambda o, a, b, op, r, w, eng="dve": self.op(eng, lambda e: e.tensor_tensor(out=o, in0=a, in1=b, op=op), r=r, w=w)

            self.dma("pool", Sel.t[:, :], self.c_sel, r=[self.INb], w=[Sel])
            self.dma("sp", maskF.t[:, :], self.c_maskf, r=[self.INb], w=[maskF])
            self.dma("sp", maskB.t[:, :], self.c_maskb, r=[self.INb], w=[maskB])
            self.load_fm("are", self.ssm_a_re.rearrange("d (pr g2) p -> (d pr) (g2 p)", g2=2), 128, V(are), are, self.INb)
            self.load_fm("aim", self.ssm_a_im.rearrange("d (pr g2) p -> (d pr) (g2 p)", g2=2), 128, V(aim), aim, self.INb)
            self.load_fm("h0r", self.st_re.rearrange("d (pr g2) p -> (d pr) (g2 p)", g2=2), 128, V(H0re), H0re, self.INb)
            self.load_fm("h0i", self.st_im.rearrange("d (pr g2) p -> (d pr) (g2 p)", g2=2), 128, V(H0im), H0im, self.INb)
            self.dma("sp", lsr.t[:, :], self.ssm_log_step.rearrange("d (pr g2) -> (d pr) g2", g2=2), r=[self.INb], w=[lsr])
            for g2 in range(2):
                dve(lambda e, g2=g2: e.tensor_scalar(out=lsx.t[:, g2 * 64:(g2 + 1) * 64], in0=self.onesf.t[:, 0:64],
                                                     scalar1=lsr.t[:, g2:g2 + 1], scalar2=1.0, op0=ALU.mult, op1=ALU.mult),
                    [lsr, self.onesf], [lsx])
            ps = self.nextps()
            self.op("pe", lambda e: e.transpose(out=ps.t[:, 0:128], in_=lsx.t[:, :], identity=self.ident.t[:, :]), r=[lsx, self.ident], w=[ps])
            self.op("act", lambda e: e.activation(out=V(dt), in_=ps.t[:, 0:128], func=AF.Exp), r=[ps], w=[dt])
            TT(V(tA), V(are), V(dt), ALU.mult, [are, dt], [tA])
            self.op("act", lambda e: e.activation(out=V(mag), in_=V(tA), func=AF.Exp), r=[tA], w=[mag])
            TT(V(ang), V(aim), V(dt), ALU.mult, [aim, dt], [ang])
            dve(lambda e: e.tensor_scalar(out=V(tB), in0=V(ang), scalar1=1.0 / TWO_PI, scalar2=1.0, op0=ALU.mult, op1=ALU.mult), [ang], [tB])
            dve(lambda e: e.tensor_copy(out=ki.t[:, :], in_=V(tB)), [tB], [ki])
            dve(lambda e: e.tensor_copy(out=V(tB), in_=ki.t[:, :]), [ki], [tB])
            dve(lambda e: e.scalar_tensor_tensor(out=V(tC), in0=V(tB), scalar=-TWO_PI, in1=V(ang), op0=ALU.mult, op1=ALU.add), [tB, ang], [tC])
            dve(lambda e: e.tensor_scalar(out=V(tC), in0=V(tC), scalar1=3.141592, scalar2=-3.141592, op0=ALU.min, op1=ALU.max), [tC], [tC])
            self.op("act", lambda e: e.activation(out=V(sinr), in_=V(tC), func=AF.Sin), r=[tC], w=[sinr])
            self.op("act", lambda e: e.activation(out=V(tA), in_=V(tC), func=AF.Sin, scale=0.5), r=[tC], w=[tA])
            TT(V(tA), V(tA), V(tA), ALU.mult, [tA], [tA])
            dve(lambda e: e.tensor_scalar(out=V(cosr), in0=V(tA), scalar1=-2.0, scalar2=1.0, op0=ALU.mult, op1=ALU.add), [tA], [cosr])
            TT(V(lbre), V(mag), V(cosr), ALU.mult, [mag, cosr], [lbre])
            TT(V(lbim), V(mag), V(sinr), ALU.mult, [mag, sinr], [lbim])
            dve(lambda e: e.tensor_scalar(out=V(tA), in0=V(lbre), scalar1=-1.0, scalar2=1.0, op0=ALU.add, op1=ALU.mult), [lbre], [tA])
            TT(V(tB), V(are), V(are), ALU.mult, [are], [tB])
            TT(V(tC), V(aim), V(aim), ALU.mult, [aim], [tC])
            TT(V(tB), V(tB), V(tC), ALU.add, [tB, tC], [tB])
            dve(lambda e: e.reciprocal(out=V(tB), in_=V(tB)), [tB], [tB])
            TT(V(fre), V(tA), V(are), ALU.mult, [tA, are], [fre])
            TT(V(tC), V(lbim), V(aim), ALU.mult, [lbim, aim], [tC])
            TT(V(fre), V(fre), V(tC), ALU.add, [fre, tC], [fre])
            TT(V(fre), V(fre), V(tB), ALU.mult, [fre, tB], [fre])
            TT(V(fim), V(lbim), V(are), ALU.mult, [lbim, are], [fim])
            TT(V(tC), V(tA), V(aim), ALU.mult, [tA, aim], [tC])
            TT(V(fim), V(fim), V(tC), ALU.subtract, [fim, tC], [fim])
            TT(V(fim), V(fim), V(tB), ALU.mult, [fim, tB], [fim])
            TT(V(tA), V(mag), V(mag), ALU.mult, [mag], [tA])
            dve(lambda e: e.reciprocal(out=V(tB), in_=V(tA)), [tA], [tB])
            TT(V(ilre), V(lbre), V(tB), ALU.mult, [lbre, tB], [ilre])
            TT(V(ilim), V(lbim), V(tB), ALU.mult, [lbim, tB], [ilim])
            dve(lambda e: e.tensor_scalar(out=V(ilim), in0=V(ilim), scalar1=-1.0, scalar2=1.0, op0=ALU.mult, op1=ALU.mult), [ilim], [ilim])
            TT(V(tC), V(tA), V(tA), ALU.mult, [tA], [tC])
            TT(V(mag8), V(tC), V(tC), ALU.mult, [tC], [mag8])
            dve(lambda e: e.tensor_copy(out=CWre.t[:, 0:128], in_=V(cosr)), [cosr], [CWre])
            dve(lambda e: e.tensor_scalar(out=CWim.t[:, 0:128], in0=V(sinr), scalar1=-1.0, scalar2=1.0, op0=ALU.mult, op1=ALU.mult), [sinr], [CWim])
            for k in range(NK - 1):
                a = CWre.t[:, k * 128:(k + 1) * 128]
                b = CWim.t[:, k * 128:(k + 1) * 128]
                TT(V(tA), a, a, ALU.mult, [CWre], [tA])
                TT(V(tB), b, b, ALU.mult, [CWim], [tB])
                TT(CWre.t[:, (k + 1) * 128:(k + 2) * 128], V(tA), V(tB), ALU.subtract, [tA, tB], [CWre])
                TT(V(tC), a, b, ALU.mult, [CWre, CWim], [tC])
                dve(lambda e, k=k: e.tensor_scalar(out=CWim.t[:, (k + 1) * 128:(k + 2) * 128], in0=V(tC), scalar1=2.0, scalar2=1.0,
                                                   op0=ALU.mult, op1=ALU.mult), [tC], [CWim])
            c3r_, c3i_ = CWre.t[:, 3 * 128:4 * 128], CWim.t[:, 3 * 128:4 * 128]
            TT(V(tA), V(H0re), c3r_, ALU.mult, [H0re, CWre], [tA])
            TT(V(tB), V(H0im), c3i_, ALU.mult, [H0im, CWim], [tB])
            TT(V(tA), V(tA), V(tB), ALU.add, [tA, tB], [tA])
            TT(V(I8re), V(tA), V(mag8), ALU.mult, [tA, mag8], [I8re])
            TT(V(tA), V(H0im), c3r_, ALU.mult, [H0im, CWre], [tA])
            TT(V(tB), V(H0re), c3i_, ALU.mult, [H0re, CWim], [tB])
            TT(V(tA), V(tA), V(tB), ALU.subtract, [tA, tB], [tA])
            TT(V(I8im), V(tA), V(mag8), ALU.mult, [tA, mag8], [I8im])
            dve(lambda e: e.tensor_copy(out=H0bre.t[:, :], in_=V(H0re)), [H0re], [H0bre])
            dve(lambda e: e.tensor_copy(out=H0bim.t[:, :], in_=V(H0im)), [H0im], [H0bim])
            for i_ in range(2):
                dve(lambda e, i_=i_: e.memset(RHm[i_].t[:, :], 1.0), [], [RHm[i_]])
            for pos in (256, 288):
                dve(lambda e, pos=pos: e.memset(RHm[0].t[:, pos:pos + 1], 0.0), [], [RHm[0]])
            for pos in (32, 64):
                dve(lambda e, pos=pos: e.memset(RHm[1].t[:, pos:pos + 1], 0.0), [], [RHm[1]])
            self.barrier()
            pre.close()
            TBre = self.sb(ph, "TBre", [128, 4 * NCH], F32)
            TBim = self.sb(ph, "TBim", [128, 4 * NCH], F32)
            tt1 = self.sb(ph, "tt1", [128, 512], F32)
            tt2 = self.sb(ph, "tt2", [128, 512], F32)
            tt3 = self.sb(ph, "tt3", [128, 512], F32)
            SBre = self.sb(ph, "SBre", [128, 8 * 16], F32)
            SBim = self.sb(ph, "SBim", [128, 8 * 16], F32)
            BBre = self.sb(ph, "BBre", [128, 8 * 16], F32)
            BBim = self.sb(ph, "BBim", [128, 8 * 16], F32)
            CTre = self.sb(ph, "CTre", [128, 8 * 16], F32)
            CTim = self.sb(ph, "CTim", [128, 8 * 16], F32)
            SCx = self.sb(ph, "SCx", [32, 8 * 64], F32)
            SC2x = self.sb(ph, "SC2x", [32, 8 * 128], F32)
            LKre, LKim, ILre, ILim = [self.sb(ph, n, [128, 8], F32) for n in ("LKre", "LKim", "ILre", "ILim")]
            PWre = self.sb(ph, "PWre", [128, 9 * 8], F32)
            PWim = self.sb(ph, "PWim", [128, 9 * 8], F32)
            NPre = self.sb(ph, "NPre", [128, 8 * 8], F32)
            NPim = self.sb(ph, "NPim", [128, 8 * 8], F32)
            t8 = self.sb(ph, "t8", [128, 8], F32)
            FAre = self.sb(ph, "FAre", [128, 512], F32)
            FAim = self.sb(ph, "FAim", [128, 512], F32)
            U = [self.sb(ph, f"U{i}", [128, NCH], BF16) for i in range(8)]
            Zst = [self.sb(ph, f"Zst{i}", [128, 128], F32) for i in range(4)]
            WS = [[self.sb(ph, f"WS{s}_{i}", [128, 128], BF16) for i in range(4)] for s in range(2)]
            ZT = [[self.sb(ph, f"ZT{s}_{i}", [128, 128], BF16) for i in range(4)] for s in range(2)]
            YTn = [[self.sb(ph, f"YTn{s}_{i}", [128, 128], BF16) for i in range(2)] for s in range(2)]
            ZO = [[self.sb(ph, f"ZO{c}_{i}", [128, 128], BF16) for i in range(4)] for c in range(8)]
            HBre = [self.sb(ph, f"HBre{c}", [128, NCH], BF16) for c in range(8)]
            HBim = [self.sb(ph, f"HBim{c}", [128, NCH], BF16) for c in range(8)]
            Tacc = [self.sb(ph, f"Tacc{g}", [128, 128], F32) for g in range(8)]
            Tb = [self.sb(ph, f"Tb{g}", [128, 128], BF16) for g in range(8)]
            Yhi = [self.sb(ph, f"Yhi{g}", [128, NCH], BF16) for g in range(8)]
            Ylo = [self.sb(ph, f"Ylo{g}", [128, NCH], BF16) for g in range(8)]
            m1, m2, m3, m4 = [self.sb(ph, f"m{i}", [128, NCH], F32) for i in range(4)]
            zre, zim, hsre, hsim = [self.sb(ph, n, [128, NCH], F32) for n in ("zre", "zim", "hsre", "hsim")]
            ini = self.sb(ph, "ini", [128, 8], F32)
            for z_ in Zst:
                dve(lambda e, z_=z_: e.memset(z_.t[:, :], 0.0), [], [z_])
            for lst in (WS[0], WS[1], ZT[0], ZT[1]) + tuple(ZO):
                for z_ in lst:
                    dve(lambda e, z_=z_: e.memset(z_.t[:, :], 0.0), [], [z_])
            for hb in HBre + HBim:
                dve(lambda e, hb=hb: e.memset(hb.t[:, :], 0.0), [], [hb])

            seti = 0
            for kc in range(KC):
                for gl in range(8):
                    ps = self.nextps()
                    for i in range(8):
                        self.op("pe", lambda e, i=i: e.matmul(ps.t[:, 0:NCH], lhsT=selw(gl, i),
                                                             rhs=HT.t[:, kc * T + i:(kc + 1) * T:8], start=(i == 0), stop=(i == 7)),
                                r=[Sel] + self.HTb, w=[ps], sig=(i == 7))
                    self.op("act", lambda e: e.activation(out=U[gl].t[:, :], in_=ps.t[:, 0:NCH], func=AF.Identity), r=[ps], w=[U[gl]])
                for (SB_, src) in ((SBre, self.ssm_b_re), (SBim, self.ssm_b_im)):
                    for d in range(2):
                        self.dma("sp", SB_.t[:, d * 64:(d + 1) * 64].rearrange("p (c k) -> p c k", k=16),
                                 src[d, 8 * kc:8 * kc + 8].rearrange("(pq g2) p ci -> (g2 p) pq ci", g2=2), r=[self.INb], w=[SB_])
                for (src, CT_) in ((self.ssm_c_re, CTre), (self.ssm_c_im, CTim)):
                    for d in range(2):
                        self.dma("sp", SCx.t[:, d * 256:(d + 1) * 256].rearrange("p (c k) -> p c k", k=64),
                                 src[d, 8 * kc:8 * kc + 8].rearrange("(pq g2) co p -> (g2 co) pq p", g2=2), r=[self.INb], w=[SCx])
                    for rep in range(2):
                        dve(lambda e, rep=rep: e.tensor_copy(
                            out=SC2x.t[:, :].rearrange("p (c r k) -> p c r k", r=2, k=64)[:, :, rep, :],
                            in_=SCx.t[:, :].rearrange("p (c k) -> p c k", k=64)), [SCx], [SC2x])
                    for cc in range(8):
                        pst = self.nextps()
                        self.op("pe", lambda e: e.transpose(out=pst.t[:, 0:32], in_=SC2x.t[:, cc * 128:(cc + 1) * 128], identity=self.ident.t[0:32, 0:32]),
                                r=[SC2x, self.ident], w=[pst])
                        for g2 in range(2):
                            self.op("act", lambda e, g2=g2: e.activation(out=CT_.t[g2 * 64:(g2 + 1) * 64, cc * 16:(cc + 1) * 16],
                                                                         in_=pst.t[g2 * 64:(g2 + 1) * 64, 16 * g2:16 * g2 + 16], func=AF.Identity),
                                    r=[pst], w=[CT_])
                for d in range(2):
                    c0 = d * 64 + 4 * kc
                    fr = self.bc(fre.t[:, c0:c0 + 4], 16)
                    fi = self.bc(fim.t[:, c0:c0 + 4], 16)
                    sl = slice(d * 64, (d + 1) * 64)
                    v3 = lambda b: b.t[:, sl].rearrange("p (c k) -> p c k", k=16)
                    t3 = tt1.t[:, 0:64].rearrange("p (c k) -> p c k", k=16)
                    TT(v3(BBre), v3(SBre), fr, ALU.mult, [SBre, fre], [BBre])
                    TT(t3, v3(SBim), fi, ALU.mult, [SBim, fim], [tt1])
                    TT(v3(BBre), v3(BBre), t3, ALU.subtract, [BBre, tt1], [BBre])
                    TT(v3(BBim), v3(SBre), fi, ALU.mult, [SBre, fim], [BBim])
                    TT(t3, v3(SBim), fr, ALU.mult, [SBim, fre], [tt1])
                    TT(v3(BBim), v3(BBim), t3, ALU.add, [BBim, tt1], [BBim])
                for d in range(2):
                    c0 = d * 64 + 4 * kc
                    for (dst, src) in ((LKre, lbre), (LKim, lbim), (ILre, ilre), (ILim, ilim)):
                        dve(lambda e, dst=dst, src=src: e.tensor_copy(out=dst.t[:, d * 4:(d + 1) * 4], in_=src.t[:, c0:c0 + 4]), [src], [dst])
                for (Pr, Pi, Lr, Li, nk) in ((PWre, PWim, LKre, LKim, 9), (NPre, NPim, ILre, ILim, 8)):
                    dve(lambda e: e.memset(Pr.t[:, 0:8], 1.0), [], [Pr])
                    dve(lambda e: e.memset(Pi.t[:, 0:8], 0.0), [], [Pi])
                    for k in range(nk - 1):
                        a, b = Pr.t[:, k * 8:(k + 1) * 8], Pi.t[:, k * 8:(k + 1) * 8]
                        a2, b2 = Pr.t[:, (k + 1) * 8:(k + 2) * 8], Pi.t[:, (k + 1) * 8:(k + 2) * 8]
                        TT(a2, a, V(Lr), ALU.mult, [Pr, Lr], [Pr])
                        TT(V(t8), b, V(Li), ALU.mult, [Pi, Li], [t8])
                        TT(a2, a2, V(t8), ALU.subtract, [Pr, t8], [Pr])
                        TT(b2, a, V(Li), ALU.mult, [Pr, Li], [Pi])
                        TT(V(t8), b, V(Lr), ALU.mult, [Pi, Lr], [t8])
                        TT(b2, b2, V(t8), ALU.add, [Pi, t8], [Pi])

                def factor(d, Are, Aim, Pr, Pi, kbase, kstep):
                    cc0 = d * 4
                    a_re = self.ap4(Are, cc0 * 16, [[16, 4], [0, 8], [1, 16]])
                    a_im = self.ap4(Aim, cc0 * 16, [[16, 4], [0, 8], [1, 16]])
                    p_re = self.ap4(Pr, kbase * 8 + cc0, [[1, 4], [kstep * 8, 8], [0, 16]])
                    p_im = self.ap4(Pi, kbase * 8 + cc0, [[1, 4], [kstep * 8, 8], [0, 16]])
                    o_re = FAre.t[:, :].rearrange("p (c i k) -> p c i k", c=4, i=8)
                    o_im = FAim.t[:, :].rearrange("p (c i k) -> p c i k", c=4, i=8)
                    tmp = tt3.t[:, :].rearrange("p (c i k) -> p c i k", c=4, i=8)
                    TT(o_re, a_re, p_re, ALU.mult, [Are, Pr], [FAre], eng="pool")
                    TT(tmp, a_im, p_im, ALU.mult, [Aim, Pi], [tt3], eng="pool")
                    TT(o_re, o_re, tmp, ALU.subtract, [FAre, tt3], [FAre], eng="pool")
                    TT(o_im, a_re, p_im, ALU.mult, [Are, Pi], [FAim], eng="pool")
                    TT(tmp, a_im, p_re, ALU.mult, [Aim, Pr], [tt3], eng="pool")
                    TT(o_im, o_im, tmp, ALU.add, [FAim, tt3], [FAim], eng="pool")

                def half_copy(dst_tiles, src, pq, scale, eng_alt=0):
                    for g2 in range(2):
                        self.op("act", lambda e, g2=g2: e.activation(out=dst_tiles[g2].t[g2 * 64:(g2 + 1) * 64, :],
                                                                     in_=src.t[g2 * 64:(g2 + 1) * 64, pq * 128:(pq + 1) * 128],
                                                                     func=AF.Identity, scale=scale), r=[src], w=[dst_tiles[g2]])

                for d in range(2):
                    c0 = d * 64 + 4 * kc
                    tbo = 0 if d == 0 else 64
                    tre3 = TBre.t[:, :].rearrange("p (c j) -> p c j", j=NCH)[:, :, tbo:tbo + 256]
                    tim3 = TBim.t[:, :].rearrange("p (c j) -> p c j", j=NCH)[:, :, tbo:tbo + 256]
                    dve(lambda e: e.memset(tre3[:, :, 0:1], 1.0), [], [TBre])
                    dve(lambda e: e.memset(tim3[:, :, 0:1], 0.0), [], [TBim])
                    for k in range(8):
                        s = 1 << k
                        wr = self.bc(CWre.t[:, (3 + k) * 128 + c0:(3 + k) * 128 + c0 + 4], s)
                        wi = self.bc(CWim.t[:, (3 + k) * 128 + c0:(3 + k) * 128 + c0 + 4], s)
                        a = tre3[:, :, 0:s]
                        b = tim3[:, :, 0:s]
                        u1 = tt1.t[:, 0:4 * s].rearrange("p (c j) -> p c j", j=s)
                        u2 = tt2.t[:, 0:4 * s].rearrange("p (c j) -> p c j", j=s)
                        TT(u1, a, wr, ALU.mult, [TBre, CWre], [tt1])
                        TT(u2, b, wi, ALU.mult, [TBim, CWim], [tt2])
                        TT(tre3[:, :, s:2 * s], u1, u2, ALU.subtract, [tt1, tt2], [TBre])
                        TT(u1, a, wi, ALU.mult, [TBre, CWim], [tt1])
                        TT(u2, b, wr, ALU.mult, [TBim, CWre], [tt2])
                        TT(tim3[:, :, s:2 * s], u1, u2, ALU.add, [tt1, tt2], [TBim])
                    for TB_ in (TBre, TBim):
                        full3 = TB_.t[:, :].rearrange("p (c j) -> p c j", j=NCH)
                        for dst0 in ((256, 288) if d == 0 else (0, 32)):
                            self.op("pool", lambda e, full3=full3, dst0=dst0: e.tensor_copy(out=full3[:, :, dst0:dst0 + 32], in_=full3[:, :, tbo:tbo + 32]),
                                    r=[TB_], w=[TB_])
                    if d == 0:
                        factor(d, BBre, BBim, PWre, PWim, 7, -1)
                    else:
                        factor(d, BBre, BBim, PWre, PWim, 0, 1)
                    ws_sets = []
                    for pq in range(4):
                        half_copy([Zst[0], Zst[1]], FAre, pq, 1.0)
                        half_copy([Zst[2], Zst[3]], FAim, pq, 1.0)
                        wset = WS[seti % 2]
                        seti += 1
                        for q4 in range(4):
                            pst = self.nextps()
                            self.op("pe", lambda e: e.transpose(out=pst.t[:, 0:128], in_=Zst[q4].t[:, :], identity=self.ident.t[:, :]),
                                    r=[Zst[q4], self.ident], w=[pst])
                            self.op("dve", lambda e: e.tensor_copy(out=wset[q4].t[:, :], in_=pst.t[:, 0:128]), r=[pst], w=[wset[q4]])
                        if d == 1:
                            zt = ZT[seti % 2]
                            half_copy([zt[0], zt[1]], FAre, pq, 1.0)
                            half_copy([zt[2], zt[3]], FAim, pq, -1.0)
                            ws_sets.append((wset, zt))
                        else:
                            ws_sets.append((wset, None))
                        cc = d * 4 + pq
                        col = c0 + pq
                        psS = [self.nextps(), self.nextps()]
                        for ri_ in range(2):
                            for g2 in range(2):
                                self.op("pe", lambda e, g2=g2, ri_=ri_: e.matmul(psS[ri_].t[:, 0:NCH], lhsT=wset[ri_ * 2 + g2].t[:, :],
                                                                                  rhs=U[pq * 2 + g2].t[:, :], start=(g2 == 0), stop=(g2 == 1)),
                                        r=[wset[ri_ * 2 + g2], U[pq * 2 + g2]], w=[psS[ri_]], sig=(g2 == 1))
                        hbre, hbim = HBre[cc], HBim[cc]
                        rho = mag8.t[:, col:col + 1]
                        L = NCH
                        Tr = TBre.t[:, pq * NCH:(pq + 1) * NCH]
                        Ti = TBim.t[:, pq * NCH:(pq + 1) * NCH]
                        bre = psS[0].t[:, 0:L] if d == 0 else self.rev(psS[0].t[:, 0:L])
                        bim = psS[1].t[:, 0:L] if d == 0 else self.rev(psS[1].t[:, 0:L])
                        TT(m1.t[:, 0:L], bre, Tr, ALU.mult, [psS[0], TBre], [m1])
                        TT(m4.t[:, 0:L], bre, Ti, ALU.mult, [psS[0], TBim], [m4])
                        TT(m2.t[:, 0:L], bim, Ti, ALU.mult, [psS[1], TBim], [m2])
                        TT(m3.t[:, 0:L], bim, Tr, ALU.mult, [psS[1], TBre], [m3])
                        TT(zre.t[:, 0:L], m1.t[:, 0:L], m2.t[:, 0:L], ALU.subtract, [m1, m2], [zre], eng="pool")
                        TT(zim.t[:, 0:L], m3.t[:, 0:L], m4.t[:, 0:L], ALU.add, [m3, m4], [zim], eng="pool")
                        pS = 0 if d == 0 else 64
                        TT(zre.t[:, pS:pS + 1], zre.t[:, pS:pS + 1], I8re.t[:, col:col + 1], ALU.add, [zre, I8re], [zre])
                        TT(zim.t[:, pS:pS + 1], zim.t[:, pS:pS + 1], I8im.t[:, col:col + 1], ALU.add, [zim, I8im], [zim])
                        RHv = tt1.t[:, 0:NCH]
                        dve(lambda e: e.tensor_scalar(out=RHv, in0=RHm[d].t[:, :], scalar1=rho, scalar2=1.0, op0=ALU.mult, op1=ALU.mult),
                            [RHm[d], mag8], [tt1])
                        dve(lambda e: e.tensor_tensor_scan(out=hsre.t[:, 0:L], data0=RHv, data1=zre.t[:, 0:L],
                                                           initial=0.0, op0=ALU.mult, op1=ALU.add), [tt1, zre], [hsre])
                        dve(lambda e: e.tensor_tensor_scan(out=hsim.t[:, 0:L], data0=RHv, data1=zim.t[:, 0:L],
                                                           initial=0.0, op0=ALU.mult, op1=ALU.add), [tt1, zim], [hsim])
                        TT(m1.t[:, 0:L], hsre.t[:, 0:L], Tr, ALU.mult, [hsre, TBre], [m1])
                        TT(m2.t[:, 0:L], hsim.t[:, 0:L], Ti, ALU.mult, [hsim, TBim], [m2])
                        TT(m3.t[:, 0:L], hsim.t[:, 0:L], Tr, ALU.mult, [hsim, TBre], [m3])
                        TT(m4.t[:, 0:L], hsre.t[:, 0:L], Ti, ALU.mult, [hsre, TBim], [m4])
                        ore = hbre.t[:, 0:L] if d == 0 else self.rev(hbre.t[:, 0:L])
                        oim = hbim.t[:, 0:L] if d == 0 else self.rev(hbim.t[:, 0:L])
                        TT(ore, m1.t[:, 0:L], m2.t[:, 0:L], ALU.add, [m1, m2], [hbre], eng="pool")
                        TT(oim, m3.t[:, 0:L], m4.t[:, 0:L], ALU.subtract, [m3, m4], [hbim], eng="pool")
                        for sq_ in (1, 2):
                            lp = (287 if sq_ == 1 else 319) if d == 0 else (63 if sq_ == 1 else 31)
                            fcol = ((sq_ - 1) * 2 + d) * 64 + (4 * kc + pq)
                            TT(FSre.t[:, fcol:fcol + 1], m1.t[:, lp:lp + 1], m2.t[:, lp:lp + 1], ALU.add, [m1, m2], [FSre])
                            TT(FSim.t[:, fcol:fcol + 1], m3.t[:, lp:lp + 1], m4.t[:, lp:lp + 1], ALU.subtract, [m3, m4], [FSim])
                    if d == 0:
                        factor(d, BBre, BBim, NPre, NPim, 0, 1)
                        for pq in range(4):
                            zt = ZT[(seti + pq) % 2]
                            half_copy([zt[0], zt[1]], FAre, pq, 1.0)
                            half_copy([zt[2], zt[3]], FAim, pq, -1.0)
                            ws_sets[pq] = (ws_sets[pq][0], zt)
                            self._s5_tgen_pending = None
                    if d == 0:
                        factor(d, CTre, CTim, PWre, PWim, 0, 1)
                    else:
                        factor(d, CTre, CTim, NPre, NPim, 0, 1)
                    for pq in range(4):
                        yt = YTn[pq % 2]
                        self.op("act", lambda e: e.activation(out=yt[0].t[:, :], in_=FAre.t[:, pq * 128:(pq + 1) * 128], func=AF.Identity), r=[FAre], w=[yt[0]])
                        self.op("act", lambda e: e.activation(out=yt[1].t[:, :], in_=FAim.t[:, pq * 128:(pq + 1) * 128], func=AF.Identity), r=[FAim], w=[yt[1]])
                        zt = ws_sets[pq][1]
                        for g2 in range(2):
                            gl = pq * 2 + g2
                            pT = self.nextps()
                            self.op("pe", lambda e: e.matmul(pT.t[:, 0:128], lhsT=zt[g2].t[:, :], rhs=yt[0].t[:, :], start=True, stop=False),
                                    r=[zt[g2], yt[0]], w=[pT], sig=False)
                            self.op("pe", lambda e: e.matmul(pT.t[:, 0:128], lhsT=zt[2 + g2].t[:, :], rhs=yt[1].t[:, :], start=False, stop=True),
                                    r=[zt[2 + g2], yt[1]], w=[pT])
                            if d == 0:
                                TT(Tacc[gl].t[:, :], pT.t[:, 0:128], V(maskF), ALU.mult, [pT, maskF], [Tacc[gl]])
                            else:
                                TT(tt1.t[:, 0:128], pT.t[:, 0:128], V(maskB), ALU.mult, [pT, maskB], [tt1])
                                TT(Tb[gl].t[:, :], Tacc[gl].t[:, :], tt1.t[:, 0:128], ALU.add, [Tacc[gl], tt1], [Tb[gl]], eng="pool")
                    if d == 0:
                        factor(d, CTre, CTim, PWre, PWim, 1, 1)
                    else:
                        factor(d, CTre, CTim, PWre, PWim, 8, -1)
                    for pq in range(4):
                        zo = ZO[d * 4 + pq]
                        half_copy([zo[0], zo[1]], FAre, pq, 1.0)
                        half_copy([zo[2], zo[3]], FAim, pq, -1.0)
                for gl in range(8):
                    pq, g2 = gl // 2, gl % 2
                    pY = self.nextps()
                    self.op("pe", lambda e: e.matmul(pY.t[:, 0:NCH], lhsT=Tb[gl].t[:, :], rhs=U[gl].t[:, :], start=True, stop=False),
                            r=[Tb[gl], U[gl]], w=[pY], sig=False)
                    for d in range(2):
                        cc = d * 4 + pq
                        zo = ZO[cc]
                        col = d * 64 + 4 * kc + pq
                        pieces = []
                        for sq_ in range(3):
                            n0, L = SEQC[sq_]
                            if d == 0:
                                pieces.append((n0 + 1, L - 1, HBre[cc].t[:, n0:n0 + L - 1], HBim[cc].t[:, n0:n0 + L - 1], [HBre[cc]], [HBim[cc]]))
                            else:
                                pieces.append((n0, L - 1, HBre[cc].t[:, n0 + 1:n0 + L], HBim[cc].t[:, n0 + 1:n0 + L], [HBre[cc]], [HBim[cc]]))
                        ic = 0 if d == 0 else 255
                        pieces.append((ic, 1, H0bre.t[:, col:col + 1], H0bim.t[:, col:col + 1], [H0bre], [H0bim]))
                        for pi_, (o0, Ln, rre, rim, bre_, bim_) in enumerate(pieces):
                            lastmm = (d == 1 and pi_ == len(pieces) - 1)
                            self.op("pe", lambda e: e.matmul(pY.t[:, o0:o0 + Ln], lhsT=zo[g2].t[:, :], rhs=rre, start=False, stop=False),
                                    r=[zo[g2]] + bre_, w=[pY], sig=False)
                            self.op("pe", lambda e: e.matmul(pY.t[:, o0:o0 + Ln], lhsT=zo[2 + g2].t[:, :], rhs=rim, start=False, stop=lastmm),
                                    r=[zo[2 + g2]] + bim_, w=[pY], sig=lastmm)
                    self.op("act", lambda e: e.activation(out=Yhi[gl].t[:, :], in_=pY.t[:, 0:NCH], func=AF.Identity), r=[pY], w=[Yhi[gl]])
                    TT(Ylo[gl].t[:, :], pY.t[:, 0:NCH], Yhi[gl].t[:, :], ALU.subtract, [pY, Yhi[gl]], [Ylo[gl]])
                for j in range(8):
                    pU = self.nextps()
                    for gl in range(8):
                        self.op("pe", lambda e: e.matmul(pU.t[:, 0:NCH], lhsT=selw(j, gl), rhs=Yhi[gl].t[:, :],
                                                         start=(gl == 0), stop=False), r=[Sel, Yhi[gl]], w=[pU], sig=False)
                        self.op("pe", lambda e: e.matmul(pU.t[:, 0:NCH], lhsT=selw(j, gl), rhs=Ylo[gl].t[:, :],
                                                         start=False, stop=(gl == 7)), r=[Sel, Ylo[gl]], w=[pU], sig=(gl == 7))
                    self.op("act", lambda e: e.activation(out=YK.t[:, j:T:8], in_=pU.t[:, 0:NCH], func=AF.Identity), r=[pU], w=[YK])
                self.dma("sp", self.YT[kc], YK.t[:, :], r=[YK], w=self.YTb)
            for (FS, dst) in ((FSre, self.new_sre), (FSim, self.new_sim)):
                for hlf in range(2):
                    pst = self.nextps()
                    self.op("pe", lambda e, hlf=hlf, FS=FS: e.transpose(out=pst.t[:, 0:128], in_=FS.t[:, hlf * 128:(hlf + 1) * 128], identity=self.ident.t[:, :]),
                            r=[FS, self.ident], w=[pst])
                    self.op("act", lambda e: e.activation(out=fso.t[:, :], in_=pst.t[:, 0:128], func=AF.Identity), r=[pst], w=[fso])
                    self.dma("sp", dst[hlf * 128:(hlf + 1) * 128, :], fso.t[:, :], r=[fso], w=[self.OUTb])
            self.barrier()

    def phase_glu(self):
        l = 1
        C1 = 2.0 * 0.7978845608028654
        with ExitStack() as ph:
            xt = self.sb(ph, "gx", [128, KC * TW], F32)
            yt = self.sb(ph, "gy", [128, KC * TW], F32)
            GB = self.sb(ph, "gb", [128, KC * TW], BF16)
            wblk = [self.sb(ph, f"gw{i}", [128, KC * 512], BF16) for i in range(2)]
            ta = [self.sb(ph, f"gta{i}", [128, TW], F32) for i in range(2)]
            tb_ = [self.sb(ph, f"gtb{i}", [128, TW], F32) for i in range(2)]
            sg = [self.sb(ph, f"gsg{i}", [128, TW], F32) for i in range(2)]
            DS = self.sb(ph, "DS", [128, 16], F32)
            BG = self.sb(ph, "BG", [128, 16], F32)
            rst = self.sb(ph, "rst", [128, TW], F32)
            self.load_fm("ds", self.ssm_d.rearrange("(k p) -> k p", p=128), 16, DS.t[:, :], DS, self.INb)
            self.load_fm("bg", self.b_glu.rearrange("(k p) -> k p", p=128), 16, BG.t[:, :], BG, self.INb)
            wi = 0
            for tt in range(NT):
                n = 0 if tt < 4 else 1
                tok = slice(tt * TW, (tt + 1) * TW)
                self.dma("sp", xt.t[:, :].rearrange("p (k t) -> p k t", k=KC), self.xt_dram(tt), r=[self.XTb[tt]], w=[xt])
                self.dma("sp", yt.t[:, :].rearrange("p (k t) -> p k t", k=KC),
                         self.YT[:, :, tt * TW:(tt + 1) * TW].rearrange("k p t -> p k t"), r=[self.YTb[tt]], w=[yt])
                self.dma("sp", rst.t[:, :], self.RSD[:, tt * TW:(tt + 1) * TW], r=[self.RSDb], w=[rst])
                for kc in range(KC):
                    a, b = ta[kc % 2], tb_[kc % 2]
                    ks = slice(kc * TW, (kc + 1) * TW)
                    self.op("dve", lambda e: e.scalar_tensor_tensor(out=a.t[:, :], in0=xt.t[:, ks], scalar=self.av(l, 0, kc, n), in1=rst.t[:, :],
                                                                    op0=ALU.mult, op1=ALU.mult), r=[xt, rst, self.AV], w=[a])
                    self.op("act", lambda e: e.activation(out=a.t[:, :], in_=a.t[:, :], func=AF.Identity, bias=self.modv(l, 0, kc, n), scale=1.0),
                            r=[a, self.MOD], w=[a])
                    self.op("dve", lambda e: e.scalar_tensor_tensor(out=yt.t[:, ks], in0=a.t[:, :], scalar=DS.t[:, kc:kc + 1], in1=yt.t[:, ks],
                                                                    op0=ALU.mult, op1=ALU.add), r=[a, DS, yt], w=[yt])
                    self.op("dve", lambda e: e.tensor_tensor(out=b.t[:, :], in0=yt.t[:, ks], in1=yt.t[:, ks], op=ALU.mult), r=[yt], w=[b])
                    self.op("dve", lambda e: e.tensor_scalar(out=b.t[:, :], in0=b.t[:, :], scalar1=0.044715, scalar2=1.0, op0=ALU.mult, op1=ALU.add),
                            r=[b], w=[b])
                    self.op("dve", lambda e: e.tensor_tensor(out=b.t[:, :], in0=b.t[:, :], in1=yt.t[:, ks], op=ALU.mult), r=[b, yt], w=[b])
                    self.op("act", lambda e: e.activation(out=b.t[:, :], in_=b.t[:, :], func=AF.Sigmoid, scale=C1), r=[b], w=[b])
                    self.op("dve", lambda e: e.tensor_tensor(out=yt.t[:, ks], in0=yt.t[:, ks], in1=b.t[:, :], op=ALU.mult), r=[b, yt], w=[yt])
                    self.op("act", lambda e: e.activation(out=GB.t[:, ks], in_=yt.t[:, ks], func=AF.Identity), r=[yt], w=[GB])
                for mcb in range(4):
                    wb = wblk[wi % 2]
                    wi += 1
                    self.dma("pool", wb.t[:, :].rearrange("p (k m) -> p k m", k=KC),
                             self.w_glu[:, mcb * 512:(mcb + 1) * 512].rearrange("(k p) m -> p k m", p=128), r=[self.INb], w=[wb])
                    for m2 in range(4):
                        mc = mcb * 4 + m2
                        ms_ = slice(mc * TW, (mc + 1) * TW)
                        ps = self.nextps()
                        for kc in range(KC):
                            self.op("pe", lambda e, kc=kc: e.matmul(ps.t[:, :], lhsT=wb.t[:, kc * 512 + m2 * 128: kc * 512 + (m2 + 1) * 128],
                                                                  rhs=GB.t[:, kc * TW:(kc + 1) * TW], start=(kc == 0), stop=(kc == KC - 1)),
                                    r=[wb, GB], w=[ps], sig=(kc == KC - 1))
                        s_ = sg[mc % 2]
                        self.op("act", lambda e: e.activation(out=s_.t[:, :], in_=ps.t[:, :], func=AF.Sigmoid, bias=BG.t[:, mc:mc + 1], scale=1.0),
                                r=[ps, BG], w=[s_])
                        self.op("dve", lambda e: e.tensor_tensor(out=s_.t[:, :], in0=s_.t[:, :], in1=yt.t[:, ms_], op=ALU.mult), r=[s_, yt], w=[s_])
                        self.op("dve", lambda e: e.scalar_tensor_tensor(out=xt.t[:, ms_], in0=s_.t[:, :], scalar=self.modv(l, 2, mc, n), in1=xt.t[:, ms_],
                                                                        op0=ALU.mult, op1=ALU.add), r=[s_, xt, self.MOD], w=[xt])
                self.dma("sp", self.xt_dram(tt), xt.t[:, :].rearrange("p (k t) -> p k t", k=KC), r=[xt], w=[self.XTb[tt]])
            self.barrier()


def _rope_consts():
    half = 64
    inv = (10000.0 ** (-np.arange(0, half, 2, dtype=np.float32) / np.float32(half))).astype(np.float32)
    r, col = np.meshgrid(np.arange(32), np.arange(64), indexing="ij")
    r = r.reshape(-1).astype(np.float32)
    col = col.reshape(-1).astype(np.float32)
    ar = r[:, None] * inv
    ac = col[:, None] * inv
    ang = np.concatenate([ar, ar, ac, ac], axis=-1)
    cosT = np.ascontiguousarray(np.cos(ang).T.astype(np.float32))
    sinT = np.ascontiguousarray(np.sin(ang).T.astype(np.float32))
    P = np.zeros((128, 128), np.float32)
    for a in range(2):
        for i in range(32):
            P[a * 64 + i, a * 64 + 32 + i] = -1.0
            P[a * 64 + 32 + i, a * 64 + i] = 1.0
    return cosT, sinT, np.ascontiguousarray(P.T)


def _s5_consts():
    sel = np.zeros((128, 8, 240), np.float32)
    for a in range(8):
        for c in range(16):
            sel[a * 16 + c, a, 112 + c] = 1.0
    ii = np.arange(128) // 16
    maskf = (ii[None, :] >= ii[:, None]).astype(np.float32)
    maskb = (ii[:, None] >= ii[None, :]).astype(np.float32)
    return np.ascontiguousarray(sel.reshape(128, 8 * 240)), maskf, maskb


_SEL, _MASKF, _MASKB = _s5_consts()
_CACHE = {}


def _get_prog(debug=False, stop_after=None):
    key = (debug, stop_after)
    if key not in _CACHE:
        mk = MK(debug=debug, stop_after=stop_after)
        mk.build()
        _CACHE[key] = mk
    return _CACHE[key]


def make_in_maps(inp, cores):
    cosT, sinT, ropeT = _rope_consts()
    ident = np.eye(128, dtype=np.float32)
    f = lambda a: np.ascontiguousarray(np.asarray(a, dtype=np.float32))
    shared = {
        "w_mod": f(inp["w_mod"]), "b_mod": f(inp["b_mod"]), "norm_g": f(inp["norm_g"]),
        "w_qkv": f(inp["w_qkv"][0]), "lam_vecs": f(inp["lam_vecs"][0]), "subln_g": f(inp["subln_g"][0]),
        "w_o": f(inp["w_o"][0]),
        "ssm_a_re": f(inp["ssm_a_re"][0]), "ssm_a_im": f(inp["ssm_a_im"][0]), "ssm_log_step": f(inp["ssm_log_step"][0]),
        "ssm_b_re": f(inp["ssm_b_re"][0]), "ssm_b_im": f(inp["ssm_b_im"][0]),
        "ssm_c_re": f(inp["ssm_c_re"][0]), "ssm_c_im": f(inp["ssm_c_im"][0]),
        "ssm_d": f(inp["ssm_d"][0]), "w_glu": f(inp["w_glu"][0]), "b_glu": f(inp["b_glu"][0]),
        "w_up": f(inp["w_up"]), "conv_w": f(inp["conv_w"]), "conv_b": f(inp["conv_b"]), "w_down": f(inp["w_down"]),
        "final_g": f(inp["final_g"]),
        "c_ident": ident, "c_ropeT": ropeT, "c_cos": cosT, "c_sin": sinT,
        "c_sel": _SEL, "c_maskf": _MASKF, "c_maskb": _MASKB,
    }
    maps = []
    for c in cores:
        m = dict(shared)
        m["x_all"] = np.ascontiguousarray(np.concatenate(
            [f(inp["x_sample"][c]), f(inp["x_prompt"][2 * c]), f(inp["x_prompt"][2 * c + 1])], axis=0))
        m["cond"] = np.ascontiguousarray(np.stack([f(inp["c"][c]), f(inp["c_ctx"])], axis=0))
        m["cache_k"] = f(inp["cache_k"][c, 0]).reshape(256, 2048)
        m["cache_v"] = f(inp["cache_v"][c, 0]).reshape(256, 2048)
        m["st_re"] = f(inp["state_re"][c, 0])
        m["st_im"] = f(inp["state_im"][c, 0])
        maps.append(m)
    return maps


def kernel(**inputs):
    mk = _get_prog()
    cores = list(range(NCORES))
    maps = make_in_maps(inputs, cores)
    res = run_bass_kernel_spmd(mk.nc, maps, core_ids=cores)
    R = res.results
    y_prompt = np.zeros((16, 256, D), np.float32)
    y_sample = np.zeros((8, 2048, D), np.float32)
    nk = np.zeros((16, 1, 256, 8, 2, 128), np.float32)
    nv = np.zeros((16, 1, 256, 8, 256), np.float32)
    sre = np.zeros((16, 1, 2, 128, 64), np.float32)
    sim = np.zeros((16, 1, 2, 128, 64), np.float32)
    for c in cores:
        r = R[c]
        ya = np.asarray(r["y_all"])
        y_sample[c] = ya[0:2048]
        y_prompt[2 * c] = ya[2048:2304]
        y_prompt[2 * c + 1] = ya[2304:2560]
        k = np.asarray(r["new_k"]).reshape(2, 256, 8, 2, 128)
        v = np.asarray(r["new_v"]).reshape(2, 256, 8, 256)
        nk[2 * c:2 * c + 2, 0] = k
        nv[2 * c:2 * c + 2, 0] = v
        a = np.asarray(r["new_sre"]).reshape(2, 2, 64, 2, 64).reshape(2, 2, 128, 64)
        b = np.asarray(r["new_sim"]).reshape(2, 2, 64, 2, 64).reshape(2, 2, 128, 64)
        sre[2 * c:2 * c + 2, 0] = a
        sim[2 * c:2 * c + 2, 0] = b
    return (y_prompt, y_sample, nk, nv, sre, sim)
```

```python
import math
import os
from contextlib import ExitStack

import numpy as np
import concourse.bass as bass
import concourse.mybir as mybir
from concourse.bass_utils import run_bass_kernel_spmd

F32 = mybir.dt.float32
BF16 = mybir.dt.bfloat16
I32 = mybir.dt.int32
ALU = mybir.AluOpType
AF = mybir.ActivationFunctionType

NCORES = 8
D = 2048
KC = 16
T = 2560
NT = 5
TW = 512
DFF = 5632
JC = 44
NH = 8
SEQS = [(0, 2048), (2048, 256), (2304, 256)]
NORM_EPS = 1e-6
SUBLN_EPS = 1e-5
LAM_INIT0 = 0.8 - 0.6 * math.exp(-0.3 * 0)
ATT_SCALE = 128 ** -0.5
CPAD = [1, 2051, 2309]
CW_TOT = 2566


class Buf:
    __slots__ = ("name", "t", "lw", "rd", "psum")

    def __init__(self, name, t=None, psum=False):
        self.name = name
        self.t = t
        self.lw = None
        self.rd = {}
        self.psum = psum


class MK:
    def __init__(self, debug=False, stop_after=None):
        self.debug = debug
        self.stop_after = stop_after
        self.nc = bass.Bass("TRN2", target_bir_lowering=False)
        nc = self.nc
        self.es = ExitStack()
        self.engs = {"pe": nc.tensor, "act": nc.scalar, "dve": nc.vector, "pool": nc.gpsimd, "sp": nc.sync}
        self.sems = {}
        self.cnt = {}
        for k in self.engs:
            self.sems["e_" + k] = self.es.enter_context(nc.semaphore("e_" + k))
            self.cnt["e_" + k] = 0
        self.ndq = {"sp": 14, "pool": 14}
        self.drr = {"sp": 0, "pool": 0}
        for q, n in self.ndq.items():
            for i in range(n):
                key = f"d_{q}_{i}"
                self.sems[key] = self.es.enter_context(nc.semaphore(key))
                self.cnt[key] = 0
        self.waited = {}
        self.ps = []
        for i in range(8):
            t = self.es.enter_context(nc.psum_tensor(f"ps{i}", [128, 512], F32))
            self.ps.append(Buf(f"ps{i}", t, psum=True))
        self.psrr = 0
        self.dram_in = {}
        self.dram_out = {}

    def cur(self, key):
        return self.cnt[key] * (16 if key.startswith("d_") else 1)

    def wait(self, eng, ev):
        if ev is None:
            return
        key, val = ev
        if val <= 0:
            return
        if eng == "pe" and key == "e_pe":
            return
        if self.waited.get((eng, key), 0) >= val:
            return
        self.engs[eng].wait_ge(self.sems[key], val)
        self.waited[(eng, key)] = val

    def _deps(self, eng, r, w):
        for b in r:
            self.wait(eng, b.lw)
            if b.psum:
                for k, v in b.rd.items():
                    if k != "e_" + eng:
                        self.wait(eng, (k, v))
        for b in w:
            self.wait(eng, b.lw)
            for k, v in b.rd.items():
                self.wait(eng, (k, v))

    def _record(self, ev, r, w):
        for b in r:
            if b.rd.get(ev[0], 0) < ev[1]:
                b.rd[ev[0]] = ev[1]
        for b in w:
            b.lw = ev
            b.rd = {}

    def op(self, eng, fn, r=(), w=(), sig=True):
        self._deps(eng, r, w)
        ins = fn(self.engs[eng])
        key = "e_" + eng
        if sig:
            self.cnt[key] += 1
            ins.then_inc(self.sems[key], 1)
            ev = (key, self.cnt[key])
        else:
            ev = (key, self.cnt[key] + 1)
        self._record(ev, r, w)
        return ins

    def dma(self, q, out, in_, r=(), w=(), **kw):
        self._deps(q, r, w)
        idx = self.drr[q] % self.ndq[q]
        self.drr[q] += 1
        key = f"d_{q}_{idx}"
        self.wait(q, (key, self.cur(key)))
        ins = self.engs[q].dma_start(out=out, in_=in_, **kw)
        ins.then_inc(self.sems[key], 16)
        self.cnt[key] += 1
        ev = (key, self.cur(key))
        self._record(ev, r, w)

    def barrier(self):
        for eng in self.engs:
            for key in self.sems:
                self.wait(eng, (key, self.cur(key)))

    def dbg(self, name, buf, ap, shape, dtype=F32):
        if not self.debug:
            return
        d = self.dout("dbg_" + name, shape, dtype)
        self.dma("sp", d, ap, r=[buf], w=[self.OUTb])

    def nextps(self):
        p = self.ps[self.psrr % 8]
        self.psrr += 1
        return p

    def sb(self, st, name, shape, dtype):
        self.uid = getattr(self, "uid", 0) + 1
        name = f"{name}_{self.uid}"
        t = st.enter_context(self.nc.sbuf_tensor(name, list(shape), dtype))
        return Buf(name, t)

    def din(self, name, shape, dtype=F32):
        h = self.nc.dram_tensor(name, list(shape), dtype, kind="ExternalInput")
        self.dram_in[name] = h
        return h.ap()

    def dout(self, name, shape, dtype=F32):
        h = self.nc.dram_tensor(name, list(shape), dtype, kind="ExternalOutput")
        self.dram_out[name] = h
        return h.ap()

    def dscr(self, name, shape, dtype):
        if self.debug:
            return self.dout(name, shape, dtype)
        return self.nc.dram_tensor(name, list(shape), dtype).ap()

    def load_fm(self, st_name, src2d, R, dst_ap, dstbuf, srcbuf):
        stg = self.fm_stage[self.fm_i % 2]
        self.fm_i += 1
        self.dma("sp", stg.t[0:R, :], src2d, r=[srcbuf], w=[stg])
        ps = self.nextps()
        self.op("pe", lambda e: e.transpose(out=ps.t[:, 0:R], in_=stg.t[0:R, :], identity=self.ident.t[0:R, 0:R]),
                r=[stg, self.ident], w=[ps])
        self.op("dve", lambda e: e.tensor_copy(out=dst_ap, in_=ps.t[:, 0:R]), r=[ps], w=[dstbuf])

    def build(self):
        nc = self.nc
        es = self.es
        self.x_all = self.din("x_all", [T, D])
        self.cond = self.din("cond", [2, D])
        self.cache_k = self.din("cache_k", [256, 2048])
        self.cache_v = self.din("cache_v", [256, 2048])
        self.st_re = self.din("st_re", [2, 128, 64])
        self.st_im = self.din("st_im", [2, 128, 64])
        self.w_mod = self.din("w_mod", [2, D, 6 * D])
        self.b_mod = self.din("b_mod", [2, 6 * D])
        self.norm_g = self.din("norm_g", [2, 2, D])
        self.w_qkv = self.din("w_qkv", [D, 3 * D])
        self.lam_vecs = self.din("lam_vecs", [4, 128])
        self.subln_g = self.din("subln_g", [256])
        self.w_o = self.din("w_o", [D, D])
        self.ssm_a_re = self.din("ssm_a_re", [2, 128, 64])
        self.ssm_a_im = self.din("ssm_a_im", [2, 128, 64])
        self.ssm_log_step = self.din("ssm_log_step", [2, 128])
        self.ssm_b_re = self.din("ssm_b_re", [2, 128, 64, 16])
        self.ssm_b_im = self.din("ssm_b_im", [2, 128, 64, 16])
        self.ssm_c_re = self.din("ssm_c_re", [2, 128, 16, 64])
        self.ssm_c_im = self.din("ssm_c_im", [2, 128, 16, 64])
        self.ssm_d = self.din("ssm_d", [D])
        self.w_glu = self.din("w_glu", [D, D])
        self.b_glu = self.din("b_glu", [D])
        self.w_up = self.din("w_up", [2, D, 2 * DFF])
        self.conv_w = self.din("conv_w", [2, 3, 2 * DFF])
        self.conv_b = self.din("conv_b", [2, 2 * DFF])
        self.w_down = self.din("w_down", [2, DFF, D])
        self.final_g = self.din("final_g", [D])
        self.c_ident = self.din("c_ident", [128, 128])
        self.c_ropeT = self.din("c_ropeT", [128, 128])
        self.c_cos = self.din("c_cos", [128, 2048])
        self.c_sin = self.din("c_sin", [128, 2048])
        self.c_sel = self.din("c_sel", [128, 8 * 240])
        self.c_maskf = self.din("c_maskf", [128, 128])
        self.c_maskb = self.din("c_maskb", [128, 128])

        self.y_all = self.dout("y_all", [T, D])
        self.new_k = self.dout("new_k", [2, 256, 2048])
        self.new_v = self.dout("new_v", [2, 256, 2048])
        self.new_sre = self.dout("new_sre", [256, 128])
        self.new_sim = self.dout("new_sim", [256, 128])

        self.XT = self.dscr("XT", [KC, 128, T], F32)
        self.OT = self.dscr("OT", [KC, 128, T], BF16)
        self.AT = self.dscr("AT", [JC, 128, T], BF16)
        self.YT = self.dscr("YT", [KC, 128, T], F32)
        self.INb = Buf("inputs")
        self.XTb = [Buf(f"XT{i}") for i in range(NT)]
        self.OTb = [Buf(f"OT{i}") for i in range(NT)]
        self.ATb = [Buf(f"AT{i}") for i in range(NT)]
        self.YTb = [Buf(f"YT{i}") for i in range(NT)]
        self.OUTb = Buf("outputs")

        self.ident = self.sb(es, "ident", [128, 128], F32)
        self.onesb = self.sb(es, "onesb", [128, 128], BF16)
        self.onesf = self.sb(es, "onesf", [128, 128], F32)
        self.cst = self.sb(es, "cst", [128, 8], F32)
        self.MOD = self.sb(es, "MOD", [128, 2 * 96 * 2], F32)
        self.AV = self.sb(es, "AV", [128, 2 * 2 * 16 * 2], F32)
        self.AFN = self.sb(es, "AFN", [128, 16], F32)
        self.RSTD = None
        self.RSD = self.dscr("RSD", [128, T], F32)
        self.RSDb = Buf("RSD")
        self.HT = None
        self.HTb = [Buf(f"HT{i}") for i in range(NT)]
        self.ht_scope = None
        self.fm_stage = [self.sb(es, f"fmst{i}", [128, 128], F32) for i in range(2)]
        self.fm_i = 0

        self.dma("sp", self.ident.t[:], self.c_ident, r=[self.INb], w=[self.ident])
        self.op("dve", lambda e: e.memset(self.onesb.t[:], 1.0), w=[self.onesb])
        self.op("dve", lambda e: e.memset(self.onesf.t[:], 1.0), w=[self.onesf])
        self.op("dve", lambda e: e.memset(self.cst.t[:, 0:1], NORM_EPS), w=[self.cst])
        self.op("dve", lambda e: e.memset(self.cst.t[:, 1:2], SUBLN_EPS), w=[self.cst])
        self.op("dve", lambda e: e.memset(self.cst.t[:, 2:3], 0.0), w=[self.cst])
        self.op("dve", lambda e: e.memset(self.cst.t[:, 3:4], 1.0), w=[self.cst])

        phases = [
            ("mod", self.phase_mod),
            ("loadx", lambda: (self.ht_open(), self.phase_loadx())),
            ("attn", lambda: (self.phase_attn(), self.ht_close())),
            ("wo", lambda: self.phase_proj(self.OT, self.OTb, KC, self.w_o, 0, 0)),
            ("norm02", lambda: (self.ht_open(), self.phase_norm(0, 1))),
            ("up0", lambda: (self.phase_up(0), self.ht_close())),
            ("down0", lambda: self.phase_proj(self.AT, self.ATb, JC, self.w_down[0], 0, 1)),
            ("norm11", lambda: (self.ht_open(), self.phase_norm(1, 0))),
            ("s5", lambda: (self.phase_s5(), self.ht_close())),
            ("glu", self.phase_glu),
            ("norm12", lambda: (self.ht_open(), self.phase_norm(1, 1))),
            ("up1", lambda: (self.phase_up(1), self.ht_close())),
            ("down1", lambda: self.phase_proj(self.AT, self.ATb, JC, self.w_down[1], 1, 1)),
            ("final", self.phase_final),
        ]
        for name, fn in phases:
            if name in os.environ.get('MK_SKIP', '').split(','):
                continue
            fn()
            self.barrier()
            if self.stop_after == name:
                if self.ht_scope is not None:
                    self.ht_close()
                break
        self.barrier()
        return nc

    def ht_open(self):
        self.ht_scope = ExitStack()
        self.HT = self.sb(self.ht_scope, "HT", [128, KC * T], BF16)
        self.HTb = [Buf(f"HT{i}") for i in range(NT)]

    def ht_close(self):
        self.barrier()
        self.ht_scope.close()
        self.ht_scope = None
        self.HT = None

    def modv(self, l, j, kc, n):
        c = ((l * 96 + j * 16 + kc) * 2 + n)
        return self.MOD.t[:, c:c + 1]

    def av(self, l, s, kc, n):
        c = (((l * 2 + s) * 16 + kc) * 2 + n)
        return self.AV.t[:, c:c + 1]

    def phase_mod(self):
        with ExitStack() as ph:
            condT = self.sb(ph, "condT", [128, 32], F32)
            scT = self.sb(ph, "scT", [128, 32], BF16)
            bmT = self.sb(ph, "bmT", [128, 192], F32)
            gT = self.sb(ph, "gT", [128, 64], F32)
            wblk = [self.sb(ph, f"wmod{i}", [128, 16 * 512], BF16) for i in range(2)]
            self.load_fm("cond", self.cond.rearrange("n (k p) -> (n k) p", p=128), 32, condT.t[:, :], condT, self.INb)
            self.op("act", lambda e: e.activation(out=scT.t[:, :], in_=condT.t[:, :], func=AF.Silu), r=[condT], w=[scT])
            for l in range(2):
                self.load_fm("bm", self.b_mod[l].rearrange("(m p) -> m p", p=128), 96, bmT.t[:, l * 96:(l + 1) * 96], bmT, self.INb)
            self.load_fm("ng", self.norm_g.rearrange("l s (k p) -> (l s k) p", p=128), 64, gT.t[:, :], gT, self.INb)
            self.load_fm("fg", self.final_g.rearrange("(k p) -> k p", p=128), 16, self.AFN.t[:, :], self.AFN, self.INb)
            i = 0
            for l in range(2):
                for blk in range(24):
                    wb = wblk[i % 2]
                    i += 1
                    self.dma("pool", wb.t[:, :].rearrange("p (k m) -> p k m", k=16),
                             self.w_mod[l][:, blk * 512:(blk + 1) * 512].rearrange("(k p) m -> p k m", p=128),
                             r=[self.INb], w=[wb])
                    ps = self.nextps()
                    for m4 in range(4):
                        for kc in range(16):
                            self.op("pe", lambda e, m4=m4, kc=kc: e.matmul(
                                ps.t[:, m4 * 2:(m4 + 1) * 2], lhsT=wb.t[:, kc * 512 + m4 * 128: kc * 512 + (m4 + 1) * 128],
                                rhs=scT.t[:, kc:32:16], start=(kc == 0), stop=(kc == 15)),
                                r=[wb, scT], w=[ps], sig=(kc == 15))
                    for n in range(2):
                        base = (l * 96 + blk * 4) * 2 + n
                        self.op("dve", lambda e, n=n, base=base: e.tensor_tensor(
                            out=self.MOD.t[:, base:base + 7:2], in0=ps.t[:, n:8:2],
                            in1=bmT.t[:, l * 96 + blk * 4: l * 96 + blk * 4 + 4], op=ALU.add),
                            r=[ps, bmT], w=[self.MOD])
            for l in range(2):
                for s in range(2):
                    for n in range(2):
                        j = 1 + 3 * s
                        mb = (l * 96 + j * 16) * 2 + n
                        ab = ((l * 2 + s) * 16) * 2 + n
                        self.op("dve", lambda e, mb=mb, ab=ab, l=l, s=s: e.scalar_tensor_tensor(
                            out=self.AV.t[:, ab:ab + 31:2], in0=self.MOD.t[:, mb:mb + 31:2], scalar=1.0,
                            in1=gT.t[:, (l * 2 + s) * 16:(l * 2 + s + 1) * 16], op0=ALU.add, op1=ALU.mult),
                            r=[self.MOD, gT], w=[self.AV])
            self.dbg("MOD", self.MOD, self.MOD.t[:, :], [128, 384])
            self.dbg("AV", self.AV, self.AV.t[:, :], [128, 128])
            self.barrier()

    def norm_tile(self, xt, tt, l, s, sq, rs, tmp, final_out=None):
        n = 0 if tt < 4 else 1
        tok = slice(tt * TW, (tt + 1) * TW)
        self.op("act", lambda e: e.activation(out=sq.t[:, :], in_=xt.t[:, :], func=AF.Square), r=[xt], w=[sq])
        ps = self.nextps()
        for kc in range(KC):
            self.op("pe", lambda e, kc=kc: e.matmul(ps.t[:, :], lhsT=self.onesb.t[:, :], rhs=sq.t[:, kc * TW:(kc + 1) * TW],
                                                  start=(kc == 0), stop=(kc == KC - 1)),
                    r=[sq, self.onesb], w=[ps], sig=(kc == KC - 1))
        self.op("act", lambda e: e.activation(out=rs.t[:, :], in_=ps.t[:, :], func=AF.Sqrt, scale=1.0 / D, bias=self.cst.t[:, 0:1]),
                r=[ps, self.cst], w=[rs])
        self.op("dve", lambda e: e.reciprocal(out=self.RSTD.t[:, tok], in_=rs.t[:, :]), r=[rs], w=[self.RSTD])
        for kc in range(KC):
            if final_out is None:
                tb = tmp[kc % 2]
                a = self.av(l, s, kc, n)
                b = self.modv(l, 3 * s, kc, n)
                self.op("dve", lambda e, kc=kc, tb=tb, a=a: e.scalar_tensor_tensor(
                    out=tb.t[:, :], in0=xt.t[:, kc * TW:(kc + 1) * TW], scalar=a, in1=self.RSTD.t[:, tok],
                    op0=ALU.mult, op1=ALU.mult), r=[xt, self.RSTD, self.AV], w=[tb])
                self.op("act", lambda e, kc=kc, tb=tb, b=b: e.activation(
                    out=self.HT.t[:, kc * T + tt * TW: kc * T + (tt + 1) * TW], in_=tb.t[:, :], func=AF.Identity, bias=b, scale=1.0),
                    r=[tb, self.MOD], w=[self.HTb[tt]])
            else:
                self.op("dve", lambda e, kc=kc: e.scalar_tensor_tensor(
                    out=final_out.t[:, kc * TW:(kc + 1) * TW], in0=xt.t[:, kc * TW:(kc + 1) * TW], scalar=self.AFN.t[:, kc:kc + 1],
                    in1=self.RSTD.t[:, tok], op0=ALU.mult, op1=ALU.mult), r=[xt, self.RSTD, self.AFN], w=[final_out])

    def xt_dram(self, tt, k0=0, k1=KC):
        return self.XT[k0:k1, :, tt * TW:(tt + 1) * TW].rearrange("k p t -> p k t")

    def phase_loadx(self):
        with ExitStack() as ph:
            xin = [self.sb(ph, f"xin{i}", [128, D], F32) for i in range(2)]
            xtile = [self.sb(ph, f"xtile{i}", [128, KC * TW], F32) for i in range(2)]
            self.RSTD = self.sb(ph, "RSTD", [128, T], F32)
            sq = self.sb(ph, "sq", [128, KC * TW], BF16)
            rs = self.sb(ph, "rs", [128, TW], F32)
            tmp = [self.sb(ph, f"ntmp{i}", [128, TW], F32) for i in range(2)]
            for tt in range(NT):
                xt = xtile[tt % 2]
                for b4 in range(4):
                    tb = tt * 4 + b4
                    xi = xin[tb % 2]
                    self.dma("sp", xi.t[:, :], self.x_all[tb * 128:(tb + 1) * 128, :], r=[self.INb], w=[xi])
                    for kq in range(4):
                        ps = self.nextps()
                        for k4 in range(4):
                            kc = kq * 4 + k4
                            self.op("pe", lambda e, kc=kc, k4=k4: e.transpose(
                                out=ps.t[:, k4 * 128:(k4 + 1) * 128], in_=xi.t[:, kc * 128:(kc + 1) * 128], identity=self.ident.t[:, :]),
                                r=[xi, self.ident], w=[ps], sig=(k4 == 3))
                        for k4 in range(4):
                            kc = kq * 4 + k4
                            eng = "act" if k4 % 2 == 0 else "dve"
                            if eng == "act":
                                self.op("act", lambda e, kc=kc, k4=k4: e.activation(
                                    out=xt.t[:, kc * TW + b4 * 128: kc * TW + (b4 + 1) * 128], in_=ps.t[:, k4 * 128:(k4 + 1) * 128],
                                    func=AF.Identity), r=[ps], w=[xt])
                            else:
                                self.op("dve", lambda e, kc=kc, k4=k4: e.tensor_copy(
                                    out=xt.t[:, kc * TW + b4 * 128: kc * TW + (b4 + 1) * 128], in_=ps.t[:, k4 * 128:(k4 + 1) * 128]),
                                    r=[ps], w=[xt])
                self.dma("sp", self.xt_dram(tt), xt.t[:, :].rearrange("p (k t) -> p k t", k=KC), r=[xt], w=[self.XTb[tt]])
                self.norm_tile(xt, tt, 0, 0, sq, rs, tmp)
            self.dbg("HT0", self.HTb[4], self.HT.t[:, 0:2 * T], [128, 2 * T], BF16)
            self.barrier()

    def phase_norm(self, l, s):
        with ExitStack() as ph:
            xtile = [self.sb(ph, f"xtile{i}", [128, KC * TW], F32) for i in range(2)]
            self.RSTD = self.sb(ph, "RSTD", [128, T], F32)
            sq = self.sb(ph, "sq", [128, KC * TW], BF16)
            rs = self.sb(ph, "rs", [128, TW], F32)
            tmp = [self.sb(ph, f"ntmp{i}", [128, TW], F32) for i in range(2)]
            for tt in range(NT):
                xt = xtile[tt % 2]
                self.dma("sp", xt.t[:, :].rearrange("p (k t) -> p k t", k=KC), self.xt_dram(tt), r=[self.XTb[tt]], w=[xt])
                self.norm_tile(xt, tt, l, s, sq, rs, tmp)
            if l == 1 and s == 0:
                self.dma("sp", self.RSD, self.RSTD.t[:, :], r=[self.RSTD], w=[self.RSDb])
            self.barrier()

    def phase_attn(self):
        HT = self.HT
        with ExitStack() as ph:
            Wp = [self.sb(ph, f"Wh{i}", [128, KC * 256], BF16) for i in range(3)]
            QT = self.sb(ph, "QT", [128, 2 * T], BF16)
            KT = self.sb(ph, "KT", [128, 2 * 2816], BF16)
            VS = 264
            Vx = self.sb(ph, "Vx", [128, 22 * VS], BF16)
            cosT = self.sb(ph, "cosT", [128, 2048], F32)
            sinT = self.sb(ph, "sinT", [128, 2048], F32)
            ropeT = self.sb(ph, "ropeT", [128, 128], BF16)
            xqb = [self.sb(ph, f"xqb{i}", [128, TW], BF16) for i in range(2)]
            lamb = self.sb(ph, "lamb", [128, 4], BF16)
            CKh = self.sb(ph, "CKh", [128, 2 * 256], F32)
            xq = [self.sb(ph, f"xq{i}", [128, TW], F32) for i in range(2)]
            t1 = [self.sb(ph, f"t1{i}", [128, TW], F32) for i in range(1)]
            t2 = [self.sb(ph, f"t2{i}", [128, TW], F32) for i in range(1)]
            PTb = [self.sb(ph, f"PT{i}", [128, TW], BF16) for i in range(4)]
            r12 = [self.sb(ph, f"r12{i}", [128, TW], F32) for i in range(2)]
            Dh = [self.sb(ph, f"Dh{i}", [128, TW], F32) for i in range(2)]
            o1 = self.sb(ph, "o1", [128, TW], F32)
            sqd = [self.sb(ph, f"sqd{i}", [128, TW], BF16) for i in range(2)]
            rsd = self.sb(ph, "rsd", [128, TW], F32)
            ost = [self.sb(ph, f"ost{i}", [128, TW], BF16) for i in range(2)]
            kvst = [self.sb(ph, f"kvst{i}", [128, 256], F32) for i in range(2)]
            lamt = self.sb(ph, "lamt", [128, 8], F32)
            gsub = self.sb(ph, "gsub", [128, 2], F32)

            self.dma("sp", cosT.t[:, :], self.c_cos, r=[self.INb], w=[cosT])
            self.dma("sp", sinT.t[:, :], self.c_sin, r=[self.INb], w=[sinT])
            self.dma("pool", ropeT.t[:, :], self.c_ropeT, r=[self.INb], w=[ropeT])
            for blk in range(22):
                self.op("dve", lambda e, blk=blk: e.memset(Vx.t[:, blk * VS + 256: blk * VS + 257], 1.0), w=[Vx])
            stage = int(os.environ.get('ATT_STAGE', 9))
            if stage < 1:
                return
            self.load_fm("lv", self.lam_vecs, 4, lamt.t[:, 0:4], lamt, self.INb)
            self.op("dve", lambda e: e.tensor_tensor(out=lamt.t[:, 4:6], in0=lamt.t[:, 0:4:2], in1=lamt.t[:, 1:4:2], op=ALU.mult),
                    r=[lamt], w=[lamt])
            self.op("dve", lambda e: e.tensor_copy(out=lamb.t[:, 0:2], in_=lamt.t[:, 4:6]), r=[lamt], w=[lamb])
            self.op("dve", lambda e: e.tensor_tensor(out=lamb.t[:, 2:4], in0=lamt.t[:, 4:6], in1=lamb.t[:, 0:2], op=ALU.subtract), r=[lamt, lamb], w=[lamb])
            psl = self.nextps()
            self.op("pe", lambda e: e.matmul(psl.t[:, 0:4], lhsT=self.onesb.t[:, :], rhs=lamb.t[:, 0:4], start=True, stop=True),
                    r=[lamb, self.onesb], w=[psl])
            self.op("dve", lambda e: e.tensor_copy(out=lamt.t[:, 0:4], in_=psl.t[:, 0:4]), r=[psl], w=[lamt])
            self.op("dve", lambda e: e.tensor_tensor(out=lamt.t[:, 4:6], in0=lamt.t[:, 0:2], in1=lamt.t[:, 2:4], op=ALU.add), r=[lamt], w=[lamt])
            self.op("act", lambda e: e.activation(out=lamt.t[:, 6:8], in_=lamt.t[:, 4:6], func=AF.Exp), r=[lamt], w=[lamt])
            self.op("dve", lambda e: e.tensor_tensor(out=lamt.t[:, 4:5], in0=lamt.t[:, 6:7], in1=lamt.t[:, 7:8], op=ALU.subtract),
                    r=[lamt], w=[lamt])
            self.op("dve", lambda e: e.tensor_scalar(out=lamt.t[:, 5:6], in0=lamt.t[:, 4:5], scalar1=LAM_INIT0, scalar2=1.0,
                                                     op0=ALU.add, op1=ALU.mult), r=[lamt], w=[lamt])
            LAM = lamt.t[:, 5:6]
            if stage < 2:
                return
            self.load_fm("sg", self.subln_g.rearrange("(k p) -> k p", p=128), 2, gsub.t[:, 0:2], gsub, self.INb)
            self.op("dve", lambda e: e.tensor_scalar(out=gsub.t[:, 0:2], in0=gsub.t[:, 0:2], scalar1=(1.0 - LAM_INIT0), scalar2=1.0,
                                                     op0=ALU.mult, op1=ALU.mult), r=[gsub], w=[gsub])
            if stage < 3:
                return
            kvi = 0
            osti = 0
            pti = 0
            skip = os.environ.get('ATT_SKIP', '').split(',')
            def load_head_w(hh):
                for part in range(3):
                    self.dma("pool", Wp[part].t[:, :].rearrange("p (k m) -> p k m", k=KC),
                             self.w_qkv[:, part * 2048 + hh * 256: part * 2048 + (hh + 1) * 256].rearrange("(k p) m -> p k m", p=128),
                             r=[self.INb], w=[Wp[part]])

            NHEADS = int(os.environ.get('ATT_HEADS', NH))
            load_head_w(0)
            for h in range(NHEADS):
                for b in range(2):
                    self.dma("sp", CKh.t[:, b * 256:(b + 1) * 256], self.cache_k[b * 128:(b + 1) * 128, h * 256:(h + 1) * 256],
                             r=[self.INb], w=[CKh])
                for b in range(2):
                    self.dma("pool", Vx.t[:, (16 + b) * VS: (16 + b) * VS + 256],
                             self.cache_v[b * 128:(b + 1) * 128, h * 256:(h + 1) * 256], r=[self.INb], w=[Vx])
                ri = 0
                pending = []
                for which in range(0 if 'qk' in skip else 2):
                    for c in range(2):
                        for tt in [int(x) for x in os.environ.get('ATT_TT', '0,1,2,3,4').split(',')]:
                            ps = self.nextps()
                            for kc in range(KC):
                                self.op("pe", lambda e, kc=kc: e.matmul(
                                    ps.t[:, :], lhsT=Wp[which].t[:, kc * 256 + c * 128: kc * 256 + (c + 1) * 128],
                                    rhs=HT.t[:, kc * T + tt * TW: kc * T + (tt + 1) * TW], start=(kc == 0), stop=(kc == KC - 1)),
                                    r=[Wp[which], self.HTb[tt]], w=[ps], sig=(kc == KC - 1))
                            while pending:
                                pending.pop(0)()
                            if which == 0:
                                dst = QT.t[:, c * T + tt * TW: c * T + (tt + 1) * TW]
                                dstb = QT
                            else:
                                off = tt * TW if tt < 4 else 2304
                                dst = KT.t[:, c * 2816 + off: c * 2816 + off + TW]
                                dstb = KT
                            if tt < 4 and os.environ.get('ATT_NOROPE', '0') == '0':
                                xb = xq[ri % 2]
                                a1 = t1[0]
                                a2 = t2[0]
                                ri += 1
                                self.op("act", lambda e, xb=xb: e.activation(out=xb.t[:, :], in_=ps.t[:, :], func=AF.Identity), r=[ps], w=[xb])
                                xbb = xqb[ri % 2]
                                self.op("dve", lambda e, xbb=xbb: e.tensor_copy(out=xbb.t[:, :], in_=ps.t[:, :]), r=[ps], w=[xbb])
                                def rope_tail(xb=xb, xbb=xbb, a1=a1, a2=a2, dst=dst, dstb=dstb, tt=tt):
                                    ps2 = self.nextps()
                                    self.op("pe", lambda e: e.matmul(ps2.t[:, :], lhsT=ropeT.t[:, :], rhs=xbb.t[:, :], start=True, stop=True),
                                            r=[xbb, ropeT], w=[ps2])
                                    self.op("dve", lambda e: e.tensor_tensor(out=a1.t[:, :], in0=xb.t[:, :], in1=cosT.t[:, tt * TW:(tt + 1) * TW], op=ALU.mult),
                                            r=[xb, cosT], w=[a1])
                                    self.op("dve", lambda e: e.tensor_tensor(out=a2.t[:, :], in0=ps2.t[:, :], in1=sinT.t[:, tt * TW:(tt + 1) * TW], op=ALU.mult),
                                            r=[ps2, sinT], w=[a2])
                                    self.op("pool", lambda e: e.tensor_tensor(out=dst, in0=a1.t[:, :], in1=a2.t[:, :], op=ALU.add),
                                            r=[a1, a2], w=[dstb])
                                pending.append(rope_tail)
                            else:
                                self.op("act", lambda e, dst=dst: e.activation(out=dst, in_=ps.t[:, :], func=AF.Identity), r=[ps], w=[dstb])
                while pending:
                    pending.pop(0)()
                for c in range(0 if 'ck' in skip else 2):
                    ps = self.nextps()
                    for b in range(2):
                        self.op("pe", lambda e, b=b, c=c: e.transpose(out=ps.t[:, b * 128:(b + 1) * 128],
                                                                      in_=CKh.t[:, b * 256 + c * 128: b * 256 + (c + 1) * 128], identity=self.ident.t[:, :]),
                                r=[CKh, self.ident], w=[ps], sig=(b == 1))
                    self.op("act", lambda e, c=c, ps=ps: e.activation(out=KT.t[:, c * 2816 + 2048: c * 2816 + 2304], in_=ps.t[:, 0:256], func=AF.Identity),
                            r=[ps], w=[KT])
                for tb in range(0 if 'v' in skip else 20):
                    blk = tb if tb < 16 else tb + 2
                    ps = self.nextps()
                    for kc in range(KC):
                        self.op("pe", lambda e, kc=kc: e.matmul(
                            ps.t[:, 0:256], lhsT=HT.t[:, kc * T + tb * 128: kc * T + (tb + 1) * 128],
                            rhs=Wp[2].t[:, kc * 256: kc * 256 + 256], start=(kc == 0), stop=(kc == KC - 1)),
                            r=[Wp[2], self.HTb[tb // 4]], w=[ps], sig=(kc == KC - 1))
                    self.op("act", lambda e, blk=blk, ps=ps: e.activation(out=Vx.t[:, blk * VS: blk * VS + 256], in_=ps.t[:, 0:256], func=AF.Identity),
                            r=[ps], w=[Vx])
                    if tb >= 16:
                        seq = (tb - 16) // 2
                        row0 = ((tb - 16) % 2) * 128
                        st = kvst[kvi % 2]
                        kvi += 1
                        self.op("dve", lambda e, st=st, ps=ps: e.tensor_copy(out=st.t[:, :], in_=ps.t[:, 0:256]), r=[ps], w=[st])
                        self.dma("sp", self.new_v[seq, row0:row0 + 128, h * 256:(h + 1) * 256], st.t[:, :], r=[st], w=[self.OUTb])
                        ps = self.nextps()
                        for kc in range(KC):
                            self.op("pe", lambda e, kc=kc: e.matmul(
                                ps.t[:, 0:256], lhsT=HT.t[:, kc * T + tb * 128: kc * T + (tb + 1) * 128],
                                rhs=Wp[1].t[:, kc * 256: kc * 256 + 256], start=(kc == 0), stop=(kc == KC - 1)),
                                r=[Wp[1], self.HTb[4]], w=[ps], sig=(kc == KC - 1))
                        st = kvst[kvi % 2]
                        kvi += 1
                        self.op("dve", lambda e, st=st, ps=ps: e.tensor_copy(out=st.t[:, :], in_=ps.t[:, 0:256]), r=[ps], w=[st])
                        self.dma("sp", self.new_k[seq, row0:row0 + 128, h * 256:(h + 1) * 256], st.t[:, :], r=[st], w=[self.OUTb])
                if h + 1 < NHEADS:
                    load_head_w(h + 1)
                jobs = [(qt * TW, TW, list(range(18)), qt) for qt in range(4)]
                jobs.append((2048, 256, [18, 19], 4))
                jobs.append((2304, 256, [20, 21], 4))
                if os.environ.get('ATT_CORE', '1') == '0':
                    jobs = []
                acc_o = [[self.ps[0], self.ps[1]], [self.ps[2], self.ps[3]]]
                acc_s = [self.ps[4], self.ps[5]]
                sps = [self.ps[6], self.ps[7]]
                si = 0
                ep_pending = []
                for (q0, N, blks, tt) in jobs:
                    steps = [(c, bi, blk) for c in range(2) for bi, blk in enumerate(blks)]
                    Sof = {}

                    def emit_S(k):
                        nonlocal si
                        c, bi, blk = steps[k]
                        if blk < 16:
                            koff = blk * 128
                        elif blk < 18:
                            koff = 2048 + (blk - 16) * 128
                        else:
                            koff = 2304 + (blk - 18) * 128
                        S = sps[si % 2]
                        si += 1
                        Sof[k] = S
                        self.op("pe", lambda e: e.matmul(
                            S.t[:, 0:N], lhsT=KT.t[:, c * 2816 + koff: c * 2816 + koff + 128],
                            rhs=QT.t[:, c * T + q0: c * T + q0 + N], start=True, stop=True), r=[KT, QT], w=[S])

                    emit_S(0)
                    if len(steps) > 1:
                        emit_S(1)
                    for k, (c, bi, blk) in enumerate(steps):
                        S = Sof[k]
                        P = PTb[pti % 4]
                        pti += 1
                        self.op("act", lambda e: e.activation(out=P.t[:, 0:N], in_=S.t[:, 0:N], func=AF.Exp, scale=ATT_SCALE),
                                r=[S], w=[P])
                        first = (bi == 0)
                        last = (bi == len(blks) - 1)
                        for half in range(2):
                            self.op("pe", lambda e, half=half: e.matmul(
                                acc_o[c][half].t[:, 0:N], lhsT=Vx.t[:, blk * VS + half * 128: blk * VS + (half + 1) * 128],
                                rhs=P.t[:, 0:N], start=first, stop=last), r=[Vx, P], w=[acc_o[c][half]], sig=False)
                        self.op("pe", lambda e: e.matmul(acc_s[c].t[:, 0:N], lhsT=self.onesb.t[:, :], rhs=P.t[:, 0:N],
                                                         start=first, stop=last), r=[self.onesb, P], w=[acc_s[c]] + acc_o[c])
                        if k == 1 and ep_pending:
                            ep_pending.pop(0)(Sof[k])
                        if k + 2 < len(steps):
                            emit_S(k + 2)
                    self.op("dve", lambda e: e.reciprocal(out=r12[0].t[:, 0:N], in_=acc_s[0].t[:, 0:N]), r=[acc_s[0]], w=[r12[0]])
                    self.op("dve", lambda e: e.reciprocal(out=r12[1].t[:, 0:N], in_=acc_s[1].t[:, 0:N]), r=[acc_s[1]], w=[r12[1]])
                    self.op("dve", lambda e: e.tensor_scalar(out=r12[1].t[:, 0:N], in0=r12[1].t[:, 0:N], scalar1=LAM, scalar2=1.0,
                                                             op0=ALU.mult, op1=ALU.mult), r=[r12[1], lamt], w=[r12[1]])
                    for half in range(2):
                        self.op("dve", lambda e, half=half: e.tensor_tensor(out=o1.t[:, 0:N], in0=acc_o[0][half].t[:, 0:N], in1=r12[0].t[:, 0:N], op=ALU.mult),
                                r=[acc_o[0][half], r12[0]], w=[o1])
                        self.op("dve", lambda e, half=half: e.tensor_tensor(out=Dh[half].t[:, 0:N], in0=acc_o[1][half].t[:, 0:N], in1=r12[1].t[:, 0:N], op=ALU.mult),
                                r=[acc_o[1][half], r12[1]], w=[Dh[half]])
                        self.op("pool", lambda e, half=half: e.tensor_tensor(out=Dh[half].t[:, 0:N], in0=o1.t[:, 0:N], in1=Dh[half].t[:, 0:N], op=ALU.subtract),
                                r=[o1, Dh[half]], w=[Dh[half]])
                        self.op("act", lambda e, half=half: e.activation(out=sqd[half].t[:, 0:N], in_=Dh[half].t[:, 0:N], func=AF.Square),
                                r=[Dh[half]], w=[sqd[half]])
                    def ep_tail(S, N=N, q0=q0, tt=tt, h=h):
                        nonlocal osti
                        for half in range(2):
                            self.op("pe", lambda e, half=half: e.matmul(S.t[:, 0:N], lhsT=self.onesb.t[:, :], rhs=sqd[half].t[:, 0:N],
                                                                        start=(half == 0), stop=(half == 1)), r=[sqd[half], self.onesb], w=[S], sig=(half == 1))
                        self.op("act", lambda e: e.activation(out=rsd.t[:, 0:N], in_=S.t[:, 0:N], func=AF.Sqrt, scale=1.0 / 256, bias=self.cst.t[:, 1:2]),
                                r=[S, self.cst], w=[rsd])
                        self.op("dve", lambda e: e.reciprocal(out=rsd.t[:, 0:N], in_=rsd.t[:, 0:N]), r=[rsd], w=[rsd])
                        for half in range(2):
                            ob = ost[osti % 2]
                            osti += 1
                            self.op("dve", lambda e, half=half, ob=ob: e.scalar_tensor_tensor(
                                out=ob.t[:, 0:N], in0=Dh[half].t[:, 0:N], scalar=gsub.t[:, half:half + 1], in1=rsd.t[:, 0:N],
                                op0=ALU.mult, op1=ALU.mult), r=[Dh[half], gsub, rsd], w=[ob])
                            self.dma("sp", self.OT[h * 2 + half, :, q0:q0 + N], ob.t[:, 0:N], r=[ob], w=[self.OTb[tt]])
                    ep_pending.append(ep_tail)
                while ep_pending:
                    ep_pending.pop(0)(sps[si % 2])
                    si += 1
            self.barrier()

    def phase_proj(self, IN, INb, KCin, W, l, s):
        gj = 2 + 3 * s
        with ExitStack() as ph:
            intile = [self.sb(ph, f"pin{i}", [128, KCin * TW], BF16) for i in range(2)]
            wblk = [self.sb(ph, f"pw{i}", [128, KCin * 512], BF16) for i in range(2)]
            xpc = [self.sb(ph, f"pxp{i}", [128, 4 * TW], F32) for i in range(2)]
            wi = 0
            xi = 0
            for tt in range(NT):
                n = 0 if tt < 4 else 1
                it = intile[tt % 2]
                self.dma("sp", it.t[:, :].rearrange("p (k t) -> p k t", k=KCin),
                         IN[:, :, tt * TW:(tt + 1) * TW].rearrange("k p t -> p k t"), r=[INb[tt]], w=[it])
                for mcb in range(4):
                    wb = wblk[wi % 2]
                    wi += 1
                    self.dma("pool", wb.t[:, :].rearrange("p (k m) -> p k m", k=KCin),
                             W[:, mcb * 512:(mcb + 1) * 512].rearrange("(k p) m -> p k m", p=128), r=[self.INb], w=[wb])
                    xp = xpc[xi % 2]
                    xi += 1
                    self.dma("sp", xp.t[:, :].rearrange("p (k t) -> p k t", k=4), self.xt_dram(tt, mcb * 4, mcb * 4 + 4),
                             r=[self.XTb[tt]], w=[xp])
                    for m2 in range(4):
                        mc = mcb * 4 + m2
                        ps = self.nextps()
                        for kc in range(KCin):
                            self.op("pe", lambda e, kc=kc, m2=m2: e.matmul(
                                ps.t[:, :], lhsT=wb.t[:, kc * 512 + m2 * 128: kc * 512 + (m2 + 1) * 128],
                                rhs=it.t[:, kc * TW:(kc + 1) * TW], start=(kc == 0), stop=(kc == KCin - 1)),
                                r=[wb, it], w=[ps], sig=(kc == KCin - 1))
                        g = self.modv(l, gj, mc, n)
                        self.op("dve", lambda e, m2=m2, g=g, ps=ps: e.scalar_tensor_tensor(
                            out=xp.t[:, m2 * TW:(m2 + 1) * TW], in0=ps.t[:, :], scalar=g, in1=xp.t[:, m2 * TW:(m2 + 1) * TW],
                            op0=ALU.mult, op1=ALU.add), r=[ps, xp, self.MOD], w=[xp])
                    self.dma("sp", self.xt_dram(tt, mcb * 4, mcb * 4 + 4), xp.t[:, :].rearrange("p (k t) -> p k t", k=4),
                             r=[xp], w=[self.XTb[tt]])
            self.barrier()

    def phase_up(self, l):
        HT = self.HT
        with ExitStack() as ph:
            wg = [self.sb(ph, f"wg{i}", [128, KC * 512], BF16) for i in range(2)]
            wv = [self.sb(ph, f"wv{i}", [128, KC * 512], BF16) for i in range(2)]
            UGs = [self.sb(ph, f"UG{i}", [128, CW_TOT], F32) for i in range(1)]
            UVs = [self.sb(ph, f"UV{i}", [128, CW_TOT], F32) for i in range(1)]
            CG = self.sb(ph, "CG", [128, CW_TOT], F32)
            CV = self.sb(ph, "CV", [128, CW_TOT], F32)
            AO = [self.sb(ph, f"AO{i}", [128, CW_TOT], BF16) for i in range(2)]
            CWt = self.sb(ph, "CWt", [128, 3 * 88], F32)
            CBt = self.sb(ph, "CBt", [128, 88], F32)
            for k in range(3):
                self.load_fm("cw", self.conv_w[l, k].rearrange("(m p) -> m p", p=128), 88, CWt.t[:, k * 88:(k + 1) * 88], CWt, self.INb)
            self.load_fm("cb", self.conv_b[l].rearrange("(m p) -> m p", p=128), 88, CBt.t[:, :], CBt, self.INb)
            for U_ in UGs + UVs:
                self.op("dve", lambda e, U_=U_: e.memset(U_.t[:, :], 0.0), w=[U_])
            segs = [(0, 2048, CPAD[0]), (2048, 256, CPAD[1]), (2304, 256, CPAD[2])]
            ai = 0
            for jb in range(11):
                g = wg[jb % 2]
                v = wv[jb % 2]
                self.dma("pool", g.t[:, :].rearrange("p (k m) -> p k m", k=KC),
                         self.w_up[l][:, jb * 512:(jb + 1) * 512].rearrange("(k p) m -> p k m", p=128), r=[self.INb], w=[g])
                self.dma("pool", v.t[:, :].rearrange("p (k m) -> p k m", k=KC),
                         self.w_up[l][:, DFF + jb * 512: DFF + (jb + 1) * 512].rearrange("(k p) m -> p k m", p=128), r=[self.INb], w=[v])
                for j2 in range(4):
                    j = jb * 4 + j2
                    UG, UV = UGs[0], UVs[0]
                    for (wt, U, cidx) in ((g, UG, j), (v, UV, JC + j)):
                        for tt in range(NT):
                            ps = self.nextps()
                            for kc in range(KC):
                                self.op("pe", lambda e, kc=kc, wt=wt: e.matmul(
                                    ps.t[:, :], lhsT=wt.t[:, kc * 512 + j2 * 128: kc * 512 + (j2 + 1) * 128],
                                    rhs=HT.t[:, kc * T + tt * TW: kc * T + (tt + 1) * TW], start=(kc == 0), stop=(kc == KC - 1)),
                                    r=[wt, self.HTb[tt]], w=[ps], sig=(kc == KC - 1))
                            if tt < 4:
                                self.op("act", lambda e, U=U, ps=ps: e.activation(out=U.t[:, 1 + tt * TW: 1 + (tt + 1) * TW], in_=ps.t[:, :], func=AF.Identity),
                                        r=[ps], w=[U])
                            else:
                                self.op("act", lambda e, U=U, ps=ps: e.activation(out=U.t[:, CPAD[1]:CPAD[1] + 256], in_=ps.t[:, 0:256], func=AF.Identity),
                                        r=[ps], w=[U])
                                self.op("act", lambda e, U=U, ps=ps: e.activation(out=U.t[:, CPAD[2]:CPAD[2] + 256], in_=ps.t[:, 256:512], func=AF.Identity),
                                        r=[ps], w=[U])
                    L = CW_TOT - 2
                    for (U, C, cidx) in ((UG, CG, j), (UV, CV, JC + j)):
                        self.op("act", lambda e, U=U, C=C, cidx=cidx: e.activation(
                            out=C.t[:, 1:1 + L], in_=U.t[:, 1:1 + L], func=AF.Identity,
                            scale=CWt.t[:, 88 + cidx: 88 + cidx + 1], bias=CBt.t[:, cidx:cidx + 1]), r=[U, CWt, CBt], w=[C])
                        self.op("dve", lambda e, U=U, C=C, cidx=cidx: e.scalar_tensor_tensor(
                            out=C.t[:, 1:1 + L], in0=U.t[:, 0:L], scalar=CWt.t[:, cidx:cidx + 1], in1=C.t[:, 1:1 + L],
                            op0=ALU.mult, op1=ALU.add), r=[U, C, CWt], w=[C])
                        self.op("dve", lambda e, U=U, C=C, cidx=cidx: e.scalar_tensor_tensor(
                            out=C.t[:, 1:1 + L], in0=U.t[:, 2:2 + L], scalar=CWt.t[:, 176 + cidx: 176 + cidx + 1], in1=C.t[:, 1:1 + L],
                            op0=ALU.mult, op1=ALU.add), r=[U, C, CWt], w=[C])
                    self.op("act", lambda e: e.activation(out=CG.t[:, 1:1 + L], in_=CG.t[:, 1:1 + L], func=AF.Silu), r=[CG], w=[CG])
                    ao = AO[ai % 2]
                    ai += 1
                    self.op("dve", lambda e, ao=ao: e.tensor_tensor(out=ao.t[:, 1:1 + L], in0=CG.t[:, 1:1 + L], in1=CV.t[:, 1:1 + L], op=ALU.mult),
                            r=[CG, CV], w=[ao])
                    for (t0, ln, off) in segs:
                        self.dma("sp", self.AT[j, :, t0:t0 + ln], ao.t[:, off:off + ln], r=[ao], w=self.ATb)
            self.barrier()

    def phase_final(self):
        with ExitStack() as ph:
            xtile = [self.sb(ph, f"xtile{i}", [128, KC * TW], F32) for i in range(2)]
            yt = [self.sb(ph, f"yt{i}", [128, KC * TW], F32) for i in range(2)]
            self.RSTD = self.sb(ph, "RSTD", [128, T], F32)
            sq = self.sb(ph, "sq", [128, KC * TW], BF16)
            rs = self.sb(ph, "rs", [128, TW], F32)
            yo = [self.sb(ph, f"yo{i}", [128, D], F32) for i in range(2)]
            oi = 0
            for tt in range(NT):
                xt = xtile[tt % 2]
                y = yt[tt % 2]
                self.dma("sp", xt.t[:, :].rearrange("p (k t) -> p k t", k=KC), self.xt_dram(tt), r=[self.XTb[tt]], w=[xt])
                self.norm_tile(xt, tt, 0, 0, sq, rs, None, final_out=y)
                for b4 in range(4):
                    o = yo[oi % 2]
                    oi += 1
                    for kq in range(4):
                        ps = self.nextps()
                        for k4 in range(4):
                            kc = kq * 4 + k4
                            self.op("pe", lambda e, kc=kc, k4=k4: e.transpose(
                                out=ps.t[:, k4 * 128:(k4 + 1) * 128], in_=y.t[:, kc * TW + b4 * 128: kc * TW + (b4 + 1) * 128],
                                identity=self.ident.t[:, :]), r=[y, self.ident], w=[ps], sig=(k4 == 3))
                        if kq % 2 == 0:
                            self.op("act", lambda e, kq=kq, ps=ps: e.activation(out=o.t[:, kq * 512:(kq + 1) * 512], in_=ps.t[:, :], func=AF.Identity),
                                    r=[ps], w=[o])
                        else:
                            self.op("dve", lambda e, kq=kq, ps=ps: e.tensor_copy(out=o.t[:, kq * 512:(kq + 1) * 512], in_=ps.t[:, :]), r=[ps], w=[o])
                    tb = tt * 4 + b4
                    self.dma("sp", self.y_all[tb * 128:(tb + 1) * 128, :], o.t[:, :], r=[o], w=[self.OUTb])
            self.barrier()

    def bc(self, ap2d, n):
        a = ap2d.ap
        return bass.AP(ap2d.tensor, ap2d.offset, [list(a[0]), list(a[1]), [0, n]])

    def colbc(self, ap_col, n):
        a = ap_col.ap
        return bass.AP(ap_col.tensor, ap_col.offset, [list(a[0]), [0, n]])

    def rev(self, ap2d):
        a = ap2d.ap
        n = a[1][1]
        return bass.AP(ap2d.tensor, ap2d.offset + (n - 1) * a[1][0], [list(a[0]), [-a[1][0], n]])

    def phase_s5_tok(self):
        HT = self.HT
        TWO_PI = 2.0 * math.pi
        with ExitStack() as ph:
            def T128(name):
                return self.sb(ph, name, [128, 128], F32)
            are, aim, LS, dt, mag, ang, sinr, cosr = [T128(n) for n in ("are", "aim", "LS", "dt", "mag", "ang", "sinr", "cosr")]
            lbre, lbim, fre, fim, tA, tB, tC = [T128(n) for n in ("lbre", "lbim", "fre", "fim", "tA", "tB", "tC")]
            H0re, H0im = T128("H0re"), T128("H0im")
            ki = self.sb(ph, "ki", [128, 128], I32)
            lsr = self.sb(ph, "lsr", [128, 2], F32)
            lsx = self.sb(ph, "lsx", [128, 128], F32)
            NK = 10
            CWre = self.sb(ph, "CWre", [128, NK * 128], F32)
            CWim = self.sb(ph, "CWim", [128, NK * 128], F32)
            FSre = self.sb(ph, "FSre", [128, 256], F32)
            FSim = self.sb(ph, "FSim", [128, 256], F32)
            fso = self.sb(ph, "fso", [128, 128], F32)
            TBre = [self.sb(ph, f"TBre{i}", [128, 4 * 512], F32) for i in range(1)]
            TBim = [self.sb(ph, f"TBim{i}", [128, 4 * 512], F32) for i in range(1)]
            tt1 = self.sb(ph, "tt1", [128, 4 * 256], F32)
            tt2 = self.sb(ph, "tt2", [128, 4 * 256], F32)
            YK = self.sb(ph, "YK", [128, T], F32)
            SBre = self.sb(ph, "SBre", [128, 8 * 16], F32)
            SBim = self.sb(ph, "SBim", [128, 8 * 16], F32)
            BBre = self.sb(ph, "BBre", [128, 8 * 16], F32)
            BBim = self.sb(ph, "BBim", [128, 8 * 16], F32)
            SCre = self.sb(ph, "SCre", [32, 8 * 64], F32)
            SCim = self.sb(ph, "SCim", [32, 8 * 64], F32)
            SC2re = self.sb(ph, "SC2re", [32, 8 * 128], F32)
            SC2im = self.sb(ph, "SC2im", [32, 8 * 128], F32)
            STre = [self.sb(ph, f"STre{i}", [128, 128], F32) for i in range(4)]
            STim = [self.sb(ph, f"STim{i}", [128, 128], F32) for i in range(4)]
            WBre = [self.sb(ph, f"WBre{i}", [128, 128], BF16) for i in range(2)]
            WBim = [self.sb(ph, f"WBim{i}", [128, 128], BF16) for i in range(2)]
            WCre = [self.sb(ph, f"WCre{i}", [128, 128], BF16) for i in range(8)]
            WCin = [self.sb(ph, f"WCin{i}", [128, 128], BF16) for i in range(8)]
            m1, m2, m3, m4 = [self.sb(ph, f"m{i}", [128, TW], F32) for i in range(4)]
            zre, zim, hsre, hsim = [self.sb(ph, n, [128, TW], F32) for n in ("zre", "zim", "hsre", "hsim")]
            HBre = [self.sb(ph, f"HBre{i}", [128, TW], BF16) for i in range(2)]
            HBim = [self.sb(ph, f"HBim{i}", [128, TW], BF16) for i in range(2)]
            ini = self.sb(ph, "ini", [128, 8], F32)
            HPre = [self.sb(ph, f"HPre{i}", [128, TW], BF16) for i in range(2)]
            HPim = [self.sb(ph, f"HPim{i}", [128, TW], BF16) for i in range(2)]
            hpi = 0

            V = lambda b: b.t[:, :]
            dve = lambda fn, r, w: self.op("dve", fn, r=r, w=w)
            TT = lambda o, a, b, op, r, w, eng="dve": self.op(eng, lambda e: e.tensor_tensor(out=o, in0=a, in1=b, op=op), r=r, w=w)

            self.load_fm("are", self.ssm_a_re.rearrange("d (pr g2) p -> (d pr) (g2 p)", g2=2), 128, V(are), are, self.INb)
            self.load_fm("aim", self.ssm_a_im.rearrange("d (pr g2) p -> (d pr) (g2 p)", g2=2), 128, V(aim), aim, self.INb)
            self.load_fm("h0r", self.st_re.rearrange("d (pr g2) p -> (d pr) (g2 p)", g2=2), 128, V(H0re), H0re, self.INb)
            self.load_fm("h0i", self.st_im.rearrange("d (pr g2) p -> (d pr) (g2 p)", g2=2), 128, V(H0im), H0im, self.INb)
            self.dma("sp", lsr.t[:, :], self.ssm_log_step.rearrange("d (pr g2) -> (d pr) g2", g2=2), r=[self.INb], w=[lsr])
            for g2 in range(2):
                dve(lambda e, g2=g2: e.tensor_scalar(out=lsx.t[:, g2 * 64:(g2 + 1) * 64], in0=self.onesf.t[:, 0:64],
                                                     scalar1=lsr.t[:, g2:g2 + 1], scalar2=1.0, op0=ALU.mult, op1=ALU.mult),
                    [lsr, self.onesf], [lsx])
            ps = self.nextps()
            self.op("pe", lambda e: e.transpose(out=ps.t[:, 0:128], in_=lsx.t[:, :], identity=self.ident.t[:, :]), r=[lsx, self.ident], w=[ps])
            self.op("act", lambda e: e.activation(out=V(dt), in_=ps.t[:, 0:128], func=AF.Exp), r=[ps], w=[dt])
            TT(V(tA), V(are), V(dt), ALU.mult, [are, dt], [tA])
            self.op("act", lambda e: e.activation(out=V(mag), in_=V(tA), func=AF.Exp), r=[tA], w=[mag])
            TT(V(ang), V(aim), V(dt), ALU.mult, [aim, dt], [ang])
            dve(lambda e: e.tensor_scalar(out=V(tB), in0=V(ang), scalar1=1.0 / TWO_PI, scalar2=1.0, op0=ALU.mult, op1=ALU.mult), [ang], [tB])
            dve(lambda e: e.tensor_copy(out=ki.t[:, :], in_=V(tB)), [tB], [ki])
            dve(lambda e: e.tensor_copy(out=V(tB), in_=ki.t[:, :]), [ki], [tB])
            dve(lambda e: e.scalar_tensor_tensor(out=V(tC), in0=V(tB), scalar=-TWO_PI, in1=V(ang), op0=ALU.mult, op1=ALU.add), [tB, ang], [tC])
            dve(lambda e: e.tensor_scalar(out=V(tC), in0=V(tC), scalar1=3.141592, scalar2=-3.141592, op0=ALU.min, op1=ALU.max), [tC], [tC])
            self.op("act", lambda e: e.activation(out=V(sinr), in_=V(tC), func=AF.Sin), r=[tC], w=[sinr])
            self.op("act", lambda e: e.activation(out=V(tA), in_=V(tC), func=AF.Sin, scale=0.5), r=[tC], w=[tA])
            TT(V(tA), V(tA), V(tA), ALU.mult, [tA], [tA])
            dve(lambda e: e.tensor_scalar(out=V(cosr), in0=V(tA), scalar1=-2.0, scalar2=1.0, op0=ALU.mult, op1=ALU.add), [tA], [cosr])
            TT(V(lbre), V(mag), V(cosr), ALU.mult, [mag, cosr], [lbre])
            TT(V(lbim), V(mag), V(sinr), ALU.mult, [mag, sinr], [lbim])
            dve(lambda e: e.tensor_scalar(out=V(tA), in0=V(lbre), scalar1=-1.0, scalar2=1.0, op0=ALU.add, op1=ALU.mult), [lbre], [tA])
            TT(V(tB), V(are), V(are), ALU.mult, [are], [tB])
            TT(V(tC), V(aim), V(aim), ALU.mult, [aim], [tC])
            TT(V(tB), V(tB), V(tC), ALU.add, [tB, tC], [tB])
            dve(lambda e: e.reciprocal(out=V(tB), in_=V(tB)), [tB], [tB])
            TT(V(fre), V(tA), V(are), ALU.mult, [tA, are], [fre])
            TT(V(tC), V(lbim), V(aim), ALU.mult, [lbim, aim], [tC])
            TT(V(fre), V(fre), V(tC), ALU.add, [fre, tC], [fre])
            TT(V(fre), V(fre), V(tB), ALU.mult, [fre, tB], [fre])
            TT(V(fim), V(lbim), V(are), ALU.mult, [lbim, are], [fim])
            TT(V(tC), V(tA), V(aim), ALU.mult, [tA, aim], [tC])
            TT(V(fim), V(fim), V(tC), ALU.subtract, [fim, tC], [fim])
            TT(V(fim), V(fim), V(tB), ALU.mult, [fim, tB], [fim])
            dve(lambda e: e.tensor_copy(out=CWre.t[:, 0:128], in_=V(cosr)), [cosr], [CWre])
            dve(lambda e: e.tensor_scalar(out=CWim.t[:, 0:128], in0=V(sinr), scalar1=-1.0, scalar2=1.0, op0=ALU.mult, op1=ALU.mult), [sinr], [CWim])
            for k in range(NK - 1):
                a = CWre.t[:, k * 128:(k + 1) * 128]
                b = CWim.t[:, k * 128:(k + 1) * 128]
                TT(V(tA), a, a, ALU.mult, [CWre], [tA])
                TT(V(tB), b, b, ALU.mult, [CWim], [tB])
                TT(CWre.t[:, (k + 1) * 128:(k + 2) * 128], V(tA), V(tB), ALU.subtract, [tA, tB], [CWre])
                TT(V(tC), a, b, ALU.mult, [CWre, CWim], [tC])
                dve(lambda e, k=k: e.tensor_scalar(out=CWim.t[:, (k + 1) * 128:(k + 2) * 128], in0=V(tC), scalar1=2.0, scalar2=1.0,
                                                   op0=ALU.mult, op1=ALU.mult), [tC], [CWim])
            for st in STre + STim:
                dve(lambda e, st=st: e.memset(st.t[:, :], 0.0), [], [st])
            for wc in WCre + WCin:
                dve(lambda e, wc=wc: e.memset(wc.t[:, :], 0.0), [], [wc])

            seq_chunks = {0: [(0, 512), (512, 512), (1024, 512), (1536, 512)], 1: [(2048, 256)], 2: [(2304, 256)]}
            wbi = 0
            hbi = 0
            tbi = 0
            psy = [self.ps[i] for i in range(5)]
            psb = [self.ps[5], self.ps[6]]
            pst = self.ps[7]
            for kc in range(KC):
                for (SB_, src) in ((SBre, self.ssm_b_re), (SBim, self.ssm_b_im)):
                    for d in range(2):
                        self.dma("sp", SB_.t[:, d * 64:(d + 1) * 64].rearrange("p (c k) -> p c k", k=16),
                                 src[d, 8 * kc:8 * kc + 8].rearrange("(pq g2) p ci -> (g2 p) pq ci", g2=2), r=[self.INb], w=[SB_],
                                 allow_slow_non_contiguous=False)
                for (SC_, src) in ((SCre, self.ssm_c_re), (SCim, self.ssm_c_im)):
                    for d in range(2):
                        self.dma("sp", SC_.t[:, d * 256:(d + 1) * 256].rearrange("p (c k) -> p c k", k=64),
                                 src[d, 8 * kc:8 * kc + 8].rearrange("(pq g2) co p -> (g2 co) pq p", g2=2), r=[self.INb], w=[SC_])
                for (SC_, SC2_) in ((SCre, SC2re), (SCim, SC2im)):
                    for rep in range(2):
                        dve(lambda e, rep=rep, SC_=SC_, SC2_=SC2_: e.tensor_copy(
                            out=SC2_.t[:, :].rearrange("p (c r k) -> p c r k", r=2, k=64)[:, :, rep, :],
                            in_=SC_.t[:, :].rearrange("p (c k) -> p c k", k=64)), [SC_], [SC2_])
                for d in range(2):
                    c0 = d * 64 + 4 * kc
                    fr = self.bc(fre.t[:, c0:c0 + 4], 16)
                    fi = self.bc(fim.t[:, c0:c0 + 4], 16)
                    sl = slice(d * 64, (d + 1) * 64)
                    v3 = lambda b: b.t[:, sl].rearrange("p (c k) -> p c k", k=16)
                    t3 = tt1.t[:, 0:64].rearrange("p (c k) -> p c k", k=16)
                    TT(v3(BBre), v3(SBre), fr, ALU.mult, [SBre, fre], [BBre])
                    TT(t3, v3(SBim), fi, ALU.mult, [SBim, fim], [tt1])
                    TT(v3(BBre), v3(BBre), t3, ALU.subtract, [BBre, tt1], [BBre])
                    TT(v3(BBim), v3(SBre), fi, ALU.mult, [SBre, fim], [BBim])
                    TT(t3, v3(SBim), fr, ALU.mult, [SBim, fre], [tt1])
                    TT(v3(BBim), v3(BBim), t3, ALU.add, [BBim, tt1], [BBim])
                for d in range(2):
                    tre = TBre[0]
                    tim = TBim[0]
                    tbi += 1
                    c0 = d * 64 + 4 * kc
                    tre3 = tre.t[:, :].rearrange("p (c j) -> p c j", j=512)
                    tim3 = tim.t[:, :].rearrange("p (c j) -> p c j", j=512)
                    dve(lambda e: e.memset(tre3[:, :, 0:1], 1.0), [], [tre])
                    dve(lambda e: e.memset(tim3[:, :, 0:1], 0.0), [], [tim])
                    for k in range(9):
                        s = 1 << k
                        wr = self.bc(CWre.t[:, k * 128 + c0: k * 128 + c0 + 4], s)
                        wi = self.bc(CWim.t[:, k * 128 + c0: k * 128 + c0 + 4], s)
                        a = tre3[:, :, 0:s]
                        b = tim3[:, :, 0:s]
                        u1 = tt1.t[:, 0:4 * s].rearrange("p (c j) -> p c j", j=s)
                        u2 = tt2.t[:, 0:4 * s].rearrange("p (c j) -> p c j", j=s)
                        TT(u1, a, wr, ALU.mult, [tre, CWre], [tt1])
                        TT(u2, b, wi, ALU.mult, [tim, CWim], [tt2])
                        TT(tre3[:, :, s:2 * s], u1, u2, ALU.subtract, [tt1, tt2], [tre])
                        TT(u1, a, wi, ALU.mult, [tre, CWim], [tt1])
                        TT(u2, b, wr, ALU.mult, [tim, CWre], [tt2])
                        TT(tim3[:, :, s:2 * s], u1, u2, ALU.add, [tt1, tt2], [tim])
                    for pq in range(4):
                        col = c0 + pq
                        cc = d * 4 + pq
                        for (ST_, BB_, WB_) in ((STre[pq], BBre, WBre[wbi % 2]), (STim[pq], BBim, WBim[wbi % 2])):
                            for g2 in range(2):
                                dve(lambda e, g2=g2, ST_=ST_, BB_=BB_: e.tensor_copy(
                                    out=ST_.t[g2 * 64:(g2 + 1) * 64, 32 * pq + 16 * g2: 32 * pq + 16 * g2 + 16],
                                    in_=BB_.t[g2 * 64:(g2 + 1) * 64, cc * 16:(cc + 1) * 16]), [BB_], [ST_])
                            self.op("pe", lambda e, ST_=ST_: e.transpose(out=pst.t[:, 0:128], in_=ST_.t[:, :], identity=self.ident.t[:, :]),
                                    r=[ST_, self.ident], w=[pst])
                            self.op("act", lambda e, WB_=WB_: e.activation(out=WB_.t[:, :], in_=pst.t[:, 0:128], func=AF.Identity), r=[pst], w=[WB_])
                        wbre, wbim = WBre[wbi % 2], WBim[wbi % 2]
                        wbi += 1
                        wcre, wcin = WCre[cc], WCin[cc]
                        for (SC_, WC_, sgn) in ((SC2re, wcre, 1.0), (SC2im, wcin, -1.0)):
                            dup = SC_.t[:, cc * 128:(cc + 1) * 128]
                            self.op("pe", lambda e, dup=dup: e.transpose(out=pst.t[:, 0:32], in_=dup, identity=self.ident.t[0:32, 0:32]),
                                    r=[SC_, self.ident], w=[pst])
                            for g2 in range(2):
                                self.op("act", lambda e, g2=g2, WC_=WC_, sgn=sgn: e.activation(
                                    out=WC_.t[g2 * 64:(g2 + 1) * 64, 32 * pq + 16 * g2: 32 * pq + 16 * g2 + 16],
                                    in_=pst.t[g2 * 64:(g2 + 1) * 64, 16 * g2: 16 * g2 + 16], func=AF.Identity, scale=sgn), r=[pst], w=[WC_])
                        rho = mag.t[:, col:col + 1]
                        wre_c = cosr.t[:, col:col + 1]
                        wim_c = sinr.t[:, col:col + 1]
                        for sq_ in range(3):
                            chunks = seq_chunks[sq_] if d == 0 else list(reversed(seq_chunks[sq_]))
                            for ci_, (t0, N) in enumerate(chunks):
                                tile_i = t0 // TW
                                off = t0 - tile_i * TW
                                Tr = tre.t[:, pq * 512: pq * 512 + N]
                                Ti = tim.t[:, pq * 512: pq * 512 + N]
                                self.op("pe", lambda e: e.matmul(psb[0].t[:, 0:N], lhsT=wbre.t[:, :], rhs=HT.t[:, kc * T + t0: kc * T + t0 + N],
                                                                 start=True, stop=True), r=[wbre, self.HTb[tile_i]], w=[psb[0]])
                                self.op("pe", lambda e: e.matmul(psb[1].t[:, 0:N], lhsT=wbim.t[:, :], rhs=HT.t[:, kc * T + t0: kc * T + t0 + N],
                                                                 start=True, stop=True), r=[wbim, self.HTb[tile_i]], w=[psb[1]])
                                bre = psb[0].t[:, 0:N] if d == 0 else self.rev(psb[0].t[:, 0:N])
                                bim = psb[1].t[:, 0:N] if d == 0 else self.rev(psb[1].t[:, 0:N])
                                TT(m1.t[:, 0:N], bre, Tr, ALU.mult, [psb[0], tre], [m1])
                                TT(m4.t[:, 0:N], bre, Ti, ALU.mult, [psb[0], tim], [m4])
                                TT(m2.t[:, 0:N], bim, Ti, ALU.mult, [psb[1], tim], [m2])
                                TT(m3.t[:, 0:N], bim, Tr, ALU.mult, [psb[1], tre], [m3])
                                TT(zre.t[:, 0:N], m1.t[:, 0:N], m2.t[:, 0:N], ALU.subtract, [m1, m2], [zre], eng="pool")
                                TT(zim.t[:, 0:N], m3.t[:, 0:N], m4.t[:, 0:N], ALU.add, [m3, m4], [zim], eng="pool")
                                if ci_ == 0:
                                    if sq_ == 0:
                                        h0r = H0re.t[:, col:col + 1]
                                        h0i = H0im.t[:, col:col + 1]
                                        TT(ini.t[:, 2:3], h0r, wre_c, ALU.mult, [H0re, cosr], [ini])
                                        TT(ini.t[:, 3:4], h0i, wim_c, ALU.mult, [H0im, sinr], [ini])
                                        TT(ini.t[:, 0:1], ini.t[:, 2:3], ini.t[:, 3:4], ALU.subtract, [ini], [ini])
                                        TT(ini.t[:, 2:3], h0r, wim_c, ALU.mult, [H0re, sinr], [ini])
                                        TT(ini.t[:, 3:4], h0i, wre_c, ALU.mult, [H0im, cosr], [ini])
                                        TT(ini.t[:, 1:2], ini.t[:, 2:3], ini.t[:, 3:4], ALU.add, [ini], [ini])
                                    else:
                                        dve(lambda e: e.memset(ini.t[:, 0:2], 0.0), [], [ini])
                                else:
                                    c9r = CWre.t[:, 9 * 128 + col: 9 * 128 + col + 1]
                                    c9i = CWim.t[:, 9 * 128 + col: 9 * 128 + col + 1]
                                    lr = hsre.t[:, 511:512]
                                    li = hsim.t[:, 511:512]
                                    TT(ini.t[:, 2:3], lr, c9r, ALU.mult, [hsre, CWre], [ini])
                                    TT(ini.t[:, 3:4], li, c9i, ALU.mult, [hsim, CWim], [ini])
                                    TT(ini.t[:, 4:5], li, c9r, ALU.mult, [hsim, CWre], [ini])
                                    TT(ini.t[:, 5:6], lr, c9i, ALU.mult, [hsre, CWim], [ini])
                                    TT(ini.t[:, 0:1], ini.t[:, 2:3], ini.t[:, 3:4], ALU.add, [ini], [ini])
                                    TT(ini.t[:, 1:2], ini.t[:, 4:5], ini.t[:, 5:6], ALU.subtract, [ini], [ini])
                                dve(lambda e: e.tensor_tensor_scan(out=hsre.t[:, 0:N], data0=self.colbc(rho, N), data1=zre.t[:, 0:N],
                                                                   initial=ini.t[:, 0:1], op0=ALU.mult, op1=ALU.add), [mag, zre, ini], [hsre])
                                dve(lambda e: e.tensor_tensor_scan(out=hsim.t[:, 0:N], data0=self.colbc(rho, N), data1=zim.t[:, 0:N],
                                                                   initial=ini.t[:, 1:2], op0=ALU.mult, op1=ALU.add), [mag, zim, ini], [hsim])
                                if sq_ == 0:
                                    hbre, hbim = HBre[hbi % 2], HBim[hbi % 2]
                                    hbi += 1
                                    ho = 0
                                else:
                                    hbre, hbim = HPre[hpi % 2], HPim[hpi % 2]
                                    ho = (sq_ - 1) * 256
                                    if sq_ == 2:
                                        hpi += 1
                                TT(m1.t[:, 0:N], hsre.t[:, 0:N], Tr, ALU.mult, [hsre, tre], [m1])
                                TT(m2.t[:, 0:N], hsim.t[:, 0:N], Ti, ALU.mult, [hsim, tim], [m2])
                                TT(m3.t[:, 0:N], hsim.t[:, 0:N], Tr, ALU.mult, [hsim, tre], [m3])
                                TT(m4.t[:, 0:N], hsre.t[:, 0:N], Ti, ALU.mult, [hsre, tim], [m4])
                                ore = hbre.t[:, ho:ho + N] if d == 0 else self.rev(hbre.t[:, ho:ho + N])
                                oim = hbim.t[:, ho:ho + N] if d == 0 else self.rev(hbim.t[:, ho:ho + N])
                                TT(ore, m1.t[:, 0:N], m2.t[:, 0:N], ALU.add, [m1, m2], [hbre], eng="pool")
                                TT(oim, m3.t[:, 0:N], m4.t[:, 0:N], ALU.subtract, [m3, m4], [hbim], eng="pool")
                                if sq_ > 0:
                                    fcol = ((sq_ - 1) * 2 + d) * 64 + (4 * kc + pq)
                                    TT(FSre.t[:, fcol:fcol + 1], m1.t[:, N - 1:N], m2.t[:, N - 1:N], ALU.add, [m1, m2], [FSre])
                                    TT(FSim.t[:, fcol:fcol + 1], m3.t[:, N - 1:N], m4.t[:, N - 1:N], ALU.subtract, [m3, m4], [FSim])
                                if sq_ == 1:
                                    continue
                                NN = N if sq_ == 0 else 512
                                yv = psy[tile_i].t[:, 0:NN]
                                self.op("pe", lambda e: e.matmul(yv, lhsT=wcre.t[:, :], rhs=hbre.t[:, 0:NN], start=(pq == 0), stop=False),
                                        r=[wcre, hbre], w=[psy[tile_i]], sig=False)
                                self.op("pe", lambda e: e.matmul(yv, lhsT=wcin.t[:, :], rhs=hbim.t[:, 0:NN], start=False, stop=(pq == 3)),
                                        r=[wcin, hbim], w=[psy[tile_i]])
                    for ti in range(NT):
                        if d == 0:
                            self.op("act", lambda e, ti=ti: e.activation(out=YK.t[:, ti * TW:(ti + 1) * TW], in_=psy[ti].t[:, :], func=AF.Identity),
                                    r=[psy[ti]], w=[YK])
                        else:
                            TT(YK.t[:, ti * TW:(ti + 1) * TW], psy[ti].t[:, :], YK.t[:, ti * TW:(ti + 1) * TW], ALU.add, [psy[ti], YK], [YK])
                self.dma("sp", self.YT[kc], YK.t[:, :], r=[YK], w=self.YTb)
            for (FS, dst) in ((FSre, self.new_sre), (FSim, self.new_sim)):
                for hlf in range(2):
                    self.op("pe", lambda e, hlf=hlf, FS=FS: e.transpose(out=pst.t[:, 0:128], in_=FS.t[:, hlf * 128:(hlf + 1) * 128], identity=self.ident.t[:, :]),
                            r=[FS, self.ident], w=[pst])
                    self.op("act", lambda e: e.activation(out=fso.t[:, :], in_=pst.t[:, 0:128], func=AF.Identity), r=[pst], w=[fso])
                    self.dma("sp", dst[hlf * 128:(hlf + 1) * 128, :], fso.t[:, :], r=[fso], w=[self.OUTb])
            self.barrier()

    def ap4(self, buf, col0, dims, rows=None):
        base = buf.t[:, col0:col0 + 1] if rows is None else buf.t[rows[0]:rows[1], col0:col0 + 1]
        return bass.AP(base.tensor, base.offset, [list(base.ap[0])] + [list(d) for d in dims])

    def phase_s5(self):
        HT = self.HT
        TWO_PI = 2.0 * math.pi
        NCH = 320
        SEQC = [(0, 256), (256, 32), (288, 32)]
        HBASE = [0, 257, 290]
        HTOT = 323
        with ExitStack() as ph:
            def T128(name, st=ph):
                return self.sb(st, name, [128, 128], F32)
            lbre, lbim, fre, fim = [T128(n) for n in ("lbre", "lbim", "fre", "fim")]
            H0re, H0im, ilre, ilim, mag8 = [T128(n) for n in ("H0re", "H0im", "ilre", "ilim", "mag8")]
            maskF, maskB = T128("maskF"), T128("maskB")
            I8re, I8im = T128("I8re"), T128("I8im")
            H0bre = self.sb(ph, "H0bre", [128, 128], BF16)
            H0bim = self.sb(ph, "H0bim", [128, 128], BF16)
            RHm = [self.sb(ph, f"RHm{i}", [128, NCH], F32) for i in range(2)]
            NK = 11
            CWre = self.sb(ph, "CWre", [128, NK * 128], F32)
            CWim = self.sb(ph, "CWim", [128, NK * 128], F32)
            FSre = self.sb(ph, "FSre", [128, 256], F32)
            FSim = self.sb(ph, "FSim", [128, 256], F32)
            fso = self.sb(ph, "fso", [128, 128], F32)
            Sel = self.sb(ph, "Sel", [128, 8 * 240], BF16)
            selw = lambda a, b: Sel.t[:, a * 240 + 112 - 16 * b: a * 240 + 112 - 16 * b + 128]
            YK = self.sb(ph, "YK", [128, T], F32)
            pre = ExitStack()
            are, aim, dt, mag, ang, sinr, cosr, tA, tB, tC = [T128(n, pre) for n in ("are", "aim", "dt", "mag", "ang", "sinr", "cosr", "tA", "tB", "tC")]
            ki = self.sb(pre, "ki", [128, 128], I32)
            lsr = self.sb(pre, "lsr", [128, 2], F32)
            lsx = self.sb(pre, "lsx", [128, 128], F32)
            V = lambda b: b.t[:, :]
            dve = lambda fn, r, w: self.op("dve", fn, r=r, w=w)
            TT = lambda o, a, b, op, r, w, eng="dve": self.op(eng, lambda e: e.tensor_tensor(out=o, in0=a, in1=b, op=op), r=r, w=w)

            self.dma("pool", Sel.t[:, :], self.c_sel, r=[self.INb], w=[Sel])
            self.dma("sp", maskF.t[:, :], self.c_maskf, r=[self.INb], w=[maskF])
            self.dma("sp", maskB.t[:, :], self.c_maskb, r=[self.INb], w=[maskB])
            self.load_fm("are", self.ssm_a_re.rearrange("d (pr g2) p -> (d pr) (g2 p)", g2=2), 128, V(are), are, self.INb)
            self.load_fm("aim", self.ssm_a_im.rearrange("d (pr g2) p -> (d pr) (g2 p)", g2=2), 128, V(aim), aim, self.INb)
            self.load_fm("h0r", self.st_re.rearrange("d (pr g2) p -> (d pr) (g2 p)", g2=2), 128, V(H0re), H0re, self.INb)
            self.load_fm("h0i", self.st_im.rearrange("d (pr g2) p -> (d pr) (g2 p)", g2=2), 128, V(H0im), H0im, self.INb)
            self.dma("sp", lsr.t[:, :], self.ssm_log_step.rearrange("d (pr g2) -> (d pr) g2", g2=2), r=[self.INb], w=[lsr])
            for g2 in range(2):
                dve(lambda e, g2=g2: e.tensor_scalar(out=lsx.t[:, g2 * 64:(g2 + 1) * 64], in0=self.onesf.t[:, 0:64],
                                                     scalar1=lsr.t[:, g2:g2 + 1], scalar2=1.0, op0=ALU.mult, op1=ALU.mult),
                    [lsr, self.onesf], [lsx])
            ps = self.nextps()
            self.op("pe", lambda e: e.transpose(out=ps.t[:, 0:128], in_=lsx.t[:, :], identity=self.ident.t[:, :]), r=[lsx, self.ident], w=[ps])
            self.op("act", lambda e: e.activation(out=V(dt), in_=ps.t[:, 0:128], func=AF.Exp), r=[ps], w=[dt])
            TT(V(tA), V(are), V(dt), ALU.mult, [are, dt], [tA])
            self.op("act", lambda e: e.activation(out=V(mag), in_=V(tA), func=AF.Exp), r=[tA], w=[mag])
            TT(V(ang), V(aim), V(dt), ALU.mult, [aim, dt], [ang])
            dve(lambda e: e.tensor_scalar(out=V(tB), in0=V(ang), scalar1=1.0 / TWO_PI, scalar2=1.0, op0=ALU.mult, op1=ALU.mult), [ang], [tB])
            dve(lambda e: e.tensor_copy(out=ki.t[:, :], in_=V(tB)), [tB], [ki])
            dve(lambda e: e.tensor_copy(out=V(tB), in_=ki.t[:, :]), [ki], [tB])
            dve(lambda e: e.scalar_tensor_tensor(out=V(tC), in0=V(tB), scalar=-TWO_PI, in1=V(ang), op0=ALU.mult, op1=ALU.add), [tB, ang], [tC])
            dve(lambda e: e.tensor_scalar(out=V(tC), in0=V(tC), scalar1=3.141592, scalar2=-3.141592, op0=ALU.min, op1=ALU.max), [tC], [tC])
            self.op("act", lambda e: e.activation(out=V(sinr), in_=V(tC), func=AF.Sin), r=[tC], w=[sinr])
            self.op("act", lambda e: e.activation(out=V(tA), in_=V(tC), func=AF.Sin, scale=0.5), r=[tC], w=[tA])
            TT(V(tA), V(tA), V(tA), ALU.mult, [tA], [tA])
            dve(lambda e: e.tensor_scalar(out=V(cosr), in0=V(tA), scalar1=-2.0, scalar2=1.0, op0=ALU.mult, op1=ALU.add), [tA], [cosr])
            TT(V(lbre), V(mag), V(cosr), ALU.mult, [mag, cosr], [lbre])
            TT(V(lbim), V(mag), V(sinr), ALU.mult, [mag, sinr], [lbim])
            dve(lambda e: e.tensor_scalar(out=V(tA), in0=V(lbre), scalar1=-1.0, scalar2=1.0, op0=ALU.add, op1=ALU.mult), [lbre], [tA])
            TT(V(tB), V(are), V(are), ALU.mult, [are], [tB])
            TT(V(tC), V(aim), V(aim), ALU.mult, [aim], [tC])
            TT(V(tB), V(tB), V(tC), ALU.add, [tB, tC], [tB])
            dve(lambda e: e.reciprocal(out=V(tB), in_=V(tB)), [tB], [tB])
            TT(V(fre), V(tA), V(are), ALU.mult, [tA, are], [fre])
            TT(V(tC), V(lbim), V(aim), ALU.mult, [lbim, aim], [tC])
            TT(V(fre), V(fre), V(tC), ALU.add, [fre, tC], [fre])
            TT(V(fre), V(fre), V(tB), ALU.mult, [fre, tB], [fre])
            TT(V(fim), V(lbim), V(are), ALU.mult, [lbim, are], [fim])
            TT(V(tC), V(tA), V(aim), ALU.mult, [tA, aim], [tC])
            TT(V(fim), V(fim), V(tC), ALU.subtract, [fim, tC], [fim])
            TT(V(fim), V(fim), V(tB), ALU.mult, [fim, tB], [fim])
            TT(V(tA), V(mag), V(mag), ALU.mult, [mag], [tA])
            dve(lambda e: e.reciprocal(out=V(tB), in_=V(tA)), [tA], [tB])
            TT(V(ilre), V(lbre), V(tB), ALU.mult, [lbre, tB], [ilre])
            TT(V(ilim), V(lbim), V(tB), ALU.mult, [lbim, tB], [ilim])
            dve(lambda e: e.tensor_scalar(out=V(ilim), in0=V(ilim), scalar1=-1.0, scalar2=1.0, op0=ALU.mult, op1=ALU.mult), [ilim], [ilim])
            TT(V(tC), V(tA), V(tA), ALU.mult, [tA], [tC])
            TT(V(mag8), V(tC), V(tC), ALU.mult, [tC], [mag8])
            dve(lambda e: e.tensor_copy(out=CWre.t[:, 0:128], in_=V(cosr)), [cosr], [CWre])
            dve(lambda e: e.tensor_scalar(out=CWim.t[:, 0:128], in0=V(sinr), scalar1=-1.0, scalar2=1.0, op0=ALU.mult, op1=ALU.mult), [sinr], [CWim])
            for k in range(NK - 1):
                a = CWre.t[:, k * 128:(k + 1) * 128]
                b = CWim.t[:, k * 128:(k + 1) * 128]
                TT(V(tA), a, a, ALU.mult, [CWre], [tA])
                TT(V(tB), b, b, ALU.mult, [CWim], [tB])
                TT(CWre.t[:, (k + 1) * 128:(k + 2) * 128], V(tA), V(tB), ALU.subtract, [tA, tB], [CWre])
                TT(V(tC), a, b, ALU.mult, [CWre, CWim], [tC])
                dve(lambda e, k=k: e.tensor_scalar(out=CWim.t[:, (k + 1) * 128:(k + 2) * 128], in0=V(tC), scalar1=2.0, scalar2=1.0,
                                                   op0=ALU.mult, op1=ALU.mult), [tC], [CWim])
            c3r_, c3i_ = CWre.t[:, 3 * 128:4 * 128], CWim.t[:, 3 * 128:4 * 128]
            TT(V(tA), V(H0re), c3r_, ALU.mult, [H0re, CWre], [tA])
            TT(V(tB), V(H0im), c3i_, ALU.mult, [H0im, CWim], [tB])
            TT(V(tA), V(tA), V(tB), ALU.add, [tA, tB], [tA])
            TT(V(I8re), V(tA), V(mag8), ALU.mult, [tA, mag8], [I8re])
            TT(V(tA), V(H0im), c3r_, ALU.mult, [H0im, CWre], [tA])
            TT(V(tB), V(H0re), c3i_, ALU.mult, [H0re, CWim], [tB])
            TT(V(tA), V(tA), V(tB), ALU.subtract, [tA, tB], [tA])
            TT(V(I8im), V(tA), V(mag8), ALU.mult, [tA, mag8], [I8im])
            dve(lambda e: e.tensor_copy(out=H0bre.t[:, :], in_=V(H0re)), [H0re], [H0bre])
            dve(lambda e: e.tensor_copy(out=H0bim.t[:, :], in_=V(H0im)), [H0im], [H0bim])
            for i_ in range(2):
                dve(lambda e, i_=i_: e.memset(RHm[i_].t[:, :], 1.0), [], [RHm[i_]])
            for pos in (256, 288):
                dve(lambda e, pos=pos: e.memset(RHm[0].t[:, pos:pos + 1], 0.0), [], [RHm[0]])
            for pos in (32, 64):
                dve(lambda e, pos=pos: e.memset(RHm[1].t[:, pos:pos + 1], 0.0), [], [RHm[1]])
            self.barrier()
            pre.close()
            TBre = self.sb(ph, "TBre", [128, 4 * NCH], F32)
            TBim = self.sb(ph, "TBim", [128, 4 * NCH], F32)
            tt1 = self.sb(ph, "tt1", [128, 512], F32)
            tt2 = self.sb(ph, "tt2", [128, 512], F32)
            tt3 = self.sb(ph, "tt3", [128, 512], F32)
            SBre = self.sb(ph, "SBre", [128, 8 * 16], F32)
            SBim = self.sb(ph, "SBim", [128, 8 * 16], F32)
            BBre = self.sb(ph, "BBre", [128, 8 * 16], F32)
            BBim = self.sb(ph, "BBim", [128, 8 * 16], F32)
            CTre = self.sb(ph, "CTre", [128, 8 * 16], F32)
            CTim = self.sb(ph, "CTim", [128, 8 * 16], F32)
            SCx = self.sb(ph, "SCx", [32, 8 * 64], F32)
            SC2x = self.sb(ph, "SC2x", [32, 8 * 128], F32)
            LKre, LKim, ILre, ILim = [self.sb(ph, n, [128, 8], F32) for n in ("LKre", "LKim", "ILre", "ILim")]
            PWre = self.sb(ph, "PWre", [128, 9 * 8], F32)
            PWim = self.sb(ph, "PWim", [128, 9 * 8], F32)
            NPre = self.sb(ph, "NPre", [128, 8 * 8], F32)
            NPim = self.sb(ph, "NPim", [128, 8 * 8], F32)
            t8 = self.sb(ph, "t8", [128, 8], F32)
            FAre = self.sb(ph, "FAre", [128, 512], F32)
            FAim = self.sb(ph, "FAim", [128, 512], F32)
            U = [self.sb(ph, f"U{i}", [128, NCH], BF16) for i in range(8)]
            Zst = [self.sb(ph, f"Zst{i}", [128, 128], F32) for i in range(4)]
            WS = [[self.sb(ph, f"WS{s}_{i}", [128, 128], BF16) for i in range(4)] for s in range(2)]
            ZT = [[self.sb(ph, f"ZT{s}_{i}", [128, 128], BF16) for i in range(4)] for s in range(2)]
            YTn = [[self.sb(ph, f"YTn{s}_{i}", [128, 128], BF16) for i in range(2)] for s in range(2)]
            ZO = [[self.sb(ph, f"ZO{c}_{i}", [128, 128], BF16) for i in range(4)] for c in range(8)]
            HBre = [self.sb(ph, f"HBre{c}", [128, NCH], BF16) for c in range(8)]
            HBim = [self.sb(ph, f"HBim{c}", [128, NCH], BF16) for c in range(8)]
            Tacc = [self.sb(ph, f"Tacc{g}", [128, 128], F32) for g in range(8)]
            Tb = [self.sb(ph, f"Tb{g}", [128, 128], BF16) for g in range(8)]
            Yhi = [self.sb(ph, f"Yhi{g}", [128, NCH], BF16) for g in range(8)]
            Ylo = [self.sb(ph, f"Ylo{g}", [128, NCH], BF16) for g in range(8)]
            m1, m2, m3, m4 = [self.sb(ph, f"m{i}", [128, NCH], F32) for i in range(4)]
            zre, zim, hsre, hsim = [self.sb(ph, n, [128, NCH], F32) for n in ("zre", "zim", "hsre", "hsim")]
            ini = self.sb(ph, "ini", [128, 8], F32)
            for z_ in Zst:
                dve(lambda e, z_=z_: e.memset(z_.t[:, :], 0.0), [], [z_])
            for lst in (WS[0], WS[1], ZT[0], ZT[1]) + tuple(ZO):
                for z_ in lst:
                    dve(lambda e, z_=z_: e.memset(z_.t[:, :], 0.0), [], [z_])
            for hb in HBre + HBim:
                dve(lambda e, hb=hb: e.memset(hb.t[:, :], 0.0), [], [hb])

            seti = 0
            for kc in range(KC):
                for gl in range(8):
                    ps = self.nextps()
                    for i in range(8):
                        self.op("pe", lambda e, i=i: e.matmul(ps.t[:, 0:NCH], lhsT=selw(gl, i),
                                                             rhs=HT.t[:, kc * T + i:(kc + 1) * T:8], start=(i == 0), stop=(i == 7)),
                                r=[Sel] + self.HTb, w=[ps], sig=(i == 7))
                    self.op("act", lambda e: e.activation(out=U[gl].t[:, :], in_=ps.t[:, 0:NCH], func=AF.Identity), r=[ps], w=[U[gl]])
                for (SB_, src) in ((SBre, self.ssm_b_re), (SBim, self.ssm_b_im)):
                    for d in range(2):
                        self.dma("sp", SB_.t[:, d * 64:(d + 1) * 64].rearrange("p (c k) -> p c k", k=16),
                                 src[d, 8 * kc:8 * kc + 8].rearrange("(pq g2) p ci -> (g2 p) pq ci", g2=2), r=[self.INb], w=[SB_])
                for (src, CT_) in ((self.ssm_c_re, CTre), (self.ssm_c_im, CTim)):
                    for d in range(2):
                        self.dma("sp", SCx.t[:, d * 256:(d + 1) * 256].rearrange("p (c k) -> p c k", k=64),
                                 src[d, 8 * kc:8 * kc + 8].rearrange("(pq g2) co p -> (g2 co) pq p", g2=2), r=[self.INb], w=[SCx])
                    for rep in range(2):
                        dve(lambda e, rep=rep: e.tensor_copy(
                            out=SC2x.t[:, :].rearrange("p (c r k) -> p c r k", r=2, k=64)[:, :, rep, :],
                            in_=SCx.t[:, :].rearrange("p (c k) -> p c k", k=64)), [SCx], [SC2x])
                    for cc in range(8):
                        pst = self.nextps()
                        self.op("pe", lambda e: e.transpose(out=pst.t[:, 0:32], in_=SC2x.t[:, cc * 128:(cc + 1) * 128], identity=self.ident.t[0:32, 0:32]),
                                r=[SC2x, self.ident], w=[pst])
                        for g2 in range(2):
                            self.op("act", lambda e, g2=g2: e.activation(out=CT_.t[g2 * 64:(g2 + 1) * 64, cc * 16:(cc + 1) * 16],
                                                                         in_=pst.t[g2 * 64:(g2 + 1) * 64, 16 * g2:16 * g2 + 16], func=AF.Identity),
                                    r=[pst], w=[CT_])
                for d in range(2):
                    c0 = d * 64 + 4 * kc
                    fr = self.bc(fre.t[:, c0:c0 + 4], 16)
                    fi = self.bc(fim.t[:, c0:c0 + 4], 16)
                    sl = slice(d * 64, (d + 1) * 64)
                    v3 = lambda b: b.t[:, sl].rearrange("p (c k) -> p c k", k=16)
                    t3 = tt1.t[:, 0:64].rearrange("p (c k) -> p c k", k=16)
                    TT(v3(BBre), v3(SBre), fr, ALU.mult, [SBre, fre], [BBre])
                    TT(t3, v3(SBim), fi, ALU.mult, [SBim, fim], [tt1])
                    TT(v3(BBre), v3(BBre), t3, ALU.subtract, [BBre, tt1], [BBre])
                    TT(v3(BBim), v3(SBre), fi, ALU.mult, [SBre, fim], [BBim])
                    TT(t3, v3(SBim), fr, ALU.mult, [SBim, fre], [tt1])
                    TT(v3(BBim), v3(BBim), t3, ALU.add, [BBim, tt1], [BBim])
                for d in range(2):
                    c0 = d * 64 + 4 * kc
                    for (dst, src) in ((LKre, lbre), (LKim, lbim), (ILre, ilre), (ILim, ilim)):
                        dve(lambda e, dst=dst, src=src: e.tensor_copy(out=dst.t[:, d * 4:(d + 1) * 4], in_=src.t[:, c0:c0 + 4]), [src], [dst])
                for (Pr, Pi, Lr, Li, nk) in ((PWre, PWim, LKre, LKim, 9), (NPre, NPim, ILre, ILim, 8)):
                    dve(lambda e: e.memset(Pr.t[:, 0:8], 1.0), [], [Pr])
                    dve(lambda e: e.memset(Pi.t[:, 0:8], 0.0), [], [Pi])
                    for k in range(nk - 1):
                        a, b = Pr.t[:, k * 8:(k + 1) * 8], Pi.t[:, k * 8:(k + 1) * 8]
                        a2, b2 = Pr.t[:, (k + 1) * 8:(k + 2) * 8], Pi.t[:, (k + 1) * 8:(k + 2) * 8]
                        TT(a2, a, V(Lr), ALU.mult, [Pr, Lr], [Pr])
                        TT(V(t8), b, V(Li), ALU.mult, [Pi, Li], [t8])
                        TT(a2, a2, V(t8), ALU.subtract, [Pr, t8], [Pr])
                        TT(b2, a, V(Li), ALU.mult, [Pr, Li], [Pi])
                        TT(V(t8), b, V(Lr), ALU.mult, [Pi, Lr], [t8])
                        TT(b2, b2, V(t8), ALU.add, [Pi, t8], [Pi])

                def factor(d, Are, Aim, Pr, Pi, kbase, kstep):
                    cc0 = d * 4
                    a_re = self.ap4(Are, cc0 * 16, [[16, 4], [0, 8], [1, 16]])
                    a_im = self.ap4(Aim, cc0 * 16, [[16, 4], [0, 8], [1, 16]])
                    p_re = self.ap4(Pr, kbase * 8 + cc0, [[1, 4], [kstep * 8, 8], [0, 16]])
                    p_im = self.ap4(Pi, kbase * 8 + cc0, [[1, 4], [kstep * 8, 8], [0, 16]])
                    o_re = FAre.t[:, :].rearrange("p (c i k) -> p c i k", c=4, i=8)
                    o_im = FAim.t[:, :].rearrange("p (c i k) -> p c i k", c=4, i=8)
                    tmp = tt3.t[:, :].rearrange("p (c i k) -> p c i k", c=4, i=8)
                    TT(o_re, a_re, p_re, ALU.mult, [Are, Pr], [FAre], eng="pool")
                    TT(tmp, a_im, p_im, ALU.mult, [Aim, Pi], [tt3], eng="pool")
                    TT(o_re, o_re, tmp, ALU.subtract, [FAre, tt3], [FAre], eng="pool")
                    TT(o_im, a_re, p_im, ALU.mult, [Are, Pi], [FAim], eng="pool")
                    TT(tmp, a_im, p_re, ALU.mult, [Aim, Pr], [tt3], eng="pool")
                    TT(o_im, o_im, tmp, ALU.add, [FAim, tt3], [FAim], eng="pool")

                def half_copy(dst_tiles, src, pq, scale, eng_alt=0):
                    for g2 in range(2):
                        self.op("act", lambda e, g2=g2: e.activation(out=dst_tiles[g2].t[g2 * 64:(g2 + 1) * 64, :],
                                                                     in_=src.t[g2 * 64:(g2 + 1) * 64, pq * 128:(pq + 1) * 128],
                                                                     func=AF.Identity, scale=scale), r=[src], w=[dst_tiles[g2]])

                for d in range(2):
                    c0 = d * 64 + 4 * kc
                    tbo = 0 if d == 0 else 64
                    tre3 = TBre.t[:, :].rearrange("p (c j) -> p c j", j=NCH)[:, :, tbo:tbo + 256]
                    tim3 = TBim.t[:, :].rearrange("p (c j) -> p c j", j=NCH)[:, :, tbo:tbo + 256]
                    dve(lambda e: e.memset(tre3[:, :, 0:1], 1.0), [], [TBre])
                    dve(lambda e: e.memset(tim3[:, :, 0:1], 0.0), [], [TBim])
                    for k in range(8):
                        s = 1 << k
                        wr = self.bc(CWre.t[:, (3 + k) * 128 + c0:(3 + k) * 128 + c0 + 4], s)
                        wi = self.bc(CWim.t[:, (3 + k) * 128 + c0:(3 + k) * 128 + c0 + 4], s)
                        a = tre3[:, :, 0:s]
                        b = tim3[:, :, 0:s]
                        u1 = tt1.t[:, 0:4 * s].rearrange("p (c j) -> p c j", j=s)
                        u2 = tt2.t[:, 0:4 * s].rearrange("p (c j) -> p c j", j=s)
                        TT(u1, a, wr, ALU.mult, [TBre, CWre], [tt1])
                        TT(u2, b, wi, ALU.mult, [TBim, CWim], [tt2])
                        TT(tre3[:, :, s:2 * s], u1, u2, ALU.subtract, [tt1, tt2], [TBre])
                        TT(u1, a, wi, ALU.mult, [TBre, CWim], [tt1])
                        TT(u2, b, wr, ALU.mult, [TBim, CWre], [tt2])
                        TT(tim3[:, :, s:2 * s], u1, u2, ALU.add, [tt1, tt2], [TBim])
                    for TB_ in (TBre, TBim):
                        full3 = TB_.t[:, :].rearrange("p (c j) -> p c j", j=NCH)
                        for dst0 in ((256, 288) if d == 0 else (0, 32)):
                            self.op("pool", lambda e, full3=full3, dst0=dst0: e.tensor_copy(out=full3[:, :, dst0:dst0 + 32], in_=full3[:, :, tbo:tbo + 32]),
                                    r=[TB_], w=[TB_])
                    if d == 0:
                        factor(d, BBre, BBim, PWre, PWim, 7, -1)
                    else:
                        factor(d, BBre, BBim, PWre, PWim, 0, 1)
                    ws_sets = []
                    for pq in range(4):
                        half_copy([Zst[0], Zst[1]], FAre, pq, 1.0)
                        half_copy([Zst[2], Zst[3]], FAim, pq, 1.0)
                        wset = WS[seti % 2]
                        seti += 1
                        for q4 in range(4):
                            pst = self.nextps()
                            self.op("pe", lambda e: e.transpose(out=pst.t[:, 0:128], in_=Zst[q4].t[:, :], identity=self.ident.t[:, :]),
                                    r=[Zst[q4], self.ident], w=[pst])
                            self.op("dve", lambda e: e.tensor_copy(out=wset[q4].t[:, :], in_=pst.t[:, 0:128]), r=[pst], w=[wset[q4]])
                        if d == 1:
                            zt = ZT[seti % 2]
                            half_copy([zt[0], zt[1]], FAre, pq, 1.0)
                            half_copy([zt[2], zt[3]], FAim, pq, -1.0)
                            ws_sets.append((wset, zt))
                        else:
                            ws_sets.append((wset, None))
                        cc = d * 4 + pq
                        col = c0 + pq
                        psS = [self.nextps(), self.nextps()]
                        for ri_ in range(2):
                            for g2 in range(2):
                                self.op("pe", lambda e, g2=g2, ri_=ri_: e.matmul(psS[ri_].t[:, 0:NCH], lhsT=wset[ri_ * 2 + g2].t[:, :],
                                                                                  rhs=U[pq * 2 + g2].t[:, :], start=(g2 == 0), stop=(g2 == 1)),
                                        r=[wset[ri_ * 2 + g2], U[pq * 2 + g2]], w=[psS[ri_]], sig=(g2 == 1))
                        hbre, hbim = HBre[cc], HBim[cc]
                        rho = mag8.t[:, col:col + 1]
                        L = NCH
                        Tr = TBre.t[:, pq * NCH:(pq + 1) * NCH]
                        Ti = TBim.t[:, pq * NCH:(pq + 1) * NCH]
                        bre = psS[0].t[:, 0:L] if d == 0 else self.rev(psS[0].t[:, 0:L])
                        bim = psS[1].t[:, 0:L] if d == 0 else self.rev(psS[1].t[:, 0:L])
                        TT(m1.t[:, 0:L], bre, Tr, ALU.mult, [psS[0], TBre], [m1])
                        TT(m4.t[:, 0:L], bre, Ti, ALU.mult, [psS[0], TBim], [m4])
                        TT(m2.t[:, 0:L], bim, Ti, ALU.mult, [psS[1], TBim], [m2])
                        TT(m3.t[:, 0:L], bim, Tr, ALU.mult, [psS[1], TBre], [m3])
                        TT(zre.t[:, 0:L], m1.t[:, 0:L], m2.t[:, 0:L], ALU.subtract, [m1, m2], [zre], eng="pool")
                        TT(zim.t[:, 0:L], m3.t[:, 0:L], m4.t[:, 0:L], ALU.add, [m3, m4], [zim], eng="pool")
                        pS = 0 if d == 0 else 64
                        TT(zre.t[:, pS:pS + 1], zre.t[:, pS:pS + 1], I8re.t[:, col:col + 1], ALU.add, [zre, I8re], [zre])
                        TT(zim.t[:, pS:pS + 1], zim.t[:, pS:pS + 1], I8im.t[:, col:col + 1], ALU.add, [zim, I8im], [zim])
                        RHv = tt1.t[:, 0:NCH]
                        dve(lambda e: e.tensor_scalar(out=RHv, in0=RHm[d].t[:, :], scalar1=rho, scalar2=1.0, op0=ALU.mult, op1=ALU.mult),
                            [RHm[d], mag8], [tt1])
                        dve(lambda e: e.tensor_tensor_scan(out=hsre.t[:, 0:L], data0=RHv, data1=zre.t[:, 0:L],
                                                           initial=0.0, op0=ALU.mult, op1=ALU.add), [tt1, zre], [hsre])
                        dve(lambda e: e.tensor_tensor_scan(out=hsim.t[:, 0:L], data0=RHv, data1=zim.t[:, 0:L],
                                                           initial=0.0, op0=ALU.mult, op1=ALU.add), [tt1, zim], [hsim])
                        TT(m1.t[:, 0:L], hsre.t[:, 0:L], Tr, ALU.mult, [hsre, TBre], [m1])
                        TT(m2.t[:, 0:L], hsim.t[:, 0:L], Ti, ALU.mult, [hsim, TBim], [m2])
                        TT(m3.t[:, 0:L], hsim.t[:, 0:L], Tr, ALU.mult, [hsim, TBre], [m3])
                        TT(m4.t[:, 0:L], hsre.t[:, 0:L], Ti, ALU.mult, [hsre, TBim], [m4])
                        ore = hbre.t[:, 0:L] if d == 0 else self.rev(hbre.t[:, 0:L])
                        oim = hbim.t[:, 0:L] if d == 0 else self.rev(hbim.t[:, 0:L])
                        TT(ore, m1.t[:, 0:L], m2.t[:, 0:L], ALU.add, [m1, m2], [hbre], eng="pool")
                        TT(oim, m3.t[:, 0:L], m4.t[:, 0:L], ALU.subtract, [m3, m4], [hbim], eng="pool")
                        for sq_ in (1, 2):
                            lp = (287 if sq_ == 1 else 319) if d == 0 else (63 if sq_ == 1 else 31)
                            fcol = ((sq_ - 1) * 2 + d) * 64 + (4 * kc + pq)
                            TT(FSre.t[:, fcol:fcol + 1], m1.t[:, lp:lp + 1], m2.t[:, lp:lp + 1], ALU.add, [m1, m2], [FSre])
                            TT(FSim.t[:, fcol:fcol + 1], m3.t[:, lp:lp + 1], m4.t[:, lp:lp + 1], ALU.subtract, [m3, m4], [FSim])
                    if d == 0:
                        factor(d, BBre, BBim, NPre, NPim, 0, 1)
                        for pq in range(4):
                            zt = ZT[(seti + pq) % 2]
                            half_copy([zt[0], zt[1]], FAre, pq, 1.0)
                            half_copy([zt[2], zt[3]], FAim, pq, -1.0)
                            ws_sets[pq] = (ws_sets[pq][0], zt)
                            self._s5_tgen_pending = None
                    if d == 0:
                        factor(d, CTre, CTim, PWre, PWim, 0, 1)
                    else:
                        factor(d, CTre, CTim, NPre, NPim, 0, 1)
                    for pq in range(4):
                        yt = YTn[pq % 2]
                        self.op("act", lambda e: e.activation(out=yt[0].t[:, :], in_=FAre.t[:, pq * 128:(pq + 1) * 128], func=AF.Identity), r=[FAre], w=[yt[0]])
                        self.op("act", lambda e: e.activation(out=yt[1].t[:, :], in_=FAim.t[:, pq * 128:(pq + 1) * 128], func=AF.Identity), r=[FAim], w=[yt[1]])
                        zt = ws_sets[pq][1]
                        for g2 in range(2):
                            gl = pq * 2 + g2
                            pT = self.nextps()
                            self.op("pe", lambda e: e.matmul(pT.t[:, 0:128], lhsT=zt[g2].t[:, :], rhs=yt[0].t[:, :], start=True, stop=False),
                                    r=[zt[g2], yt[0]], w=[pT], sig=False)
                            self.op("pe", lambda e: e.matmul(pT.t[:, 0:128], lhsT=zt[2 + g2].t[:, :], rhs=yt[1].t[:, :], start=False, stop=True),
                                    r=[zt[2 + g2], yt[1]], w=[pT])
                            if d == 0:
                                TT(Tacc[gl].t[:, :], pT.t[:, 0:128], V(maskF), ALU.mult, [pT, maskF], [Tacc[gl]])
                            else:
                                TT(tt1.t[:, 0:128], pT.t[:, 0:128], V(maskB), ALU.mult, [pT, maskB], [tt1])
                                TT(Tb[gl].t[:, :], Tacc[gl].t[:, :], tt1.t[:, 0:128], ALU.add, [Tacc[gl], tt1], [Tb[gl]], eng="pool")
                    if d == 0:
                        factor(d, CTre, CTim, PWre, PWim, 1, 1)
                    else:
                        factor(d, CTre, CTim, PWre, PWim, 8, -1)
                    for pq in range(4):
                        zo = ZO[d * 4 + pq]
                        half_copy([zo[0], zo[1]], FAre, pq, 1.0)
                        half_copy([zo[2], zo[3]], FAim, pq, -1.0)
                for gl in range(8):
                    pq, g2 = gl // 2, gl % 2
                    pY = self.nextps()
                    self.op("pe", lambda e: e.matmul(pY.t[:, 0:NCH], lhsT=Tb[gl].t[:, :], rhs=U[gl].t[:, :], start=True, stop=False),
                            r=[Tb[gl], U[gl]], w=[pY], sig=False)
                    for d in range(2):
                        cc = d * 4 + pq
                        zo = ZO[cc]
                        col = d * 64 + 4 * kc + pq
                        pieces = []
                        for sq_ in range(3):
                            n0, L = SEQC[sq_]
                            if d == 0:
                                pieces.append((n0 + 1, L - 1, HBre[cc].t[:, n0:n0 + L - 1], HBim[cc].t[:, n0:n0 + L - 1], [HBre[cc]], [HBim[cc]]))
                            else:
                                pieces.append((n0, L - 1, HBre[cc].t[:, n0 + 1:n0 + L], HBim[cc].t[:, n0 + 1:n0 + L], [HBre[cc]], [HBim[cc]]))
                        ic = 0 if d == 0 else 255
                        pieces.append((ic, 1, H0bre.t[:, col:col + 1], H0bim.t[:, col:col + 1], [H0bre], [H0bim]))
                        for pi_, (o0, Ln, rre, rim, bre_, bim_) in enumerate(pieces):
                            lastmm = (d == 1 and pi_ == len(pieces) - 1)
                            self.op("pe", lambda e: e.matmul(pY.t[:, o0:o0 + Ln], lhsT=zo[g2].t[:, :], rhs=rre, start=False, stop=False),
                                    r=[zo[g2]] + bre_, w=[pY], sig=False)
                            self.op("pe", lambda e: e.matmul(pY.t[:, o0:o0 + Ln], lhsT=zo[2 + g2].t[:, :], rhs=rim, start=False, stop=lastmm),
                                    r=[zo[2 + g2]] + bim_, w=[pY], sig=lastmm)
                    self.op("act", lambda e: e.activation(out=Yhi[gl].t[:, :], in_=pY.t[:, 0:NCH], func=AF.Identity), r=[pY], w=[Yhi[gl]])
                    TT(Ylo[gl].t[:, :], pY.t[:, 0:NCH], Yhi[gl].t[:, :], ALU.subtract, [pY, Yhi[gl]], [Ylo[gl]])
                for j in range(8):
                    pU = self.nextps()
                    for gl in range(8):
                        self.op("pe", lambda e: e.matmul(pU.t[:, 0:NCH], lhsT=selw(j, gl), rhs=Yhi[gl].t[:, :],
                                                         start=(gl == 0), stop=False), r=[Sel, Yhi[gl]], w=[pU], sig=False)
                        self.op("pe", lambda e: e.matmul(pU.t[:, 0:NCH], lhsT=selw(j, gl), rhs=Ylo[gl].t[:, :],
                                                         start=False, stop=(gl == 7)), r=[Sel, Ylo[gl]], w=[pU], sig=(gl == 7))
                    self.op("act", lambda e: e.activation(out=YK.t[:, j:T:8], in_=pU.t[:, 0:NCH], func=AF.Identity), r=[pU], w=[YK])
                self.dma("sp", self.YT[kc], YK.t[:, :], r=[YK], w=self.YTb)
            for (FS, dst) in ((FSre, self.new_sre), (FSim, self.new_sim)):
                for hlf in range(2):
                    pst = self.nextps()
                    self.op("pe", lambda e, hlf=hlf, FS=FS: e.transpose(out=pst.t[:, 0:128], in_=FS.t[:, hlf * 128:(hlf + 1) * 128], identity=self.ident.t[:, :]),
                            r=[FS, self.ident], w=[pst])
                    self.op("act", lambda e: e.activation(out=fso.t[:, :], in_=pst.t[:, 0:128], func=AF.Identity), r=[pst], w=[fso])
                    self.dma("sp", dst[hlf * 128:(hlf + 1) * 128, :], fso.t[:, :], r=[fso], w=[self.OUTb])
            self.barrier()

    def phase_glu(self):
        l = 1
        C1 = 2.0 * 0.7978845608028654
        with ExitStack() as ph:
            xt = self.sb(ph, "gx", [128, KC * TW], F32)
            yt = self.sb(ph, "gy", [128, KC * TW], F32)
            GB = self.sb(ph, "gb", [128, KC * TW], BF16)
            wblk = [self.sb(ph, f"gw{i}", [128, KC * 512], BF16) for i in range(2)]
            ta = [self.sb(ph, f"gta{i}", [128, TW], F32) for i in range(2)]
            tb_ = [self.sb(ph, f"gtb{i}", [128, TW], F32) for i in range(2)]
            sg = [self.sb(ph, f"gsg{i}", [128, TW], F32) for i in range(2)]
            DS = self.sb(ph, "DS", [128, 16], F32)
            BG = self.sb(ph, "BG", [128, 16], F32)
            rst = self.sb(ph, "rst", [128, TW], F32)
            self.load_fm("ds", self.ssm_d.rearrange("(k p) -> k p", p=128), 16, DS.t[:, :], DS, self.INb)
            self.load_fm("bg", self.b_glu.rearrange("(k p) -> k p", p=128), 16, BG.t[:, :], BG, self.INb)
            wi = 0
            for tt in range(NT):
                n = 0 if tt < 4 else 1
                tok = slice(tt * TW, (tt + 1) * TW)
                self.dma("sp", xt.t[:, :].rearrange("p (k t) -> p k t", k=KC), self.xt_dram(tt), r=[self.XTb[tt]], w=[xt])
                self.dma("sp", yt.t[:, :].rearrange("p (k t) -> p k t", k=KC),
                         self.YT[:, :, tt * TW:(tt + 1) * TW].rearrange("k p t -> p k t"), r=[self.YTb[tt]], w=[yt])
                self.dma("sp", rst.t[:, :], self.RSD[:, tt * TW:(tt + 1) * TW], r=[self.RSDb], w=[rst])
                for kc in range(KC):
                    a, b = ta[kc % 2], tb_[kc % 2]
                    ks = slice(kc * TW, (kc + 1) * TW)
                    self.op("dve", lambda e: e.scalar_tensor_tensor(out=a.t[:, :], in0=xt.t[:, ks], scalar=self.av(l, 0, kc, n), in1=rst.t[:, :],
                                                                    op0=ALU.mult, op1=ALU.mult), r=[xt, rst, self.AV], w=[a])
                    self.op("act", lambda e: e.activation(out=a.t[:, :], in_=a.t[:, :], func=AF.Identity, bias=self.modv(l, 0, kc, n), scale=1.0),
                            r=[a, self.MOD], w=[a])
                    self.op("dve", lambda e: e.scalar_tensor_tensor(out=yt.t[:, ks], in0=a.t[:, :], scalar=DS.t[:, kc:kc + 1], in1=yt.t[:, ks],
                                                                    op0=ALU.mult, op1=ALU.add), r=[a, DS, yt], w=[yt])
                    self.op("dve", lambda e: e.tensor_tensor(out=b.t[:, :], in0=yt.t[:, ks], in1=yt.t[:, ks], op=ALU.mult), r=[yt], w=[b])
                    self.op("dve", lambda e: e.tensor_scalar(out=b.t[:, :], in0=b.t[:, :], scalar1=0.044715, scalar2=1.0, op0=ALU.mult, op1=ALU.add),
                            r=[b], w=[b])
                    self.op("dve", lambda e: e.tensor_tensor(out=b.t[:, :], in0=b.t[:, :], in1=yt.t[:, ks], op=ALU.mult), r=[b, yt], w=[b])
                    self.op("act", lambda e: e.activation(out=b.t[:, :], in_=b.t[:, :], func=AF.Sigmoid, scale=C1), r=[b], w=[b])
                    self.op("dve", lambda e: e.tensor_tensor(out=yt.t[:, ks], in0=yt.t[:, ks], in1=b.t[:, :], op=ALU.mult), r=[b, yt], w=[yt])
                    self.op("act", lambda e: e.activation(out=GB.t[:, ks], in_=yt.t[:, ks], func=AF.Identity), r=[yt], w=[GB])
                for mcb in range(4):
                    wb = wblk[wi % 2]
                    wi += 1
                    self.dma("pool", wb.t[:, :].rearrange("p (k m) -> p k m", k=KC),
                             self.w_glu[:, mcb * 512:(mcb + 1) * 512].rearrange("(k p) m -> p k m", p=128), r=[self.INb], w=[wb])
                    for m2 in range(4):
                        mc = mcb * 4 + m2
                        ms_ = slice(mc * TW, (mc + 1) * TW)
                        ps = self.nextps()
                        for kc in range(KC):
                            self.op("pe", lambda e, kc=kc: e.matmul(ps.t[:, :], lhsT=wb.t[:, kc * 512 + m2 * 128: kc * 512 + (m2 + 1) * 128],
                                                                  rhs=GB.t[:, kc * TW:(kc + 1) * TW], start=(kc == 0), stop=(kc == KC - 1)),
                                    r=[wb, GB], w=[ps], sig=(kc == KC - 1))
                        s_ = sg[mc % 2]
                        self.op("act", lambda e: e.activation(out=s_.t[:, :], in_=ps.t[:, :], func=AF.Sigmoid, bias=BG.t[:, mc:mc + 1], scale=1.0),
                                r=[ps, BG], w=[s_])
                        self.op("dve", lambda e: e.tensor_tensor(out=s_.t[:, :], in0=s_.t[:, :], in1=yt.t[:, ms_], op=ALU.mult), r=[s_, yt], w=[s_])
                        self.op("dve", lambda e: e.scalar_tensor_tensor(out=xt.t[:, ms_], in0=s_.t[:, :], scalar=self.modv(l, 2, mc, n), in1=xt.t[:, ms_],
                                                                        op0=ALU.mult, op1=ALU.add), r=[s_, xt, self.MOD], w=[xt])
                self.dma("sp", self.xt_dram(tt), xt.t[:, :].rearrange("p (k t) -> p k t", k=KC), r=[xt], w=[self.XTb[tt]])
            self.barrier()


def _rope_consts():
    half = 64
    inv = (10000.0 ** (-np.arange(0, half, 2, dtype=np.float32) / np.float32(half))).astype(np.float32)
    r, col = np.meshgrid(np.arange(32), np.arange(64), indexing="ij")
    r = r.reshape(-1).astype(np.float32)
    col = col.reshape(-1).astype(np.float32)
    ar = r[:, None] * inv
    ac = col[:, None] * inv
    ang = np.concatenate([ar, ar, ac, ac], axis=-1)
    cosT = np.ascontiguousarray(np.cos(ang).T.astype(np.float32))
    sinT = np.ascontiguousarray(np.sin(ang).T.astype(np.float32))
    P = np.zeros((128, 128), np.float32)
    for a in range(2):
        for i in range(32):
            P[a * 64 + i, a * 64 + 32 + i] = -1.0
            P[a * 64 + 32 + i, a * 64 + i] = 1.0
    return cosT, sinT, np.ascontiguousarray(P.T)


def _s5_consts():
    sel = np.zeros((128, 8, 240), np.float32)
    for a in range(8):
        for c in range(16):
            sel[a * 16 + c, a, 112 + c] = 1.0
    ii = np.arange(128) // 16
    maskf = (ii[None, :] >= ii[:, None]).astype(np.float32)
    maskb = (ii[:, None] >= ii[None, :]).astype(np.float32)
    return np.ascontiguousarray(sel.reshape(128, 8 * 240)), maskf, maskb


_SEL, _MASKF, _MASKB = _s5_consts()
_CACHE = {}


def _get_prog(debug=False, stop_after=None):
    key = (debug, stop_after)
    if key not in _CACHE:
        mk = MK(debug=debug, stop_after=stop_after)
        mk.build()
        _CACHE[key] = mk
    return _CACHE[key]


def make_in_maps(inp, cores):
    cosT, sinT, ropeT = _rope_consts()
    ident = np.eye(128, dtype=np.float32)
    f = lambda a: np.ascontiguousarray(np.asarray(a, dtype=np.float32))
    shared = {
        "w_mod": f(inp["w_mod"]), "b_mod": f(inp["b_mod"]), "norm_g": f(inp["norm_g"]),
        "w_qkv": f(inp["w_qkv"][0]), "lam_vecs": f(inp["lam_vecs"][0]), "subln_g": f(inp["subln_g"][0]),
        "w_o": f(inp["w_o"][0]),
        "ssm_a_re": f(inp["ssm_a_re"][0]), "ssm_a_im": f(inp["ssm_a_im"][0]), "ssm_log_step": f(inp["ssm_log_step"][0]),
        "ssm_b_re": f(inp["ssm_b_re"][0]), "ssm_b_im": f(inp["ssm_b_im"][0]),
        "ssm_c_re": f(inp["ssm_c_re"][0]), "ssm_c_im": f(inp["ssm_c_im"][0]),
        "ssm_d": f(inp["ssm_d"][0]), "w_glu": f(inp["w_glu"][0]), "b_glu": f(inp["b_glu"][0]),
        "w_up": f(inp["w_up"]), "conv_w": f(inp["conv_w"]), "conv_b": f(inp["conv_b"]), "w_down": f(inp["w_down"]),
        "final_g": f(inp["final_g"]),
        "c_ident": ident, "c_ropeT": ropeT, "c_cos": cosT, "c_sin": sinT,
        "c_sel": _SEL, "c_maskf": _MASKF, "c_maskb": _MASKB,
    }
    maps = []
    for c in cores:
        m = dict(shared)
        m["x_all"] = np.ascontiguousarray(np.concatenate(
            [f(inp["x_sample"][c]), f(inp["x_prompt"][2 * c]), f(inp["x_prompt"][2 * c + 1])], axis=0))
        m["cond"] = np.ascontiguousarray(np.stack([f(inp["c"][c]), f(inp["c_ctx"])], axis=0))
        m["cache_k"] = f(inp["cache_k"][c, 0]).reshape(256, 2048)
        m["cache_v"] = f(inp["cache_v"][c, 0]).reshape(256, 2048)
        m["st_re"] = f(inp["state_re"][c, 0])
        m["st_im"] = f(inp["state_im"][c, 0])
        maps.append(m)
    return maps


def kernel(**inputs):
    mk = _get_prog()
    cores = list(range(NCORES))
    maps = make_in_maps(inputs, cores)
    res = run_bass_kernel_spmd(mk.nc, maps, core_ids=cores)
    R = res.results
    y_prompt = np.zeros((16, 256, D), np.float32)
    y_sample = np.zeros((8, 2048, D), np.float32)
    nk = np.zeros((16, 1, 256, 8, 2, 128), np.float32)
    nv = np.zeros((16, 1, 256, 8, 256), np.float32)
    sre = np.zeros((16, 1, 2, 128, 64), np.float32)
    sim = np.zeros((16, 1, 2, 128, 64), np.float32)
    for c in cores:
        r = R[c]
        ya = np.asarray(r["y_all"])
        y_sample[c] = ya[0:2048]
        y_prompt[2 * c] = ya[2048:2304]
        y_prompt[2 * c + 1] = ya[2304:2560]
        k = np.asarray(r["new_k"]).reshape(2, 256, 8, 2, 128)
        v = np.asarray(r["new_v"]).reshape(2, 256, 8, 256)
        nk[2 * c:2 * c + 2, 0] = k
        nv[2 * c:2 * c + 2, 0] = v
        a = np.asarray(r["new_sre"]).reshape(2, 2, 64, 2, 64).reshape(2, 2, 128, 64)
        b = np.asarray(r["new_sim"]).reshape(2, 2, 64, 2, 64).reshape(2, 2, 128, 64)
        sre[2 * c:2 * c + 2, 0] = a
        sim[2 * c:2 * c + 2, 0] = b
    return (y_prompt, y_sample, nk, nv, sre, sim)
```

```python
import math
import os
from contextlib import ExitStack

import numpy as np
import concourse.bass as bass
import concourse.mybir as mybir
from concourse.bass_utils import run_bass_kernel_spmd

F32 = mybir.dt.float32
BF16 = mybir.dt.bfloat16
I32 = mybir.dt.int32
ALU = mybir.AluOpType
AF = mybir.ActivationFunctionType

NCORES = 8
D = 2048
KC = 16
T = 2560
NT = 5
TW = 512
DFF = 5632
JC = 44
NH = 8
SEQS = [(0, 2048), (2048, 256), (2304, 256)]
NORM_EPS = 1e-6
SUBLN_EPS = 1e-5
LAM_INIT0 = 0.8 - 0.6 * math.exp(-0.3 * 0)
ATT_SCALE = 128 ** -0.5
CPAD = [1, 2051, 2309]
CW_TOT = 2566


class Buf:
    __slots__ = ("name", "t", "lw", "rd", "psum")

    def __init__(self, name, t=None, psum=False):
        self.name = name
        self.t = t
        self.lw = None
        self.rd = {}
        self.psum = psum


class MK:
    def __init__(self, debug=False, stop_after=None):
        self.debug = debug
        self.stop_after = stop_after
        self.nc = bass.Bass("TRN2", target_bir_lowering=False)
        nc = self.nc
        self.es = ExitStack()
        self.engs = {"pe": nc.tensor, "act": nc.scalar, "dve": nc.vector, "pool": nc.gpsimd, "sp": nc.sync}
        self.sems = {}
        self.cnt = {}
        for k in self.engs:
            self.sems["e_" + k] = self.es.enter_context(nc.semaphore("e_" + k))
            self.cnt["e_" + k] = 0
        self.ndq = {"sp": 14, "pool": 14}
        self.drr = {"sp": 0, "pool": 0}
        for q, n in self.ndq.items():
            for i in range(n):
                key = f"d_{q}_{i}"
                self.sems[key] = self.es.enter_context(nc.semaphore(key))
                self.cnt[key] = 0
        self.waited = {}
        self.ps = []
        for i in range(8):
            t = self.es.enter_context(nc.psum_tensor(f"ps{i}", [128, 512], F32))
            self.ps.append(Buf(f"ps{i}", t, psum=True))
        self.psrr = 0
        self.dram_in = {}
        self.dram_out = {}

    def cur(self, key):
        return self.cnt[key] * (16 if key.startswith("d_") else 1)

    def wait(self, eng, ev):
        if ev is None:
            return
        key, val = ev
        if val <= 0:
            return
        if eng == "pe" and key == "e_pe":
            return
        if self.waited.get((eng, key), 0) >= val:
            return
        self.engs[eng].wait_ge(self.sems[key], val)
        self.waited[(eng, key)] = val

    def _deps(self, eng, r, w):
        for b in r:
            self.wait(eng, b.lw)
            if b.psum:
                for k, v in b.rd.items():
                    if k != "e_" + eng:
                        self.wait(eng, (k, v))
        for b in w:
            self.wait(eng, b.lw)
            for k, v in b.rd.items():
                self.wait(eng, (k, v))

    def _record(self, ev, r, w):
        for b in r:
            if b.rd.get(ev[0], 0) < ev[1]:
                b.rd[ev[0]] = ev[1]
        for b in w:
            b.lw = ev
            b.rd = {}

    def op(self, eng, fn, r=(), w=(), sig=True):
        self._deps(eng, r, w)
        ins = fn(self.engs[eng])
        key = "e_" + eng
        if sig:
            self.cnt[key] += 1
            ins.then_inc(self.sems[key], 1)
            ev = (key, self.cnt[key])
        else:
            ev = (key, self.cnt[key] + 1)
        self._record(ev, r, w)
        return ins

    def dma(self, q, out, in_, r=(), w=(), **kw):
        self._deps(q, r, w)
        idx = self.drr[q] % self.ndq[q]
        self.drr[q] += 1
        key = f"d_{q}_{idx}"
        self.wait(q, (key, self.cur(key)))
        ins = self.engs[q].dma_start(out=out, in_=in_, **kw)
        ins.then_inc(self.sems[key], 16)
        self.cnt[key] += 1
        ev = (key, self.cur(key))
        self._record(ev, r, w)

    def barrier(self):
        for eng in self.engs:
            for key in self.sems:
                self.wait(eng, (key, self.cur(key)))

    def dbg(self, name, buf, ap, shape, dtype=F32):
        if not self.debug:
            return
        d = self.dout("dbg_" + name, shape, dtype)
        self.dma("sp", d, ap, r=[buf], w=[self.OUTb])

    def nextps(self):
        p = self.ps[self.psrr % 8]
        self.psrr += 1
        return p

    def sb(self, st, name, shape, dtype):
        self.uid = getattr(self, "uid", 0) + 1
        name = f"{name}_{self.uid}"
        t = st.enter_context(self.nc.sbuf_tensor(name, list(shape), dtype))
        return Buf(name, t)

    def din(self, name, shape, dtype=F32):
        h = self.nc.dram_tensor(name, list(shape), dtype, kind="ExternalInput")
        self.dram_in[name] = h
        return h.ap()

    def dout(self, name, shape, dtype=F32):
        h = self.nc.dram_tensor(name, list(shape), dtype, kind="ExternalOutput")
        self.dram_out[name] = h
        return h.ap()

    def dscr(self, name, shape, dtype):
        if self.debug:
            return self.dout(name, shape, dtype)
        return self.nc.dram_tensor(name, list(shape), dtype).ap()

    def load_fm(self, st_name, src2d, R, dst_ap, dstbuf, srcbuf):
        stg = self.fm_stage[self.fm_i % 2]
        self.fm_i += 1
        self.dma("sp", stg.t[0:R, :], src2d, r=[srcbuf], w=[stg])
        ps = self.nextps()
        self.op("pe", lambda e: e.transpose(out=ps.t[:, 0:R], in_=stg.t[0:R, :], identity=self.ident.t[0:R, 0:R]),
                r=[stg, self.ident], w=[ps])
        self.op("dve", lambda e: e.tensor_copy(out=dst_ap, in_=ps.t[:, 0:R]), r=[ps], w=[dstbuf])

    def build(self):
        nc = self.nc
        es = self.es
        self.x_all = self.din("x_all", [T, D])
        self.cond = self.din("cond", [2, D])
        self.cache_k = self.din("cache_k", [256, 2048])
        self.cache_v = self.din("cache_v", [256, 2048])
        self.st_re = self.din("st_re", [2, 128, 64])
        self.st_im = self.din("st_im", [2, 128, 64])
        self.w_mod = self.din("w_mod", [2, D, 6 * D])
        self.b_mod = self.din("b_mod", [2, 6 * D])
        self.norm_g = self.din("norm_g", [2, 2, D])
        self.w_qkv = self.din("w_qkv", [D, 3 * D])
        self.lam_vecs = self.din("lam_vecs", [4, 128])
        self.subln_g = self.din("subln_g", [256])
        self.w_o = self.din("w_o", [D, D])
        self.ssm_a_re = self.din("ssm_a_re", [2, 128, 64])
        self.ssm_a_im = self.din("ssm_a_im", [2, 128, 64])
        self.ssm_log_step = self.din("ssm_log_step", [2, 128])
        self.ssm_b_re = self.din("ssm_b_re", [2, 128, 64, 16])
        self.ssm_b_im = self.din("ssm_b_im", [2, 128, 64, 16])
        self.ssm_c_re = self.din("ssm_c_re", [2, 128, 16, 64])
        self.ssm_c_im = self.din("ssm_c_im", [2, 128, 16, 64])
        self.ssm_d = self.din("ssm_d", [D])
        self.w_glu = self.din("w_glu", [D, D])
        self.b_glu = self.din("b_glu", [D])
        self.w_up = self.din("w_up", [2, D, 2 * DFF])
        self.conv_w = self.din("conv_w", [2, 3, 2 * DFF])
        self.conv_b = self.din("conv_b", [2, 2 * DFF])
        self.w_down = self.din("w_down", [2, DFF, D])
        self.final_g = self.din("final_g", [D])
        self.c_ident = self.din("c_ident", [128, 128])
        self.c_ropeT = self.din("c_ropeT", [128, 128])
        self.c_cos = self.din("c_cos", [128, 2048])
        self.c_sin = self.din("c_sin", [128, 2048])
        self.c_sel = self.din("c_sel", [128, 8 * 240])
        self.c_maskf = self.din("c_maskf", [128, 128])
        self.c_maskb = self.din("c_maskb", [128, 128])

        self.y_all = self.dout("y_all", [T, D])
        self.new_k = self.dout("new_k", [2, 256, 2048])
        self.new_v = self.dout("new_v", [2, 256, 2048])
        self.new_sre = self.dout("new_sre", [256, 128])
        self.new_sim = self.dout("new_sim", [256, 128])

        self.XT = self.dscr("XT", [KC, 128, T], F32)
        self.OT = self.dscr("OT", [KC, 128, T], BF16)
        self.AT = self.dscr("AT", [JC, 128, T], BF16)
        self.YT = self.dscr("YT", [KC, 128, T], F32)
        self.INb = Buf("inputs")
        self.XTb = [Buf(f"XT{i}") for i in range(NT)]
        self.OTb = [Buf(f"OT{i}") for i in range(NT)]
        self.ATb = [Buf(f"AT{i}") for i in range(NT)]
        self.YTb = [Buf(f"YT{i}") for i in range(NT)]
        self.OUTb = Buf("outputs")

        self.ident = self.sb(es, "ident", [128, 128], F32)
        self.onesb = self.sb(es, "onesb", [128, 128], BF16)
        self.onesf = self.sb(es, "onesf", [128, 128], F32)
        self.cst = self.sb(es, "cst", [128, 8], F32)
        self.MOD = self.sb(es, "MOD", [128, 2 * 96 * 2], F32)
        self.AV = self.sb(es, "AV", [128, 2 * 2 * 16 * 2], F32)
        self.AFN = self.sb(es, "AFN", [128, 16], F32)
        self.RSTD = None
        self.RSD = self.dscr("RSD", [128, T], F32)
        self.RSDb = Buf("RSD")
        self.HT = None
        self.HTb = [Buf(f"HT{i}") for i in range(NT)]
        self.ht_scope = None
        self.fm_stage = [self.sb(es, f"fmst{i}", [128, 128], F32) for i in range(2)]
        self.fm_i = 0

        self.dma("sp", self.ident.t[:], self.c_ident, r=[self.INb], w=[self.ident])
        self.op("dve", lambda e: e.memset(self.onesb.t[:], 1.0), w=[self.onesb])
        self.op("dve", lambda e: e.memset(self.onesf.t[:], 1.0), w=[self.onesf])
        self.op("dve", lambda e: e.memset(self.cst.t[:, 0:1], NORM_EPS), w=[self.cst])
        self.op("dve", lambda e: e.memset(self.cst.t[:, 1:2], SUBLN_EPS), w=[self.cst])
        self.op("dve", lambda e: e.memset(self.cst.t[:, 2:3], 0.0), w=[self.cst])
        self.op("dve", lambda e: e.memset(self.cst.t[:, 3:4], 1.0), w=[self.cst])

        phases = [
            ("mod", self.phase_mod),
            ("loadx", lambda: (self.ht_open(), self.phase_loadx())),
            ("attn", lambda: (self.phase_attn(), self.ht_close())),
            ("wo", lambda: self.phase_proj(self.OT, self.OTb, KC, self.w_o, 0, 0)),
            ("norm02", lambda: (self.ht_open(), self.phase_norm(0, 1))),
            ("up0", lambda: (self.phase_up(0), self.ht_close())),
            ("down0", lambda: self.phase_proj(self.AT, self.ATb, JC, self.w_down[0], 0, 1)),
            ("norm11", lambda: (self.ht_open(), self.phase_norm(1, 0))),
            ("s5", lambda: (self.phase_s5(), self.ht_close())),
            ("glu", self.phase_glu),
            ("norm12", lambda: (self.ht_open(), self.phase_norm(1, 1))),
            ("up1", lambda: (self.phase_up(1), self.ht_close())),
            ("down1", lambda: self.phase_proj(self.AT, self.ATb, JC, self.w_down[1], 1, 1)),
            ("final", self.phase_final),
        ]
        for name, fn in phases:
            if name in os.environ.get('MK_SKIP', '').split(','):
                continue
            fn()
            self.barrier()
            if self.stop_after == name:
                if self.ht_scope is not None:
                    self.ht_close()
                break
        self.barrier()
        return nc

    def ht_open(self):
        self.ht_scope = ExitStack()
        self.HT = self.sb(self.ht_scope, "HT", [128, KC * T], BF16)
        self.HTb = [Buf(f"HT{i}") for i in range(NT)]

    def ht_close(self):
        self.barrier()
        self.ht_scope.close()
        self.ht_scope = None
        self.HT = None

    def modv(self, l, j, kc, n):
        c = ((l * 96 + j * 16 + kc) * 2 + n)
        return self.MOD.t[:, c:c + 1]

    def av(self, l, s, kc, n):
        c = (((l * 2 + s) * 16 + kc) * 2 + n)
        return self.AV.t[:, c:c + 1]

    def phase_mod(self):
        with ExitStack() as ph:
            condT = self.sb(ph, "condT", [128, 32], F32)
            scT = self.sb(ph, "scT", [128, 32], BF16)
            bmT = self.sb(ph, "bmT", [128, 192], F32)
            gT = self.sb(ph, "gT", [128, 64], F32)
            wblk = [self.sb(ph, f"wmod{i}", [128, 16 * 512], BF16) for i in range(2)]
            self.load_fm("cond", self.cond.rearrange("n (k p) -> (n k) p", p=128), 32, condT.t[:, :], condT, self.INb)
            self.op("act", lambda e: e.activation(out=scT.t[:, :], in_=condT.t[:, :], func=AF.Silu), r=[condT], w=[scT])
            for l in range(2):
                self.load_fm("bm", self.b_mod[l].rearrange("(m p) -> m p", p=128), 96, bmT.t[:, l * 96:(l + 1) * 96], bmT, self.INb)
            self.load_fm("ng", self.norm_g.rearrange("l s (k p) -> (l s k) p", p=128), 64, gT.t[:, :], gT, self.INb)
            self.load_fm("fg", self.final_g.rearrange("(k p) -> k p", p=128), 16, self.AFN.t[:, :], self.AFN, self.INb)
            i = 0
            for l in range(2):
                for blk in range(24):
                    wb = wblk[i % 2]
                    i += 1
                    self.dma("pool", wb.t[:, :].rearrange("p (k m) -> p k m", k=16),
                             self.w_mod[l][:, blk * 512:(blk + 1) * 512].rearrange("(k p) m -> p k m", p=128),
                             r=[self.INb], w=[wb])
                    ps = self.nextps()
                    for m4 in range(4):
                        for kc in range(16):
                            self.op("pe", lambda e, m4=m4, kc=kc: e.matmul(
                                ps.t[:, m4 * 2:(m4 + 1) * 2], lhsT=wb.t[:, kc * 512 + m4 * 128: kc * 512 + (m4 + 1) * 128],
                                rhs=scT.t[:, kc:32:16], start=(kc == 0), stop=(kc == 15)),
                                r=[wb, scT], w=[ps], sig=(kc == 15))
                    for n in range(2):
                        base = (l * 96 + blk * 4) * 2 + n
                        self.op("dve", lambda e, n=n, base=base: e.tensor_tensor(
                            out=self.MOD.t[:, base:base + 7:2], in0=ps.t[:, n:8:2],
                            in1=bmT.t[:, l * 96 + blk * 4: l * 96 + blk * 4 + 4], op=ALU.add),
                            r=[ps, bmT], w=[self.MOD])
            for l in range(2):
                for s in range(2):
                    for n in range(2):
                        j = 1 + 3 * s
                        mb = (l * 96 + j * 16) * 2 + n
                        ab = ((l * 2 + s) * 16) * 2 + n
                        self.op("dve", lambda e, mb=mb, ab=ab, l=l, s=s: e.scalar_tensor_tensor(
                            out=self.AV.t[:, ab:ab + 31:2], in0=self.MOD.t[:, mb:mb + 31:2], scalar=1.0,
                            in1=gT.t[:, (l * 2 + s) * 16:(l * 2 + s + 1) * 16], op0=ALU.add, op1=ALU.mult),
                            r=[self.MOD, gT], w=[self.AV])
            self.dbg("MOD", self.MOD, self.MOD.t[:, :], [128, 384])
            self.dbg("AV", self.AV, self.AV.t[:, :], [128, 128])
            self.barrier()

    def norm_tile(self, xt, tt, l, s, sq, rs, tmp, final_out=None):
        n = 0 if tt < 4 else 1
        tok = slice(tt * TW, (tt + 1) * TW)
        self.op("act", lambda e: e.activation(out=sq.t[:, :], in_=xt.t[:, :], func=AF.Square), r=[xt], w=[sq])
        ps = self.nextps()
        for kc in range(KC):
            self.op("pe", lambda e, kc=kc: e.matmul(ps.t[:, :], lhsT=self.onesb.t[:, :], rhs=sq.t[:, kc * TW:(kc + 1) * TW],
                                                  start=(kc == 0), stop=(kc == KC - 1)),
                    r=[sq, self.onesb], w=[ps], sig=(kc == KC - 1))
        self.op("act", lambda e: e.activation(out=rs.t[:, :], in_=ps.t[:, :], func=AF.Sqrt, scale=1.0 / D, bias=self.cst.t[:, 0:1]),
                r=[ps, self.cst], w=[rs])
        self.op("dve", lambda e: e.reciprocal(out=self.RSTD.t[:, tok], in_=rs.t[:, :]), r=[rs], w=[self.RSTD])
        for kc in range(KC):
            if final_out is None:
                tb = tmp[kc % 2]
                a = self.av(l, s, kc, n)
                b = self.modv(l, 3 * s, kc, n)
                self.op("dve", lambda e, kc=kc, tb=tb, a=a: e.scalar_tensor_tensor(
                    out=tb.t[:, :], in0=xt.t[:, kc * TW:(kc + 1) * TW], scalar=a, in1=self.RSTD.t[:, tok],
                    op0=ALU.mult, op1=ALU.mult), r=[xt, self.RSTD, self.AV], w=[tb])
                self.op("act", lambda e, kc=kc, tb=tb, b=b: e.activation(
                    out=self.HT.t[:, kc * T + tt * TW: kc * T + (tt + 1) * TW], in_=tb.t[:, :], func=AF.Identity, bias=b, scale=1.0),
                    r=[tb, self.MOD], w=[self.HTb[tt]])
            else:
                self.op("dve", lambda e, kc=kc: e.scalar_tensor_tensor(
                    out=final_out.t[:, kc * TW:(kc + 1) * TW], in0=xt.t[:, kc * TW:(kc + 1) * TW], scalar=self.AFN.t[:, kc:kc + 1],
                    in1=self.RSTD.t[:, tok], op0=ALU.mult, op1=ALU.mult), r=[xt, self.RSTD, self.AFN], w=[final_out])

    def xt_dram(self, tt, k0=0, k1=KC):
        return self.XT[k0:k1, :, tt * TW:(tt + 1) * TW].rearrange("k p t -> p k t")

    def phase_loadx(self):
        with ExitStack() as ph:
            xin = [self.sb(ph, f"xin{i}", [128, D], F32) for i in range(2)]
            xtile = [self.sb(ph, f"xtile{i}", [128, KC * TW], F32) for i in range(2)]
            self.RSTD = self.sb(ph, "RSTD", [128, T], F32)
            sq = self.sb(ph, "sq", [128, KC * TW], BF16)
            rs = self.sb(ph, "rs", [128, TW], F32)
            tmp = [self.sb(ph, f"ntmp{i}", [128, TW], F32) for i in range(2)]
            for tt in range(NT):
                xt = xtile[tt % 2]
                for b4 in range(4):
                    tb = tt * 4 + b4
                    xi = xin[tb % 2]
                    self.dma("sp", xi.t[:, :], self.x_all[tb * 128:(tb + 1) * 128, :], r=[self.INb], w=[xi])
                    for kq in range(4):
                        ps = self.nextps()
                        for k4 in range(4):
                            kc = kq * 4 + k4
                            self.op("pe", lambda e, kc=kc, k4=k4: e.transpose(
                                out=ps.t[:, k4 * 128:(k4 + 1) * 128], in_=xi.t[:, kc * 128:(kc + 1) * 128], identity=self.ident.t[:, :]),
                                r=[xi, self.ident], w=[ps], sig=(k4 == 3))
                        for k4 in range(4):
                            kc = kq * 4 + k4
                            eng = "act" if k4 % 2 == 0 else "dve"
                            if eng == "act":
                                self.op("act", lambda e, kc=kc, k4=k4: e.activation(
                                    out=xt.t[:, kc * TW + b4 * 128: kc * TW + (b4 + 1) * 128], in_=ps.t[:, k4 * 128:(k4 + 1) * 128],
                                    func=AF.Identity), r=[ps], w=[xt])
                            else:
                                self.op("dve", lambda e, kc=kc, k4=k4: e.tensor_copy(
                                    out=xt.t[:, kc * TW + b4 * 128: kc * TW + (b4 + 1) * 128], in_=ps.t[:, k4 * 128:(k4 + 1) * 128]),
                                    r=[ps], w=[xt])
                self.dma("sp", self.xt_dram(tt), xt.t[:, :].rearrange("p (k t) -> p k t", k=KC), r=[xt], w=[self.XTb[tt]])
                self.norm_tile(xt, tt, 0, 0, sq, rs, tmp)
            self.dbg("HT0", self.HTb[4], self.HT.t[:, 0:2 * T], [128, 2 * T], BF16)
            self.barrier()

    def phase_norm(self, l, s):
        with ExitStack() as ph:
            xtile = [self.sb(ph, f"xtile{i}", [128, KC * TW], F32) for i in range(2)]
            self.RSTD = self.sb(ph, "RSTD", [128, T], F32)
            sq = self.sb(ph, "sq", [128, KC * TW], BF16)
            rs = self.sb(ph, "rs", [128, TW], F32)
            tmp = [self.sb(ph, f"ntmp{i}", [128, TW], F32) for i in range(2)]
            for tt in range(NT):
                xt = xtile[tt % 2]
                self.dma("sp", xt.t[:, :].rearrange("p (k t) -> p k t", k=KC), self.xt_dram(tt), r=[self.XTb[tt]], w=[xt])
                self.norm_tile(xt, tt, l, s, sq, rs, tmp)
            if l == 1 and s == 0:
                self.dma("sp", self.RSD, self.RSTD.t[:, :], r=[self.RSTD], w=[self.RSDb])
            self.barrier()

    def phase_attn(self):
        HT = self.HT
        with ExitStack() as ph:
            Wp = [self.sb(ph, f"Wh{i}", [128, KC * 256], BF16) for i in range(3)]
            QT = self.sb(ph, "QT", [128, 2 * T], BF16)
            KT = self.sb(ph, "KT", [128, 2 * 2816], BF16)
            VS = 264
            Vx = self.sb(ph, "Vx", [128, 22 * VS], BF16)
            cosT = self.sb(ph, "cosT", [128, 2048], F32)
            sinT = self.sb(ph, "sinT", [128, 2048], F32)
            ropeT = self.sb(ph, "ropeT", [128, 128], BF16)
            xqb = [self.sb(ph, f"xqb{i}", [128, TW], BF16) for i in range(2)]
            lamb = self.sb(ph, "lamb", [128, 4], BF16)
            CKh = self.sb(ph, "CKh", [128, 2 * 256], F32)
            xq = [self.sb(ph, f"xq{i}", [128, TW], F32) for i in range(2)]
            t1 = [self.sb(ph, f"t1{i}", [128, TW], F32) for i in range(1)]
            t2 = [self.sb(ph, f"t2{i}", [128, TW], F32) for i in range(1)]
            PTb = [self.sb(ph, f"PT{i}", [128, TW], BF16) for i in range(4)]
            r12 = [self.sb(ph, f"r12{i}", [128, TW], F32) for i in range(2)]
            Dh = [self.sb(ph, f"Dh{i}", [128, TW], F32) for i in range(2)]
            o1 = self.sb(ph, "o1", [128, TW], F32)
            sqd = [self.sb(ph, f"sqd{i}", [128, TW], BF16) for i in range(2)]
            rsd = self.sb(ph, "rsd", [128, TW], F32)
            ost = [self.sb(ph, f"ost{i}", [128, TW], BF16) for i in range(2)]
            kvst = [self.sb(ph, f"kvst{i}", [128, 256], F32) for i in range(2)]
            lamt = self.sb(ph, "lamt", [128, 8], F32)
            gsub = self.sb(ph, "gsub", [128, 2], F32)

            self.dma("sp", cosT.t[:, :], self.c_cos, r=[self.INb], w=[cosT])
            self.dma("sp", sinT.t[:, :], self.c_sin, r=[self.INb], w=[sinT])
            self.dma("pool", ropeT.t[:, :], self.c_ropeT, r=[self.INb], w=[ropeT])
            for blk in range(22):
                self.op("dve", lambda e, blk=blk: e.memset(Vx.t[:, blk * VS + 256: blk * VS + 257], 1.0), w=[Vx])
            stage = int(os.environ.get('ATT_STAGE', 9))
            if stage < 1:
                return
            self.load_fm("lv", self.lam_vecs, 4, lamt.t[:, 0:4], lamt, self.INb)
            self.op("dve", lambda e: e.tensor_tensor(out=lamt.t[:, 4:6], in0=lamt.t[:, 0:4:2], in1=lamt.t[:, 1:4:2], op=ALU.mult),
                    r=[lamt], w=[lamt])
            self.op("dve", lambda e: e.tensor_copy(out=lamb.t[:, 0:2], in_=lamt.t[:, 4:6]), r=[lamt], w=[lamb])
            self.op("dve", lambda e: e.tensor_tensor(out=lamb.t[:, 2:4], in0=lamt.t[:, 4:6], in1=lamb.t[:, 0:2], op=ALU.subtract), r=[lamt, lamb], w=[lamb])
            psl = self.nextps()
            self.op("pe", lambda e: e.matmul(psl.t[:, 0:4], lhsT=self.onesb.t[:, :], rhs=lamb.t[:, 0:4], start=True, stop=True),
                    r=[lamb, self.onesb], w=[psl])
            self.op("dve", lambda e: e.tensor_copy(out=lamt.t[:, 0:4], in_=psl.t[:, 0:4]), r=[psl], w=[lamt])
            self.op("dve", lambda e: e.tensor_tensor(out=lamt.t[:, 4:6], in0=lamt.t[:, 0:2], in1=lamt.t[:, 2:4], op=ALU.add), r=[lamt], w=[lamt])
            self.op("act", lambda e: e.activation(out=lamt.t[:, 6:8], in_=lamt.t[:, 4:6], func=AF.Exp), r=[lamt], w=[lamt])
            self.op("dve", lambda e: e.tensor_tensor(out=lamt.t[:, 4:5], in0=lamt.t[:, 6:7], in1=lamt.t[:, 7:8], op=ALU.subtract),
                    r=[lamt], w=[lamt])
            self.op("dve", lambda e: e.tensor_scalar(out=lamt.t[:, 5:6], in0=lamt.t[:, 4:5], scalar1=LAM_INIT0, scalar2=1.0,
                                                     op0=ALU.add, op1=ALU.mult), r=[lamt], w=[lamt])
            LAM = lamt.t[:, 5:6]
            if stage < 2:
                return
            self.load_fm("sg", self.subln_g.rearrange("(k p) -> k p", p=128), 2, gsub.t[:, 0:2], gsub, self.INb)
            self.op("dve", lambda e: e.tensor_scalar(out=gsub.t[:, 0:2], in0=gsub.t[:, 0:2], scalar1=(1.0 - LAM_INIT0), scalar2=1.0,
                                                     op0=ALU.mult, op1=ALU.mult), r=[gsub], w=[gsub])
            if stage < 3:
                return
            kvi = 0
            osti = 0
            pti = 0
            skip = os.environ.get('ATT_SKIP', '').split(',')
            def load_head_w(hh):
                for part in range(3):
                    self.dma("pool", Wp[part].t[:, :].rearrange("p (k m) -> p k m", k=KC),
                             self.w_qkv[:, part * 2048 + hh * 256: part * 2048 + (hh + 1) * 256].rearrange("(k p) m -> p k m", p=128),
                             r=[self.INb], w=[Wp[part]])

            NHEADS = int(os.environ.get('ATT_HEADS', NH))
            load_head_w(0)
            for h in range(NHEADS):
                for b in range(2):
                    self.dma("sp", CKh.t[:, b * 256:(b + 1) * 256], self.cache_k[b * 128:(b + 1) * 128, h * 256:(h + 1) * 256],
                             r=[self.INb], w=[CKh])
                for b in range(2):
                    self.dma("pool", Vx.t[:, (16 + b) * VS: (16 + b) * VS + 256],
                             self.cache_v[b * 128:(b + 1) * 128, h * 256:(h + 1) * 256], r=[self.INb], w=[Vx])
                ri = 0
                pending = []
                for which in range(0 if 'qk' in skip else 2):
                    for c in range(2):
                        for tt in [int(x) for x in os.environ.get('ATT_TT', '0,1,2,3,4').split(',')]:
                            ps = self.nextps()
                            for kc in range(KC):
                                self.op("pe", lambda e, kc=kc: e.matmul(
                                    ps.t[:, :], lhsT=Wp[which].t[:, kc * 256 + c * 128: kc * 256 + (c + 1) * 128],
                                    rhs=HT.t[:, kc * T + tt * TW: kc * T + (tt + 1) * TW], start=(kc == 0), stop=(kc == KC - 1)),
                                    r=[Wp[which], self.HTb[tt]], w=[ps], sig=(kc == KC - 1))
                            while pending:
                                pending.pop(0)()
                            if which == 0:
                                dst = QT.t[:, c * T + tt * TW: c * T + (tt + 1) * TW]
                                dstb = QT
                            else:
                                off = tt * TW if tt < 4 else 2304
                                dst = KT.t[:, c * 2816 + off: c * 2816 + off + TW]
                                dstb = KT
                            if tt < 4 and os.environ.get('ATT_NOROPE', '0') == '0':
                                xb = xq[ri % 2]
                                a1 = t1[0]
                                a2 = t2[0]
                                ri += 1
                                self.op("act", lambda e, xb=xb: e.activation(out=xb.t[:, :], in_=ps.t[:, :], func=AF.Identity), r=[ps], w=[xb])
                                xbb = xqb[ri % 2]
                                self.op("dve", lambda e, xbb=xbb: e.tensor_copy(out=xbb.t[:, :], in_=ps.t[:, :]), r=[ps], w=[xbb])
                                def rope_tail(xb=xb, xbb=xbb, a1=a1, a2=a2, dst=dst, dstb=dstb, tt=tt):
                                    ps2 = self.nextps()
                                    self.op("pe", lambda e: e.matmul(ps2.t[:, :], lhsT=ropeT.t[:, :], rhs=xbb.t[:, :], start=True, stop=True),
                                            r=[xbb, ropeT], w=[ps2])
                                    self.op("dve", lambda e: e.tensor_tensor(out=a1.t[:, :], in0=xb.t[:, :], in1=cosT.t[:, tt * TW:(tt + 1) * TW], op=ALU.mult),
                                            r=[xb, cosT], w=[a1])
                                    self.op("dve", lambda e: e.tensor_tensor(out=a2.t[:, :], in0=ps2.t[:, :], in1=sinT.t[:, tt * TW:(tt + 1) * TW], op=ALU.mult),
                                            r=[ps2, sinT], w=[a2])
                                    self.op("pool", lambda e: e.tensor_tensor(out=dst, in0=a1.t[:, :], in1=a2.t[:, :], op=ALU.add),
                                            r=[a1, a2], w=[dstb])
                                pending.append(rope_tail)
                            else:
                                self.op("act", lambda e, dst=dst: e.activation(out=dst, in_=ps.t[:, :], func=AF.Identity), r=[ps], w=[dstb])
                while pending:
                    pending.pop(0)()
                for c in range(0 if 'ck' in skip else 2):
                    ps = self.nextps()
                    for b in range(2):
                        self.op("pe", lambda e, b=b, c=c: e.transpose(out=ps.t[:, b * 128:(b + 1) * 128],
                                                                      in_=CKh.t[:, b * 256 + c * 128: b * 256 + (c + 1) * 128], identity=self.ident.t[:, :]),
                                r=[CKh, self.ident], w=[ps], sig=(b == 1))
                    self.op("act", lambda e, c=c, ps=ps: e.activation(out=KT.t[:, c * 2816 + 2048: c * 2816 + 2304], in_=ps.t[:, 0:256], func=AF.Identity),
                            r=[ps], w=[KT])
                for tb in range(0 if 'v' in skip else 20):
                    blk = tb if tb < 16 else tb + 2
                    ps = self.nextps()
                    for kc in range(KC):
                        self.op("pe", lambda e, kc=kc: e.matmul(
                            ps.t[:, 0:256], lhsT=HT.t[:, kc * T + tb * 128: kc * T + (tb + 1) * 128],
                            rhs=Wp[2].t[:, kc * 256: kc * 256 + 256], start=(kc == 0), stop=(kc == KC - 1)),
                            r=[Wp[2], self.HTb[tb // 4]], w=[ps], sig=(kc == KC - 1))
                    self.op("act", lambda e, blk=blk, ps=ps: e.activation(out=Vx.t[:, blk * VS: blk * VS + 256], in_=ps.t[:, 0:256], func=AF.Identity),
                            r=[ps], w=[Vx])
                    if tb >= 16:
                        seq = (tb - 16) // 2
                        row0 = ((tb - 16) % 2) * 128
                        st = kvst[kvi % 2]
                        kvi += 1
                        self.op("dve", lambda e, st=st, ps=ps: e.tensor_copy(out=st.t[:, :], in_=ps.t[:, 0:256]), r=[ps], w=[st])
                        self.dma("sp", self.new_v[seq, row0:row0 + 128, h * 256:(h + 1) * 256], st.t[:, :], r=[st], w=[self.OUTb])
                        ps = self.nextps()
                        for kc in range(KC):
                            self.op("pe", lambda e, kc=kc: e.matmul(
                                ps.t[:, 0:256], lhsT=HT.t[:, kc * T + tb * 128: kc * T + (tb + 1) * 128],
                                rhs=Wp[1].t[:, kc * 256: kc * 256 + 256], start=(kc == 0), stop=(kc == KC - 1)),
                                r=[Wp[1], self.HTb[4]], w=[ps], sig=(kc == KC - 1))
                        st = kvst[kvi % 2]
                        kvi += 1
                        self.op("dve", lambda e, st=st, ps=ps: e.tensor_copy(out=st.t[:, :], in_=ps.t[:, 0:256]), r=[ps], w=[st])
                        self.dma("sp", self.new_k[seq, row0:row0 + 128, h * 256:(h + 1) * 256], st.t[:, :], r=[st], w=[self.OUTb])
                if h + 1 < NHEADS:
                    load_head_w(h + 1)
                jobs = [(qt * TW, TW, list(range(18)), qt) for qt in range(4)]
                jobs.append((2048, 256, [18, 19], 4))
                jobs.append((2304, 256, [20, 21], 4))
                if os.environ.get('ATT_CORE', '1') == '0':
                    jobs = []
                acc_o = [[self.ps[0], self.ps[1]], [self.ps[2], self.ps[3]]]
                acc_s = [self.ps[4], self.ps[5]]
                sps = [self.ps[6], self.ps[7]]
                si = 0
                for (q0, N, blks, tt) in jobs:
                    steps = [(c, bi, blk) for c in range(2) for bi, blk in enumerate(blks)]
                    Sof = {}

                    def emit_S(k):
                        nonlocal si
                        c, bi, blk = steps[k]
                        if blk < 16:
                            koff = blk * 128
                        elif blk < 18:
                            koff = 2048 + (blk - 16) * 128
                        else:
                            koff = 2304 + (blk - 18) * 128
                        S = sps[si % 2]
                        si += 1
                        Sof[k] = S
                        self.op("pe", lambda e: e.matmul(
                            S.t[:, 0:N], lhsT=KT.t[:, c * 2816 + koff: c * 2816 + koff + 128],
                            rhs=QT.t[:, c * T + q0: c * T + q0 + N], start=True, stop=True), r=[KT, QT], w=[S])

                    emit_S(0)
                    if len(steps) > 1:
                        emit_S(1)
                    for k, (c, bi, blk) in enumerate(steps):
                        S = Sof[k]
                        P = PTb[pti % 4]
                        pti += 1
                        self.op("act", lambda e: e.activation(out=P.t[:, 0:N], in_=S.t[:, 0:N], func=AF.Exp, scale=ATT_SCALE),
                                r=[S], w=[P])
                        first = (bi == 0)
                        last = (bi == len(blks) - 1)
                        for half in range(2):
                            self.op("pe", lambda e, half=half: e.matmul(
                                acc_o[c][half].t[:, 0:N], lhsT=Vx.t[:, blk * VS + half * 128: blk * VS + (half + 1) * 128],
                                rhs=P.t[:, 0:N], start=first, stop=last), r=[Vx, P], w=[acc_o[c][half]], sig=False)
                        self.op("pe", lambda e: e.matmul(acc_s[c].t[:, 0:N], lhsT=self.onesb.t[:, :], rhs=P.t[:, 0:N],
                                                         start=first, stop=last), r=[self.onesb, P], w=[acc_s[c]] + acc_o[c])
                        if k + 2 < len(steps):
                            emit_S(k + 2)
                    self.op("dve", lambda e: e.reciprocal(out=r12[0].t[:, 0:N], in_=acc_s[0].t[:, 0:N]), r=[acc_s[0]], w=[r12[0]])
                    self.op("dve", lambda e: e.reciprocal(out=r12[1].t[:, 0:N], in_=acc_s[1].t[:, 0:N]), r=[acc_s[1]], w=[r12[1]])
                    self.op("dve", lambda e: e.tensor_scalar(out=r12[1].t[:, 0:N], in0=r12[1].t[:, 0:N], scalar1=LAM, scalar2=1.0,
                                                             op0=ALU.mult, op1=ALU.mult), r=[r12[1], lamt], w=[r12[1]])
                    for half in range(2):
                        self.op("dve", lambda e, half=half: e.tensor_tensor(out=o1.t[:, 0:N], in0=acc_o[0][half].t[:, 0:N], in1=r12[0].t[:, 0:N], op=ALU.mult),
                                r=[acc_o[0][half], r12[0]], w=[o1])
                        self.op("dve", lambda e, half=half: e.tensor_tensor(out=Dh[half].t[:, 0:N], in0=acc_o[1][half].t[:, 0:N], in1=r12[1].t[:, 0:N], op=ALU.mult),
                                r=[acc_o[1][half], r12[1]], w=[Dh[half]])
                        self.op("pool", lambda e, half=half: e.tensor_tensor(out=Dh[half].t[:, 0:N], in0=o1.t[:, 0:N], in1=Dh[half].t[:, 0:N], op=ALU.subtract),
                                r=[o1, Dh[half]], w=[Dh[half]])
                        self.op("act", lambda e, half=half: e.activation(out=sqd[half].t[:, 0:N], in_=Dh[half].t[:, 0:N], func=AF.Square),
                                r=[Dh[half]], w=[sqd[half]])
                    S = sps[si % 2]
                    si += 1
                    for half in range(2):
                        self.op("pe", lambda e, half=half, S=S: e.matmul(S.t[:, 0:N], lhsT=self.onesb.t[:, :], rhs=sqd[half].t[:, 0:N],
                                                                     start=(half == 0), stop=(half == 1)), r=[sqd[half], self.onesb], w=[S], sig=(half == 1))
                    self.op("act", lambda e, S=S: e.activation(out=rsd.t[:, 0:N], in_=S.t[:, 0:N], func=AF.Sqrt, scale=1.0 / 256, bias=self.cst.t[:, 1:2]),
                            r=[S, self.cst], w=[rsd])
                    self.op("dve", lambda e: e.reciprocal(out=rsd.t[:, 0:N], in_=rsd.t[:, 0:N]), r=[rsd], w=[rsd])
                    for half in range(2):
                        ob = ost[osti % 2]
                        osti += 1
                        self.op("dve", lambda e, half=half, ob=ob: e.scalar_tensor_tensor(
                            out=ob.t[:, 0:N], in0=Dh[half].t[:, 0:N], scalar=gsub.t[:, half:half + 1], in1=rsd.t[:, 0:N],
                            op0=ALU.mult, op1=ALU.mult), r=[Dh[half], gsub, rsd], w=[ob])
                        self.dma("sp", self.OT[h * 2 + half, :, q0:q0 + N], ob.t[:, 0:N], r=[ob], w=[self.OTb[tt]])
            self.barrier()

    def phase_proj(self, IN, INb, KCin, W, l, s):
        gj = 2 + 3 * s
        with ExitStack() as ph:
            intile = [self.sb(ph, f"pin{i}", [128, KCin * TW], BF16) for i in range(2)]
            wblk = [self.sb(ph, f"pw{i}", [128, KCin * 512], BF16) for i in range(2)]
            xpc = [self.sb(ph, f"pxp{i}", [128, 4 * TW], F32) for i in range(2)]
            wi = 0
            xi = 0
            for tt in range(NT):
                n = 0 if tt < 4 else 1
                it = intile[tt % 2]
                self.dma("sp", it.t[:, :].rearrange("p (k t) -> p k t", k=KCin),
                         IN[:, :, tt * TW:(tt + 1) * TW].rearrange("k p t -> p k t"), r=[INb[tt]], w=[it])
                for mcb in range(4):
                    wb = wblk[wi % 2]
                    wi += 1
                    self.dma("pool", wb.t[:, :].rearrange("p (k m) -> p k m", k=KCin),
                             W[:, mcb * 512:(mcb + 1) * 512].rearrange("(k p) m -> p k m", p=128), r=[self.INb], w=[wb])
                    xp = xpc[xi % 2]
                    xi += 1
                    self.dma("sp", xp.t[:, :].rearrange("p (k t) -> p k t", k=4), self.xt_dram(tt, mcb * 4, mcb * 4 + 4),
                             r=[self.XTb[tt]], w=[xp])
                    for m2 in range(4):
                        mc = mcb * 4 + m2
                        ps = self.nextps()
                        for kc in range(KCin):
                            self.op("pe", lambda e, kc=kc, m2=m2: e.matmul(
                                ps.t[:, :], lhsT=wb.t[:, kc * 512 + m2 * 128: kc * 512 + (m2 + 1) * 128],
                                rhs=it.t[:, kc * TW:(kc + 1) * TW], start=(kc == 0), stop=(kc == KCin - 1)),
                                r=[wb, it], w=[ps], sig=(kc == KCin - 1))
                        g = self.modv(l, gj, mc, n)
                        self.op("dve", lambda e, m2=m2, g=g, ps=ps: e.scalar_tensor_tensor(
                            out=xp.t[:, m2 * TW:(m2 + 1) * TW], in0=ps.t[:, :], scalar=g, in1=xp.t[:, m2 * TW:(m2 + 1) * TW],
                            op0=ALU.mult, op1=ALU.add), r=[ps, xp, self.MOD], w=[xp])
                    self.dma("sp", self.xt_dram(tt, mcb * 4, mcb * 4 + 4), xp.t[:, :].rearrange("p (k t) -> p k t", k=4),
                             r=[xp], w=[self.XTb[tt]])
            self.barrier()

    def phase_up(self, l):
        HT = self.HT
        with ExitStack() as ph:
            wg = [self.sb(ph, f"wg{i}", [128, KC * 512], BF16) for i in range(2)]
            wv = [self.sb(ph, f"wv{i}", [128, KC * 512], BF16) for i in range(2)]
            UGs = [self.sb(ph, f"UG{i}", [128, CW_TOT], F32) for i in range(1)]
            UVs = [self.sb(ph, f"UV{i}", [128, CW_TOT], F32) for i in range(1)]
            CG = self.sb(ph, "CG", [128, CW_TOT], F32)
            CV = self.sb(ph, "CV", [128, CW_TOT], F32)
            AO = [self.sb(ph, f"AO{i}", [128, CW_TOT], BF16) for i in range(2)]
            CWt = self.sb(ph, "CWt", [128, 3 * 88], F32)
            CBt = self.sb(ph, "CBt", [128, 88], F32)
            for k in range(3):
                self.load_fm("cw", self.conv_w[l, k].rearrange("(m p) -> m p", p=128), 88, CWt.t[:, k * 88:(k + 1) * 88], CWt, self.INb)
            self.load_fm("cb", self.conv_b[l].rearrange("(m p) -> m p", p=128), 88, CBt.t[:, :], CBt, self.INb)
            for U_ in UGs + UVs:
                self.op("dve", lambda e, U_=U_: e.memset(U_.t[:, :], 0.0), w=[U_])
            segs = [(0, 2048, CPAD[0]), (2048, 256, CPAD[1]), (2304, 256, CPAD[2])]
            ai = 0
            for jb in range(11):
                g = wg[jb % 2]
                v = wv[jb % 2]
                self.dma("pool", g.t[:, :].rearrange("p (k m) -> p k m", k=KC),
                         self.w_up[l][:, jb * 512:(jb + 1) * 512].rearrange("(k p) m -> p k m", p=128), r=[self.INb], w=[g])
                self.dma("pool", v.t[:, :].rearrange("p (k m) -> p k m", k=KC),
                         self.w_up[l][:, DFF + jb * 512: DFF + (jb + 1) * 512].rearrange("(k p) m -> p k m", p=128), r=[self.INb], w=[v])
                for j2 in range(4):
                    j = jb * 4 + j2
                    UG, UV = UGs[0], UVs[0]
                    for (wt, U, cidx) in ((g, UG, j), (v, UV, JC + j)):
                        for tt in range(NT):
                            ps = self.nextps()
                            for kc in range(KC):
                                self.op("pe", lambda e, kc=kc, wt=wt: e.matmul(
                                    ps.t[:, :], lhsT=wt.t[:, kc * 512 + j2 * 128: kc * 512 + (j2 + 1) * 128],
                                    rhs=HT.t[:, kc * T + tt * TW: kc * T + (tt + 1) * TW], start=(kc == 0), stop=(kc == KC - 1)),
                                    r=[wt, self.HTb[tt]], w=[ps], sig=(kc == KC - 1))
                            if tt < 4:
                                self.op("act", lambda e, U=U, ps=ps: e.activation(out=U.t[:, 1 + tt * TW: 1 + (tt + 1) * TW], in_=ps.t[:, :], func=AF.Identity),
                                        r=[ps], w=[U])
                            else:
                                self.op("act", lambda e, U=U, ps=ps: e.activation(out=U.t[:, CPAD[1]:CPAD[1] + 256], in_=ps.t[:, 0:256], func=AF.Identity),
                                        r=[ps], w=[U])
                                self.op("act", lambda e, U=U, ps=ps: e.activation(out=U.t[:, CPAD[2]:CPAD[2] + 256], in_=ps.t[:, 256:512], func=AF.Identity),
                                        r=[ps], w=[U])
                    L = CW_TOT - 2
                    for (U, C, cidx) in ((UG, CG, j), (UV, CV, JC + j)):
                        self.op("act", lambda e, U=U, C=C, cidx=cidx: e.activation(
                            out=C.t[:, 1:1 + L], in_=U.t[:, 1:1 + L], func=AF.Identity,
                            scale=CWt.t[:, 88 + cidx: 88 + cidx + 1], bias=CBt.t[:, cidx:cidx + 1]), r=[U, CWt, CBt], w=[C])
                        self.op("dve", lambda e, U=U, C=C, cidx=cidx: e.scalar_tensor_tensor(
                            out=C.t[:, 1:1 + L], in0=U.t[:, 0:L], scalar=CWt.t[:, cidx:cidx + 1], in1=C.t[:, 1:1 + L],
                            op0=ALU.mult, op1=ALU.add), r=[U, C, CWt], w=[C])
                        self.op("dve", lambda e, U=U, C=C, cidx=cidx: e.scalar_tensor_tensor(
                            out=C.t[:, 1:1 + L], in0=U.t[:, 2:2 + L], scalar=CWt.t[:, 176 + cidx: 176 + cidx + 1], in1=C.t[:, 1:1 + L],
                            op0=ALU.mult, op1=ALU.add), r=[U, C, CWt], w=[C])
                    self.op("act", lambda e: e.activation(out=CG.t[:, 1:1 + L], in_=CG.t[:, 1:1 + L], func=AF.Silu), r=[CG], w=[CG])
                    ao = AO[ai % 2]
                    ai += 1
                    self.op("dve", lambda e, ao=ao: e.tensor_tensor(out=ao.t[:, 1:1 + L], in0=CG.t[:, 1:1 + L], in1=CV.t[:, 1:1 + L], op=ALU.mult),
                            r=[CG, CV], w=[ao])
                    for (t0, ln, off) in segs:
                        self.dma("sp", self.AT[j, :, t0:t0 + ln], ao.t[:, off:off + ln], r=[ao], w=self.ATb)
            self.barrier()

    def phase_final(self):
        with ExitStack() as ph:
            xtile = [self.sb(ph, f"xtile{i}", [128, KC * TW], F32) for i in range(2)]
            yt = [self.sb(ph, f"yt{i}", [128, KC * TW], F32) for i in range(2)]
            self.RSTD = self.sb(ph, "RSTD", [128, T], F32)
            sq = self.sb(ph, "sq", [128, KC * TW], BF16)
            rs = self.sb(ph, "rs", [128, TW], F32)
            yo = [self.sb(ph, f"yo{i}", [128, D], F32) for i in range(2)]
            oi = 0
            for tt in range(NT):
                xt = xtile[tt % 2]
                y = yt[tt % 2]
                self.dma("sp", xt.t[:, :].rearrange("p (k t) -> p k t", k=KC), self.xt_dram(tt), r=[self.XTb[tt]], w=[xt])
                self.norm_tile(xt, tt, 0, 0, sq, rs, None, final_out=y)
                for b4 in range(4):
                    o = yo[oi % 2]
                    oi += 1
                    for kq in range(4):
                        ps = self.nextps()
                        for k4 in range(4):
                            kc = kq * 4 + k4
                            self.op("pe", lambda e, kc=kc, k4=k4: e.transpose(
                                out=ps.t[:, k4 * 128:(k4 + 1) * 128], in_=y.t[:, kc * TW + b4 * 128: kc * TW + (b4 + 1) * 128],
                                identity=self.ident.t[:, :]), r=[y, self.ident], w=[ps], sig=(k4 == 3))
                        if kq % 2 == 0:
                            self.op("act", lambda e, kq=kq, ps=ps: e.activation(out=o.t[:, kq * 512:(kq + 1) * 512], in_=ps.t[:, :], func=AF.Identity),
                                    r=[ps], w=[o])
                        else:
                            self.op("dve", lambda e, kq=kq, ps=ps: e.tensor_copy(out=o.t[:, kq * 512:(kq + 1) * 512], in_=ps.t[:, :]), r=[ps], w=[o])
                    tb = tt * 4 + b4
                    self.dma("sp", self.y_all[tb * 128:(tb + 1) * 128, :], o.t[:, :], r=[o], w=[self.OUTb])
            self.barrier()

    def bc(self, ap2d, n):
        a = ap2d.ap
        return bass.AP(ap2d.tensor, ap2d.offset, [list(a[0]), list(a[1]), [0, n]])

    def colbc(self, ap_col, n):
        a = ap_col.ap
        return bass.AP(ap_col.tensor, ap_col.offset, [list(a[0]), [0, n]])

    def rev(self, ap2d):
        a = ap2d.ap
        n = a[1][1]
        return bass.AP(ap2d.tensor, ap2d.offset + (n - 1) * a[1][0], [list(a[0]), [-a[1][0], n]])

    def phase_s5_tok(self):
        HT = self.HT
        TWO_PI = 2.0 * math.pi
        with ExitStack() as ph:
            def T128(name):
                return self.sb(ph, name, [128, 128], F32)
            are, aim, LS, dt, mag, ang, sinr, cosr = [T128(n) for n in ("are", "aim", "LS", "dt", "mag", "ang", "sinr", "cosr")]
            lbre, lbim, fre, fim, tA, tB, tC = [T128(n) for n in ("lbre", "lbim", "fre", "fim", "tA", "tB", "tC")]
            H0re, H0im = T128("H0re"), T128("H0im")
            ki = self.sb(ph, "ki", [128, 128], I32)
            lsr = self.sb(ph, "lsr", [128, 2], F32)
            lsx = self.sb(ph, "lsx", [128, 128], F32)
            NK = 10
            CWre = self.sb(ph, "CWre", [128, NK * 128], F32)
            CWim = self.sb(ph, "CWim", [128, NK * 128], F32)
            FSre = self.sb(ph, "FSre", [128, 256], F32)
            FSim = self.sb(ph, "FSim", [128, 256], F32)
            fso = self.sb(ph, "fso", [128, 128], F32)
            TBre = [self.sb(ph, f"TBre{i}", [128, 4 * 512], F32) for i in range(1)]
            TBim = [self.sb(ph, f"TBim{i}", [128, 4 * 512], F32) for i in range(1)]
            tt1 = self.sb(ph, "tt1", [128, 4 * 256], F32)
            tt2 = self.sb(ph, "tt2", [128, 4 * 256], F32)
            YK = self.sb(ph, "YK", [128, T], F32)
            SBre = self.sb(ph, "SBre", [128, 8 * 16], F32)
            SBim = self.sb(ph, "SBim", [128, 8 * 16], F32)
            BBre = self.sb(ph, "BBre", [128, 8 * 16], F32)
            BBim = self.sb(ph, "BBim", [128, 8 * 16], F32)
            SCre = self.sb(ph, "SCre", [32, 8 * 64], F32)
            SCim = self.sb(ph, "SCim", [32, 8 * 64], F32)
            SC2re = self.sb(ph, "SC2re", [32, 8 * 128], F32)
            SC2im = self.sb(ph, "SC2im", [32, 8 * 128], F32)
            STre = [self.sb(ph, f"STre{i}", [128, 128], F32) for i in range(4)]
            STim = [self.sb(ph, f"STim{i}", [128, 128], F32) for i in range(4)]
            WBre = [self.sb(ph, f"WBre{i}", [128, 128], BF16) for i in range(2)]
            WBim = [self.sb(ph, f"WBim{i}", [128, 128], BF16) for i in range(2)]
            WCre = [self.sb(ph, f"WCre{i}", [128, 128], BF16) for i in range(8)]
            WCin = [self.sb(ph, f"WCin{i}", [128, 128], BF16) for i in range(8)]
            m1, m2, m3, m4 = [self.sb(ph, f"m{i}", [128, TW], F32) for i in range(4)]
            zre, zim, hsre, hsim = [self.sb(ph, n, [128, TW], F32) for n in ("zre", "zim", "hsre", "hsim")]
            HBre = [self.sb(ph, f"HBre{i}", [128, TW], BF16) for i in range(2)]
            HBim = [self.sb(ph, f"HBim{i}", [128, TW], BF16) for i in range(2)]
            ini = self.sb(ph, "ini", [128, 8], F32)
            HPre = [self.sb(ph, f"HPre{i}", [128, TW], BF16) for i in range(2)]
            HPim = [self.sb(ph, f"HPim{i}", [128, TW], BF16) for i in range(2)]
            hpi = 0

            V = lambda b: b.t[:, :]
            dve = lambda fn, r, w: self.op("dve", fn, r=r, w=w)
            TT = lambda o, a, b, op, r, w, eng="dve": self.op(eng, lambda e: e.tensor_tensor(out=o, in0=a, in1=b, op=op), r=r, w=w)

            self.load_fm("are", self.ssm_a_re.rearrange("d (pr g2) p -> (d pr) (g2 p)", g2=2), 128, V(are), are, self.INb)
            self.load_fm("aim", self.ssm_a_im.rearrange("d (pr g2) p -> (d pr) (g2 p)", g2=2), 128, V(aim), aim, self.INb)
            self.load_fm("h0r", self.st_re.rearrange("d (pr g2) p -> (d pr) (g2 p)", g2=2), 128, V(H0re), H0re, self.INb)
            self.load_fm("h0i", self.st_im.rearrange("d (pr g2) p -> (d pr) (g2 p)", g2=2), 128, V(H0im), H0im, self.INb)
            self.dma("sp", lsr.t[:, :], self.ssm_log_step.rearrange("d (pr g2) -> (d pr) g2", g2=2), r=[self.INb], w=[lsr])
            for g2 in range(2):
                dve(lambda e, g2=g2: e.tensor_scalar(out=lsx.t[:, g2 * 64:(g2 + 1) * 64], in0=self.onesf.t[:, 0:64],
                                                     scalar1=lsr.t[:, g2:g2 + 1], scalar2=1.0, op0=ALU.mult, op1=ALU.mult),
                    [lsr, self.onesf], [lsx])
            ps = self.nextps()
            self.op("pe", lambda e: e.transpose(out=ps.t[:, 0:128], in_=lsx.t[:, :], identity=self.ident.t[:, :]), r=[lsx, self.ident], w=[ps])
            self.op("act", lambda e: e.activation(out=V(dt), in_=ps.t[:, 0:128], func=AF.Exp), r=[ps], w=[dt])
            TT(V(tA), V(are), V(dt), ALU.mult, [are, dt], [tA])
            self.op("act", lambda e: e.activation(out=V(mag), in_=V(tA), func=AF.Exp), r=[tA], w=[mag])
            TT(V(ang), V(aim), V(dt), ALU.mult, [aim, dt], [ang])
            dve(lambda e: e.tensor_scalar(out=V(tB), in0=V(ang), scalar1=1.0 / TWO_PI, scalar2=1.0, op0=ALU.mult, op1=ALU.mult), [ang], [tB])
            dve(lambda e: e.tensor_copy(out=ki.t[:, :], in_=V(tB)), [tB], [ki])
            dve(lambda e: e.tensor_copy(out=V(tB), in_=ki.t[:, :]), [ki], [tB])
            dve(lambda e: e.scalar_tensor_tensor(out=V(tC), in0=V(tB), scalar=-TWO_PI, in1=V(ang), op0=ALU.mult, op1=ALU.add), [tB, ang], [tC])
            dve(lambda e: e.tensor_scalar(out=V(tC), in0=V(tC), scalar1=3.141592, scalar2=-3.141592, op0=ALU.min, op1=ALU.max), [tC], [tC])
            self.op("act", lambda e: e.activation(out=V(sinr), in_=V(tC), func=AF.Sin), r=[tC], w=[sinr])
            self.op("act", lambda e: e.activation(out=V(tA), in_=V(tC), func=AF.Sin, scale=0.5), r=[tC], w=[tA])
            TT(V(tA), V(tA), V(tA), ALU.mult, [tA], [tA])
            dve(lambda e: e.tensor_scalar(out=V(cosr), in0=V(tA), scalar1=-2.0, scalar2=1.0, op0=ALU.mult, op1=ALU.add), [tA], [cosr])
            TT(V(lbre), V(mag), V(cosr), ALU.mult, [mag, cosr], [lbre])
            TT(V(lbim), V(mag), V(sinr), ALU.mult, [mag, sinr], [lbim])
            dve(lambda e: e.tensor_scalar(out=V(tA), in0=V(lbre), scalar1=-1.0, scalar2=1.0, op0=ALU.add, op1=ALU.mult), [lbre], [tA])
            TT(V(tB), V(are), V(are), ALU.mult, [are], [tB])
            TT(V(tC), V(aim), V(aim), ALU.mult, [aim], [tC])
            TT(V(tB), V(tB), V(tC), ALU.add, [tB, tC], [tB])
            dve(lambda e: e.reciprocal(out=V(tB), in_=V(tB)), [tB], [tB])
            TT(V(fre), V(tA), V(are), ALU.mult, [tA, are], [fre])
            TT(V(tC), V(lbim), V(aim), ALU.mult, [lbim, aim], [tC])
            TT(V(fre), V(fre), V(tC), ALU.add, [fre, tC], [fre])
            TT(V(fre), V(fre), V(tB), ALU.mult, [fre, tB], [fre])
            TT(V(fim), V(lbim), V(are), ALU.mult, [lbim, are], [fim])
            TT(V(tC), V(tA), V(aim), ALU.mult, [tA, aim], [tC])
            TT(V(fim), V(fim), V(tC), ALU.subtract, [fim, tC], [fim])
            TT(V(fim), V(fim), V(tB), ALU.mult, [fim, tB], [fim])
            dve(lambda e: e.tensor_copy(out=CWre.t[:, 0:128], in_=V(cosr)), [cosr], [CWre])
            dve(lambda e: e.tensor_scalar(out=CWim.t[:, 0:128], in0=V(sinr), scalar1=-1.0, scalar2=1.0, op0=ALU.mult, op1=ALU.mult), [sinr], [CWim])
            for k in range(NK - 1):
                a = CWre.t[:, k * 128:(k + 1) * 128]
                b = CWim.t[:, k * 128:(k + 1) * 128]
                TT(V(tA), a, a, ALU.mult, [CWre], [tA])
                TT(V(tB), b, b, ALU.mult, [CWim], [tB])
                TT(CWre.t[:, (k + 1) * 128:(k + 2) * 128], V(tA), V(tB), ALU.subtract, [tA, tB], [CWre])
                TT(V(tC), a, b, ALU.mult, [CWre, CWim], [tC])
                dve(lambda e, k=k: e.tensor_scalar(out=CWim.t[:, (k + 1) * 128:(k + 2) * 128], in0=V(tC), scalar1=2.0, scalar2=1.0,
                                                   op0=ALU.mult, op1=ALU.mult), [tC], [CWim])
            for st in STre + STim:
                dve(lambda e, st=st: e.memset(st.t[:, :], 0.0), [], [st])
            for wc in WCre + WCin:
                dve(lambda e, wc=wc: e.memset(wc.t[:, :], 0.0), [], [wc])

            seq_chunks = {0: [(0, 512), (512, 512), (1024, 512), (1536, 512)], 1: [(2048, 256)], 2: [(2304, 256)]}
            wbi = 0
            hbi = 0
            tbi = 0
            psy = [self.ps[i] for i in range(5)]
            psb = [self.ps[5], self.ps[6]]
            pst = self.ps[7]
            for kc in range(KC):
                for (SB_, src) in ((SBre, self.ssm_b_re), (SBim, self.ssm_b_im)):
                    for d in range(2):
                        self.dma("sp", SB_.t[:, d * 64:(d + 1) * 64].rearrange("p (c k) -> p c k", k=16),
                                 src[d, 8 * kc:8 * kc + 8].rearrange("(pq g2) p ci -> (g2 p) pq ci", g2=2), r=[self.INb], w=[SB_],
                                 allow_slow_non_contiguous=False)
                for (SC_, src) in ((SCre, self.ssm_c_re), (SCim, self.ssm_c_im)):
                    for d in range(2):
                        self.dma("sp", SC_.t[:, d * 256:(d + 1) * 256].rearrange("p (c k) -> p c k", k=64),
                                 src[d, 8 * kc:8 * kc + 8].rearrange("(pq g2) co p -> (g2 co) pq p", g2=2), r=[self.INb], w=[SC_])
                for (SC_, SC2_) in ((SCre, SC2re), (SCim, SC2im)):
                    for rep in range(2):
                        dve(lambda e, rep=rep, SC_=SC_, SC2_=SC2_: e.tensor_copy(
                            out=SC2_.t[:, :].rearrange("p (c r k) -> p c r k", r=2, k=64)[:, :, rep, :],
                            in_=SC_.t[:, :].rearrange("p (c k) -> p c k", k=64)), [SC_], [SC2_])
                for d in range(2):
                    c0 = d * 64 + 4 * kc
                    fr = self.bc(fre.t[:, c0:c0 + 4], 16)
                    fi = self.bc(fim.t[:, c0:c0 + 4], 16)
                    sl = slice(d * 64, (d + 1) * 64)
                    v3 = lambda b: b.t[:, sl].rearrange("p (c k) -> p c k", k=16)
                    t3 = tt1.t[:, 0:64].rearrange("p (c k) -> p c k", k=16)
                    TT(v3(BBre), v3(SBre), fr, ALU.mult, [SBre, fre], [BBre])
                    TT(t3, v3(SBim), fi, ALU.mult, [SBim, fim], [tt1])
                    TT(v3(BBre), v3(BBre), t3, ALU.subtract, [BBre, tt1], [BBre])
                    TT(v3(BBim), v3(SBre), fi, ALU.mult, [SBre, fim], [BBim])
                    TT(t3, v3(SBim), fr, ALU.mult, [SBim, fre], [tt1])
                    TT(v3(BBim), v3(BBim), t3, ALU.add, [BBim, tt1], [BBim])
                for d in range(2):
                    tre = TBre[0]
                    tim = TBim[0]
                    tbi += 1
                    c0 = d * 64 + 4 * kc
                    tre3 = tre.t[:, :].rearrange("p (c j) -> p c j", j=512)
                    tim3 = tim.t[:, :].rearrange("p (c j) -> p c j", j=512)
                    dve(lambda e: e.memset(tre3[:, :, 0:1], 1.0), [], [tre])
                    dve(lambda e: e.memset(tim3[:, :, 0:1], 0.0), [], [tim])
                    for k in range(9):
                        s = 1 << k
                        wr = self.bc(CWre.t[:, k * 128 + c0: k * 128 + c0 + 4], s)
                        wi = self.bc(CWim.t[:, k * 128 + c0: k * 128 + c0 + 4], s)
                        a = tre3[:, :, 0:s]
                        b = tim3[:, :, 0:s]
                        u1 = tt1.t[:, 0:4 * s].rearrange("p (c j) -> p c j", j=s)
                        u2 = tt2.t[:, 0:4 * s].rearrange("p (c j) -> p c j", j=s)
                        TT(u1, a, wr, ALU.mult, [tre, CWre], [tt1])
                        TT(u2, b, wi, ALU.mult, [tim, CWim], [tt2])
                        TT(tre3[:, :, s:2 * s], u1, u2, ALU.subtract, [tt1, tt2], [tre])
                        TT(u1, a, wi, ALU.mult, [tre, CWim], [tt1])
                        TT(u2, b, wr, ALU.mult, [tim, CWre], [tt2])
                        TT(tim3[:, :, s:2 * s], u1, u2, ALU.add, [tt1, tt2], [tim])
                    for pq in range(4):
                        col = c0 + pq
                        cc = d * 4 + pq
                        for (ST_, BB_, WB_) in ((STre[pq], BBre, WBre[wbi % 2]), (STim[pq], BBim, WBim[wbi % 2])):
                            for g2 in range(2):
                                dve(lambda e, g2=g2, ST_=ST_, BB_=BB_: e.tensor_copy(
                                    out=ST_.t[g2 * 64:(g2 + 1) * 64, 32 * pq + 16 * g2: 32 * pq + 16 * g2 + 16],
                                    in_=BB_.t[g2 * 64:(g2 + 1) * 64, cc * 16:(cc + 1) * 16]), [BB_], [ST_])
                            self.op("pe", lambda e, ST_=ST_: e.transpose(out=pst.t[:, 0:128], in_=ST_.t[:, :], identity=self.ident.t[:, :]),
                                    r=[ST_, self.ident], w=[pst])
                            self.op("act", lambda e, WB_=WB_: e.activation(out=WB_.t[:, :], in_=pst.t[:, 0:128], func=AF.Identity), r=[pst], w=[WB_])
                        wbre, wbim = WBre[wbi % 2], WBim[wbi % 2]
                        wbi += 1
                        wcre, wcin = WCre[cc], WCin[cc]
                        for (SC_, WC_, sgn) in ((SC2re, wcre, 1.0), (SC2im, wcin, -1.0)):
                            dup = SC_.t[:, cc * 128:(cc + 1) * 128]
                            self.op("pe", lambda e, dup=dup: e.transpose(out=pst.t[:, 0:32], in_=dup, identity=self.ident.t[0:32, 0:32]),
                                    r=[SC_, self.ident], w=[pst])
                            for g2 in range(2):
                                self.op("act", lambda e, g2=g2, WC_=WC_, sgn=sgn: e.activation(
                                    out=WC_.t[g2 * 64:(g2 + 1) * 64, 32 * pq + 16 * g2: 32 * pq + 16 * g2 + 16],
                                    in_=pst.t[g2 * 64:(g2 + 1) * 64, 16 * g2: 16 * g2 + 16], func=AF.Identity, scale=sgn), r=[pst], w=[WC_])
                        rho = mag.t[:, col:col + 1]
                        wre_c = cosr.t[:, col:col + 1]
                        wim_c = sinr.t[:, col:col + 1]
                        for sq_ in range(3):
                            chunks = seq_chunks[sq_] if d == 0 else list(reversed(seq_chunks[sq_]))
                            for ci_, (t0, N) in enumerate(chunks):
                                tile_i = t0 // TW
                                off = t0 - tile_i * TW
                                Tr = tre.t[:, pq * 512: pq * 512 + N]
                                Ti = tim.t[:, pq * 512: pq * 512 + N]
                                self.op("pe", lambda e: e.matmul(psb[0].t[:, 0:N], lhsT=wbre.t[:, :], rhs=HT.t[:, kc * T + t0: kc * T + t0 + N],
                                                                 start=True, stop=True), r=[wbre, self.HTb[tile_i]], w=[psb[0]])
                                self.op("pe", lambda e: e.matmul(psb[1].t[:, 0:N], lhsT=wbim.t[:, :], rhs=HT.t[:, kc * T + t0: kc * T + t0 + N],
                                                                 start=True, stop=True), r=[wbim, self.HTb[tile_i]], w=[psb[1]])
                                bre = psb[0].t[:, 0:N] if d == 0 else self.rev(psb[0].t[:, 0:N])
                                bim = psb[1].t[:, 0:N] if d == 0 else self.rev(psb[1].t[:, 0:N])
                                TT(m1.t[:, 0:N], bre, Tr, ALU.mult, [psb[0], tre], [m1])
                                TT(m4.t[:, 0:N], bre, Ti, ALU.mult, [psb[0], tim], [m4])
                                TT(m2.t[:, 0:N], bim, Ti, ALU.mult, [psb[1], tim], [m2])
                                TT(m3.t[:, 0:N], bim, Tr, ALU.mult, [psb[1], tre], [m3])
                                TT(zre.t[:, 0:N], m1.t[:, 0:N], m2.t[:, 0:N], ALU.subtract, [m1, m2], [zre], eng="pool")
                                TT(zim.t[:, 0:N], m3.t[:, 0:N], m4.t[:, 0:N], ALU.add, [m3, m4], [zim], eng="pool")
                                if ci_ == 0:
                                    if sq_ == 0:
                                        h0r = H0re.t[:, col:col + 1]
                                        h0i = H0im.t[:, col:col + 1]
                                        TT(ini.t[:, 2:3], h0r, wre_c, ALU.mult, [H0re, cosr], [ini])
                                        TT(ini.t[:, 3:4], h0i, wim_c, ALU.mult, [H0im, sinr], [ini])
                                        TT(ini.t[:, 0:1], ini.t[:, 2:3], ini.t[:, 3:4], ALU.subtract, [ini], [ini])
                                        TT(ini.t[:, 2:3], h0r, wim_c, ALU.mult, [H0re, sinr], [ini])
                                        TT(ini.t[:, 3:4], h0i, wre_c, ALU.mult, [H0im, cosr], [ini])
                                        TT(ini.t[:, 1:2], ini.t[:, 2:3], ini.t[:, 3:4], ALU.add, [ini], [ini])
                                    else:
                                        dve(lambda e: e.memset(ini.t[:, 0:2], 0.0), [], [ini])
                                else:
                                    c9r = CWre.t[:, 9 * 128 + col: 9 * 128 + col + 1]
                                    c9i = CWim.t[:, 9 * 128 + col: 9 * 128 + col + 1]
                                    lr = hsre.t[:, 511:512]
                                    li = hsim.t[:, 511:512]
                                    TT(ini.t[:, 2:3], lr, c9r, ALU.mult, [hsre, CWre], [ini])
                                    TT(ini.t[:, 3:4], li, c9i, ALU.mult, [hsim, CWim], [ini])
                                    TT(ini.t[:, 4:5], li, c9r, ALU.mult, [hsim, CWre], [ini])
                                    TT(ini.t[:, 5:6], lr, c9i, ALU.mult, [hsre, CWim], [ini])
                                    TT(ini.t[:, 0:1], ini.t[:, 2:3], ini.t[:, 3:4], ALU.add, [ini], [ini])
                                    TT(ini.t[:, 1:2], ini.t[:, 4:5], ini.t[:, 5:6], ALU.subtract, [ini], [ini])
                                dve(lambda e: e.tensor_tensor_scan(out=hsre.t[:, 0:N], data0=self.colbc(rho, N), data1=zre.t[:, 0:N],
                                                                   initial=ini.t[:, 0:1], op0=ALU.mult, op1=ALU.add), [mag, zre, ini], [hsre])
                                dve(lambda e: e.tensor_tensor_scan(out=hsim.t[:, 0:N], data0=self.colbc(rho, N), data1=zim.t[:, 0:N],
                                                                   initial=ini.t[:, 1:2], op0=ALU.mult, op1=ALU.add), [mag, zim, ini], [hsim])
                                if sq_ == 0:
                                    hbre, hbim = HBre[hbi % 2], HBim[hbi % 2]
                                    hbi += 1
                                    ho = 0
                                else:
                                    hbre, hbim = HPre[hpi % 2], HPim[hpi % 2]
                                    ho = (sq_ - 1) * 256
                                    if sq_ == 2:
                                        hpi += 1
                                TT(m1.t[:, 0:N], hsre.t[:, 0:N], Tr, ALU.mult, [hsre, tre], [m1])
                                TT(m2.t[:, 0:N], hsim.t[:, 0:N], Ti, ALU.mult, [hsim, tim], [m2])
                                TT(m3.t[:, 0:N], hsim.t[:, 0:N], Tr, ALU.mult, [hsim, tre], [m3])
                                TT(m4.t[:, 0:N], hsre.t[:, 0:N], Ti, ALU.mult, [hsre, tim], [m4])
                                ore = hbre.t[:, ho:ho + N] if d == 0 else self.rev(hbre.t[:, ho:ho + N])
                                oim = hbim.t[:, ho:ho + N] if d == 0 else self.rev(hbim.t[:, ho:ho + N])
                                TT(ore, m1.t[:, 0:N], m2.t[:, 0:N], ALU.add, [m1, m2], [hbre], eng="pool")
                                TT(oim, m3.t[:, 0:N], m4.t[:, 0:N], ALU.subtract, [m3, m4], [hbim], eng="pool")
                                if sq_ > 0:
                                    fcol = ((sq_ - 1) * 2 + d) * 64 + (4 * kc + pq)
                                    TT(FSre.t[:, fcol:fcol + 1], m1.t[:, N - 1:N], m2.t[:, N - 1:N], ALU.add, [m1, m2], [FSre])
                                    TT(FSim.t[:, fcol:fcol + 1], m3.t[:, N - 1:N], m4.t[:, N - 1:N], ALU.subtract, [m3, m4], [FSim])
                                if sq_ == 1:
                                    continue
                                NN = N if sq_ == 0 else 512
                                yv = psy[tile_i].t[:, 0:NN]
                                self.op("pe", lambda e: e.matmul(yv, lhsT=wcre.t[:, :], rhs=hbre.t[:, 0:NN], start=(pq == 0), stop=False),
                                        r=[wcre, hbre], w=[psy[tile_i]], sig=False)
                                self.op("pe", lambda e: e.matmul(yv, lhsT=wcin.t[:, :], rhs=hbim.t[:, 0:NN], start=False, stop=(pq == 3)),
                                        r=[wcin, hbim], w=[psy[tile_i]])
                    for ti in range(NT):
                        if d == 0:
                            self.op("act", lambda e, ti=ti: e.activation(out=YK.t[:, ti * TW:(ti + 1) * TW], in_=psy[ti].t[:, :], func=AF.Identity),
                                    r=[psy[ti]], w=[YK])
                        else:
                            TT(YK.t[:, ti * TW:(ti + 1) * TW], psy[ti].t[:, :], YK.t[:, ti * TW:(ti + 1) * TW], ALU.add, [psy[ti], YK], [YK])
                self.dma("sp", self.YT[kc], YK.t[:, :], r=[YK], w=self.YTb)
            for (FS, dst) in ((FSre, self.new_sre), (FSim, self.new_sim)):
                for hlf in range(2):
                    self.op("pe", lambda e, hlf=hlf, FS=FS: e.transpose(out=pst.t[:, 0:128], in_=FS.t[:, hlf * 128:(hlf + 1) * 128], identity=self.ident.t[:, :]),
                            r=[FS, self.ident], w=[pst])
                    self.op("act", lambda e: e.activation(out=fso.t[:, :], in_=pst.t[:, 0:128], func=AF.Identity), r=[pst], w=[fso])
                    self.dma("sp", dst[hlf * 128:(hlf + 1) * 128, :], fso.t[:, :], r=[fso], w=[self.OUTb])
            self.barrier()

    def ap4(self, buf, col0, dims, rows=None):
        base = buf.t[:, col0:col0 + 1] if rows is None else buf.t[rows[0]:rows[1], col0:col0 + 1]
        return bass.AP(base.tensor, base.offset, [list(base.ap[0])] + [list(d) for d in dims])

    def phase_s5(self):
        HT = self.HT
        TWO_PI = 2.0 * math.pi
        NCH = 320
        SEQC = [(0, 256), (256, 32), (288, 32)]
        HBASE = [0, 257, 290]
        HTOT = 323
        with ExitStack() as ph:
            def T128(name, st=ph):
                return self.sb(st, name, [128, 128], F32)
            lbre, lbim, fre, fim = [T128(n) for n in ("lbre", "lbim", "fre", "fim")]
            H0re, H0im, ilre, ilim, mag8 = [T128(n) for n in ("H0re", "H0im", "ilre", "ilim", "mag8")]
            maskF, maskB = T128("maskF"), T128("maskB")
            I8re, I8im = T128("I8re"), T128("I8im")
            H0bre = self.sb(ph, "H0bre", [128, 128], BF16)
            H0bim = self.sb(ph, "H0bim", [128, 128], BF16)
            RHm = [self.sb(ph, f"RHm{i}", [128, NCH], F32) for i in range(2)]
            NK = 11
            CWre = self.sb(ph, "CWre", [128, NK * 128], F32)
            CWim = self.sb(ph, "CWim", [128, NK * 128], F32)
            FSre = self.sb(ph, "FSre", [128, 256], F32)
            FSim = self.sb(ph, "FSim", [128, 256], F32)
            fso = self.sb(ph, "fso", [128, 128], F32)
            Sel = self.sb(ph, "Sel", [128, 8 * 240], BF16)
            selw = lambda a, b: Sel.t[:, a * 240 + 112 - 16 * b: a * 240 + 112 - 16 * b + 128]
            YK = self.sb(ph, "YK", [128, T], F32)
            pre = ExitStack()
            are, aim, dt, mag, ang, sinr, cosr, tA, tB, tC = [T128(n, pre) for n in ("are", "aim", "dt", "mag", "ang", "sinr", "cosr", "tA", "tB", "tC")]
            ki = self.sb(pre, "ki", [128, 128], I32)
            lsr = self.sb(pre, "lsr", [128, 2], F32)
            lsx = self.sb(pre, "lsx", [128, 128], F32)
            V = lambda b: b.t[:, :]
            dve = lambda fn, r, w: self.op("dve", fn, r=r, w=w)
            TT = lambda o, a, b, op, r, w, eng="dve": self.op(eng, lambda e: e.tensor_tensor(out=o, in0=a, in1=b, op=op), r=r, w=w)

            self.dma("pool", Sel.t[:, :], self.c_sel, r=[self.INb], w=[Sel])
            self.dma("sp", maskF.t[:, :], self.c_maskf, r=[self.INb], w=[maskF])
            self.dma("sp", maskB.t[:, :], self.c_maskb, r=[self.INb], w=[maskB])
            self.load_fm("are", self.ssm_a_re.rearrange("d (pr g2) p -> (d pr) (g2 p)", g2=2), 128, V(are), are, self.INb)
            self.load_fm("aim", self.ssm_a_im.rearrange("d (pr g2) p -> (d pr) (g2 p)", g2=2), 128, V(aim), aim, self.INb)
            self.load_fm("h0r", self.st_re.rearrange("d (pr g2) p -> (d pr) (g2 p)", g2=2), 128, V(H0re), H0re, self.INb)
            self.load_fm("h0i", self.st_im.rearrange("d (pr g2) p -> (d pr) (g2 p)", g2=2), 128, V(H0im), H0im, self.INb)
            self.dma("sp", lsr.t[:, :], self.ssm_log_step.rearrange("d (pr g2) -> (d pr) g2", g2=2), r=[self.INb], w=[lsr])
            for g2 in range(2):
                dve(lambda e, g2=g2: e.tensor_scalar(out=lsx.t[:, g2 * 64:(g2 + 1) * 64], in0=self.onesf.t[:, 0:64],
                                                     scalar1=lsr.t[:, g2:g2 + 1], scalar2=1.0, op0=ALU.mult, op1=ALU.mult),
                    [lsr, self.onesf], [lsx])
            ps = self.nextps()
            self.op("pe", lambda e: e.transpose(out=ps.t[:, 0:128], in_=lsx.t[:, :], identity=self.ident.t[:, :]), r=[lsx, self.ident], w=[ps])
            self.op("act", lambda e: e.activation(out=V(dt), in_=ps.t[:, 0:128], func=AF.Exp), r=[ps], w=[dt])
            TT(V(tA), V(are), V(dt), ALU.mult, [are, dt], [tA])
            self.op("act", lambda e: e.activation(out=V(mag), in_=V(tA), func=AF.Exp), r=[tA], w=[mag])
            TT(V(ang), V(aim), V(dt), ALU.mult, [aim, dt], [ang])
            dve(lambda e: e.tensor_scalar(out=V(tB), in0=V(ang), scalar1=1.0 / TWO_PI, scalar2=1.0, op0=ALU.mult, op1=ALU.mult), [ang], [tB])
            dve(lambda e: e.tensor_copy(out=ki.t[:, :], in_=V(tB)), [tB], [ki])
            dve(lambda e: e.tensor_copy(out=V(tB), in_=ki.t[:, :]), [ki], [tB])
            dve(lambda e: e.scalar_tensor_tensor(out=V(tC), in0=V(tB), scalar=-TWO_PI, in1=V(ang), op0=ALU.mult, op1=ALU.add), [tB, ang], [tC])
            dve(lambda e: e.tensor_scalar(out=V(tC), in0=V(tC), scalar1=3.141592, scalar2=-3.141592, op0=ALU.min, op1=ALU.max), [tC], [tC])
            self.op("act", lambda e: e.activation(out=V(sinr), in_=V(tC), func=AF.Sin), r=[tC], w=[sinr])
            self.op("act", lambda e: e.activation(out=V(tA), in_=V(tC), func=AF.Sin, scale=0.5), r=[tC], w=[tA])
            TT(V(tA), V(tA), V(tA), ALU.mult, [tA], [tA])
            dve(lambda e: e.tensor_scalar(out=V(cosr), in0=V(tA), scalar1=-2.0, scalar2=1.0, op0=ALU.mult, op1=ALU.add), [tA], [cosr])
            TT(V(lbre), V(mag), V(cosr), ALU.mult, [mag, cosr], [lbre])
            TT(V(lbim), V(mag), V(sinr), ALU.mult, [mag, sinr], [lbim])
            dve(lambda e: e.tensor_scalar(out=V(tA), in0=V(lbre), scalar1=-1.0, scalar2=1.0, op0=ALU.add, op1=ALU.mult), [lbre], [tA])
            TT(V(tB), V(are), V(are), ALU.mult, [are], [tB])
            TT(V(tC), V(aim), V(aim), ALU.mult, [aim], [tC])
            TT(V(tB), V(tB), V(tC), ALU.add, [tB, tC], [tB])
            dve(lambda e: e.reciprocal(out=V(tB), in_=V(tB)), [tB], [tB])
            TT(V(fre), V(tA), V(are), ALU.mult, [tA, are], [fre])
            TT(V(tC), V(lbim), V(aim), ALU.mult, [lbim, aim], [tC])
            TT(V(fre), V(fre), V(tC), ALU.add, [fre, tC], [fre])
            TT(V(fre), V(fre), V(tB), ALU.mult, [fre, tB], [fre])
            TT(V(fim), V(lbim), V(are), ALU.mult, [lbim, are], [fim])
            TT(V(tC), V(tA), V(aim), ALU.mult, [tA, aim], [tC])
            TT(V(fim), V(fim), V(tC), ALU.subtract, [fim, tC], [fim])
            TT(V(fim), V(fim), V(tB), ALU.mult, [fim, tB], [fim])
            TT(V(tA), V(mag), V(mag), ALU.mult, [mag], [tA])
            dve(lambda e: e.reciprocal(out=V(tB), in_=V(tA)), [tA], [tB])
            TT(V(ilre), V(lbre), V(tB), ALU.mult, [lbre, tB], [ilre])
            TT(V(ilim), V(lbim), V(tB), ALU.mult, [lbim, tB], [ilim])
            dve(lambda e: e.tensor_scalar(out=V(ilim), in0=V(ilim), scalar1=-1.0, scalar2=1.0, op0=ALU.mult, op1=ALU.mult), [ilim], [ilim])
            TT(V(tC), V(tA), V(tA), ALU.mult, [tA], [tC])
            TT(V(mag8), V(tC), V(tC), ALU.mult, [tC], [mag8])
            dve(lambda e: e.tensor_copy(out=CWre.t[:, 0:128], in_=V(cosr)), [cosr], [CWre])
            dve(lambda e: e.tensor_scalar(out=CWim.t[:, 0:128], in0=V(sinr), scalar1=-1.0, scalar2=1.0, op0=ALU.mult, op1=ALU.mult), [sinr], [CWim])
            for k in range(NK - 1):
                a = CWre.t[:, k * 128:(k + 1) * 128]
                b = CWim.t[:, k * 128:(k + 1) * 128]
                TT(V(tA), a, a, ALU.mult, [CWre], [tA])
                TT(V(tB), b, b, ALU.mult, [CWim], [tB])
                TT(CWre.t[:, (k + 1) * 128:(k + 2) * 128], V(tA), V(tB), ALU.subtract, [tA, tB], [CWre])
                TT(V(tC), a, b, ALU.mult, [CWre, CWim], [tC])
                dve(lambda e, k=k: e.tensor_scalar(out=CWim.t[:, (k + 1) * 128:(k + 2) * 128], in0=V(tC), scalar1=2.0, scalar2=1.0,
                                                   op0=ALU.mult, op1=ALU.mult), [tC], [CWim])
            c3r_, c3i_ = CWre.t[:, 3 * 128:4 * 128], CWim.t[:, 3 * 128:4 * 128]
            TT(V(tA), V(H0re), c3r_, ALU.mult, [H0re, CWre], [tA])
            TT(V(tB), V(H0im), c3i_, ALU.mult, [H0im, CWim], [tB])
            TT(V(tA), V(tA), V(tB), ALU.add, [tA, tB], [tA])
            TT(V(I8re), V(tA), V(mag8), ALU.mult, [tA, mag8], [I8re])
            TT(V(tA), V(H0im), c3r_, ALU.mult, [H0im, CWre], [tA])
            TT(V(tB), V(H0re), c3i_, ALU.mult, [H0re, CWim], [tB])
            TT(V(tA), V(tA), V(tB), ALU.subtract, [tA, tB], [tA])
            TT(V(I8im), V(tA), V(mag8), ALU.mult, [tA, mag8], [I8im])
            dve(lambda e: e.tensor_copy(out=H0bre.t[:, :], in_=V(H0re)), [H0re], [H0bre])
            dve(lambda e: e.tensor_copy(out=H0bim.t[:, :], in_=V(H0im)), [H0im], [H0bim])
            for i_ in range(2):
                dve(lambda e, i_=i_: e.memset(RHm[i_].t[:, :], 1.0), [], [RHm[i_]])
            for pos in (256, 288):
                dve(lambda e, pos=pos: e.memset(RHm[0].t[:, pos:pos + 1], 0.0), [], [RHm[0]])
            for pos in (32, 64):
                dve(lambda e, pos=pos: e.memset(RHm[1].t[:, pos:pos + 1], 0.0), [], [RHm[1]])
            self.barrier()
            pre.close()
            TBre = self.sb(ph, "TBre", [128, 4 * NCH], F32)
            TBim = self.sb(ph, "TBim", [128, 4 * NCH], F32)
            tt1 = self.sb(ph, "tt1", [128, 512], F32)
            tt2 = self.sb(ph, "tt2", [128, 512], F32)
            tt3 = self.sb(ph, "tt3", [128, 512], F32)
            SBre = self.sb(ph, "SBre", [128, 8 * 16], F32)
            SBim = self.sb(ph, "SBim", [128, 8 * 16], F32)
            BBre = self.sb(ph, "BBre", [128, 8 * 16], F32)
            BBim = self.sb(ph, "BBim", [128, 8 * 16], F32)
            CTre = self.sb(ph, "CTre", [128, 8 * 16], F32)
            CTim = self.sb(ph, "CTim", [128, 8 * 16], F32)
            SCx = self.sb(ph, "SCx", [32, 8 * 64], F32)
            SC2x = self.sb(ph, "SC2x", [32, 8 * 128], F32)
            LKre, LKim, ILre, ILim = [self.sb(ph, n, [128, 8], F32) for n in ("LKre", "LKim", "ILre", "ILim")]
            PWre = self.sb(ph, "PWre", [128, 9 * 8], F32)
            PWim = self.sb(ph, "PWim", [128, 9 * 8], F32)
            NPre = self.sb(ph, "NPre", [128, 8 * 8], F32)
            NPim = self.sb(ph, "NPim", [128, 8 * 8], F32)
            t8 = self.sb(ph, "t8", [128, 8], F32)
            FAre = self.sb(ph, "FAre", [128, 512], F32)
            FAim = self.sb(ph, "FAim", [128, 512], F32)
            U = [self.sb(ph, f"U{i}", [128, NCH], BF16) for i in range(8)]
            Zst = [self.sb(ph, f"Zst{i}", [128, 128], F32) for i in range(4)]
            WS = [[self.sb(ph, f"WS{s}_{i}", [128, 128], BF16) for i in range(4)] for s in range(2)]
            ZT = [[self.sb(ph, f"ZT{s}_{i}", [128, 128], BF16) for i in range(4)] for s in range(2)]
            YTn = [[self.sb(ph, f"YTn{s}_{i}", [128, 128], BF16) for i in range(2)] for s in range(2)]
            ZO = [[self.sb(ph, f"ZO{c}_{i}", [128, 128], BF16) for i in range(4)] for c in range(8)]
            HBre = [self.sb(ph, f"HBre{c}", [128, NCH], BF16) for c in range(8)]
            HBim = [self.sb(ph, f"HBim{c}", [128, NCH], BF16) for c in range(8)]
            Tacc = [self.sb(ph, f"Tacc{g}", [128, 128], F32) for g in range(8)]
            Tb = [self.sb(ph, f"Tb{g}", [128, 128], BF16) for g in range(8)]
            Yhi = [self.sb(ph, f"Yhi{g}", [128, NCH], BF16) for g in range(8)]
            Ylo = [self.sb(ph, f"Ylo{g}", [128, NCH], BF16) for g in range(8)]
            m1, m2, m3, m4 = [self.sb(ph, f"m{i}", [128, NCH], F32) for i in range(4)]
            zre, zim, hsre, hsim = [self.sb(ph, n, [128, NCH], F32) for n in ("zre", "zim", "hsre", "hsim")]
            ini = self.sb(ph, "ini", [128, 8], F32)
            for z_ in Zst:
                dve(lambda e, z_=z_: e.memset(z_.t[:, :], 0.0), [], [z_])
            for lst in (WS[0], WS[1], ZT[0], ZT[1]) + tuple(ZO):
                for z_ in lst:
                    dve(lambda e, z_=z_: e.memset(z_.t[:, :], 0.0), [], [z_])
            for hb in HBre + HBim:
                dve(lambda e, hb=hb: e.memset(hb.t[:, :], 0.0), [], [hb])

            def prep(kc):
                    for (SB_, src) in ((SBre, self.ssm_b_re), (SBim, self.ssm_b_im)):
                        for d in range(2):
                            self.dma("sp", SB_.t[:, d * 64:(d + 1) * 64].rearrange("p (c k) -> p c k", k=16),
                                     src[d, 8 * kc:8 * kc + 8].rearrange("(pq g2) p ci -> (g2 p) pq ci", g2=2), r=[self.INb], w=[SB_])
                    for (src, CT_) in ((self.ssm_c_re, CTre), (self.ssm_c_im, CTim)):
                        for d in range(2):
                            self.dma("sp", SCx.t[:, d * 256:(d + 1) * 256].rearrange("p (c k) -> p c k", k=64),
                                     src[d, 8 * kc:8 * kc + 8].rearrange("(pq g2) co p -> (g2 co) pq p", g2=2), r=[self.INb], w=[SCx])
                        for rep in range(2):
                            dve(lambda e, rep=rep: e.tensor_copy(
                                out=SC2x.t[:, :].rearrange("p (c r k) -> p c r k", r=2, k=64)[:, :, rep, :],
                                in_=SCx.t[:, :].rearrange("p (c k) -> p c k", k=64)), [SCx], [SC2x])
                        for cc in range(8):
                            pst = self.nextps()
                            self.op("pe", lambda e: e.transpose(out=pst.t[:, 0:32], in_=SC2x.t[:, cc * 128:(cc + 1) * 128], identity=self.ident.t[0:32, 0:32]),
                                    r=[SC2x, self.ident], w=[pst])
                            for g2 in range(2):
                                self.op("act", lambda e, g2=g2: e.activation(out=CT_.t[g2 * 64:(g2 + 1) * 64, cc * 16:(cc + 1) * 16],
                                                                             in_=pst.t[g2 * 64:(g2 + 1) * 64, 16 * g2:16 * g2 + 16], func=AF.Identity),
                                        r=[pst], w=[CT_])
                    for d in range(2):
                        c0 = d * 64 + 4 * kc
                        fr = self.bc(fre.t[:, c0:c0 + 4], 16)
                        fi = self.bc(fim.t[:, c0:c0 + 4], 16)
                        sl = slice(d * 64, (d + 1) * 64)
                        v3 = lambda b: b.t[:, sl].rearrange("p (c k) -> p c k", k=16)
                        t3 = tt1.t[:, 0:64].rearrange("p (c k) -> p c k", k=16)
                        TT(v3(BBre), v3(SBre), fr, ALU.mult, [SBre, fre], [BBre])
                        TT(t3, v3(SBim), fi, ALU.mult, [SBim, fim], [tt1])
                        TT(v3(BBre), v3(BBre), t3, ALU.subtract, [BBre, tt1], [BBre])
                        TT(v3(BBim), v3(SBre), fi, ALU.mult, [SBre, fim], [BBim])
                        TT(t3, v3(SBim), fr, ALU.mult, [SBim, fre], [tt1])
                        TT(v3(BBim), v3(BBim), t3, ALU.add, [BBim, tt1], [BBim])
                    for d in range(2):
                        c0 = d * 64 + 4 * kc
                        for (dst, src) in ((LKre, lbre), (LKim, lbim), (ILre, ilre), (ILim, ilim)):
                            dve(lambda e, dst=dst, src=src: e.tensor_copy(out=dst.t[:, d * 4:(d + 1) * 4], in_=src.t[:, c0:c0 + 4]), [src], [dst])
                    for (Pr, Pi, Lr, Li, nk) in ((PWre, PWim, LKre, LKim, 9), (NPre, NPim, ILre, ILim, 8)):
                        dve(lambda e: e.memset(Pr.t[:, 0:8], 1.0), [], [Pr])
                        dve(lambda e: e.memset(Pi.t[:, 0:8], 0.0), [], [Pi])
                        for k in range(nk - 1):
                            a, b = Pr.t[:, k * 8:(k + 1) * 8], Pi.t[:, k * 8:(k + 1) * 8]
                            a2, b2 = Pr.t[:, (k + 1) * 8:(k + 2) * 8], Pi.t[:, (k + 1) * 8:(k + 2) * 8]
                            TT(a2, a, V(Lr), ALU.mult, [Pr, Lr], [Pr])
                            TT(V(t8), b, V(Li), ALU.mult, [Pi, Li], [t8])
                            TT(a2, a2, V(t8), ALU.subtract, [Pr, t8], [Pr])
                            TT(b2, a, V(Li), ALU.mult, [Pr, Li], [Pi])
                            TT(V(t8), b, V(Lr), ALU.mult, [Pi, Lr], [t8])
                            TT(b2, b2, V(t8), ALU.add, [Pi, t8], [Pi])


            seti = 0
            prep(0)
            for kc in range(KC):
                for gl in range(8):
                    ps = self.nextps()
                    for i in range(8):
                        self.op("pe", lambda e, i=i: e.matmul(ps.t[:, 0:NCH], lhsT=selw(gl, i),
                                                             rhs=HT.t[:, kc * T + i:(kc + 1) * T:8], start=(i == 0), stop=(i == 7)),
                                r=[Sel] + self.HTb, w=[ps], sig=(i == 7))
                    self.op("act", lambda e: e.activation(out=U[gl].t[:, :], in_=ps.t[:, 0:NCH], func=AF.Identity), r=[ps], w=[U[gl]])
                def factor(d, Are, Aim, Pr, Pi, kbase, kstep):
                    cc0 = d * 4
                    a_re = self.ap4(Are, cc0 * 16, [[16, 4], [0, 8], [1, 16]])
                    a_im = self.ap4(Aim, cc0 * 16, [[16, 4], [0, 8], [1, 16]])
                    p_re = self.ap4(Pr, kbase * 8 + cc0, [[1, 4], [kstep * 8, 8], [0, 16]])
                    p_im = self.ap4(Pi, kbase * 8 + cc0, [[1, 4], [kstep * 8, 8], [0, 16]])
                    o_re = FAre.t[:, :].rearrange("p (c i k) -> p c i k", c=4, i=8)
                    o_im = FAim.t[:, :].rearrange("p (c i k) -> p c i k", c=4, i=8)
                    tmp = tt3.t[:, :].rearrange("p (c i k) -> p c i k", c=4, i=8)
                    TT(o_re, a_re, p_re, ALU.mult, [Are, Pr], [FAre], eng="pool")
                    TT(tmp, a_im, p_im, ALU.mult, [Aim, Pi], [tt3], eng="pool")
                    TT(o_re, o_re, tmp, ALU.subtract, [FAre, tt3], [FAre], eng="pool")
                    TT(o_im, a_re, p_im, ALU.mult, [Are, Pi], [FAim], eng="pool")
                    TT(tmp, a_im, p_re, ALU.mult, [Aim, Pr], [tt3], eng="pool")
                    TT(o_im, o_im, tmp, ALU.add, [FAim, tt3], [FAim], eng="pool")

                def half_copy(dst_tiles, src, pq, scale, eng_alt=0):
                    for g2 in range(2):
                        self.op("act", lambda e, g2=g2: e.activation(out=dst_tiles[g2].t[g2 * 64:(g2 + 1) * 64, :],
                                                                     in_=src.t[g2 * 64:(g2 + 1) * 64, pq * 128:(pq + 1) * 128],
                                                                     func=AF.Identity, scale=scale), r=[src], w=[dst_tiles[g2]])

                for d in range(2):
                    c0 = d * 64 + 4 * kc
                    tbo = 0 if d == 0 else 64
                    tre3 = TBre.t[:, :].rearrange("p (c j) -> p c j", j=NCH)[:, :, tbo:tbo + 256]
                    tim3 = TBim.t[:, :].rearrange("p (c j) -> p c j", j=NCH)[:, :, tbo:tbo + 256]
                    dve(lambda e: e.memset(tre3[:, :, 0:1], 1.0), [], [TBre])
                    dve(lambda e: e.memset(tim3[:, :, 0:1], 0.0), [], [TBim])
                    for k in range(8):
                        s = 1 << k
                        wr = self.bc(CWre.t[:, (3 + k) * 128 + c0:(3 + k) * 128 + c0 + 4], s)
                        wi = self.bc(CWim.t[:, (3 + k) * 128 + c0:(3 + k) * 128 + c0 + 4], s)
                        a = tre3[:, :, 0:s]
                        b = tim3[:, :, 0:s]
                        u1 = tt1.t[:, 0:4 * s].rearrange("p (c j) -> p c j", j=s)
                        u2 = tt2.t[:, 0:4 * s].rearrange("p (c j) -> p c j", j=s)
                        TT(u1, a, wr, ALU.mult, [TBre, CWre], [tt1])
                        TT(u2, b, wi, ALU.mult, [TBim, CWim], [tt2])
                        TT(tre3[:, :, s:2 * s], u1, u2, ALU.subtract, [tt1, tt2], [TBre])
                        TT(u1, a, wi, ALU.mult, [TBre, CWim], [tt1])
                        TT(u2, b, wr, ALU.mult, [TBim, CWre], [tt2])
                        TT(tim3[:, :, s:2 * s], u1, u2, ALU.add, [tt1, tt2], [TBim])
                    for TB_ in (TBre, TBim):
                        full3 = TB_.t[:, :].rearrange("p (c j) -> p c j", j=NCH)
                        for dst0 in ((256, 288) if d == 0 else (0, 32)):
                            self.op("pool", lambda e, full3=full3, dst0=dst0: e.tensor_copy(out=full3[:, :, dst0:dst0 + 32], in_=full3[:, :, tbo:tbo + 32]),
                                    r=[TB_], w=[TB_])
                    if d == 0:
                        factor(d, BBre, BBim, PWre, PWim, 7, -1)
                    else:
                        factor(d, BBre, BBim, PWre, PWim, 0, 1)
                    ws_sets = []
                    for pq in range(4):
                        half_copy([Zst[0], Zst[1]], FAre, pq, 1.0)
                        half_copy([Zst[2], Zst[3]], FAim, pq, 1.0)
                        wset = WS[seti % 2]
                        seti += 1
                        for q4 in range(4):
                            pst = self.nextps()
                            self.op("pe", lambda e: e.transpose(out=pst.t[:, 0:128], in_=Zst[q4].t[:, :], identity=self.ident.t[:, :]),
                                    r=[Zst[q4], self.ident], w=[pst])
                            self.op("dve", lambda e: e.tensor_copy(out=wset[q4].t[:, :], in_=pst.t[:, 0:128]), r=[pst], w=[wset[q4]])
                        if d == 1:
                            zt = ZT[seti % 2]
                            half_copy([zt[0], zt[1]], FAre, pq, 1.0)
                            half_copy([zt[2], zt[3]], FAim, pq, -1.0)
                            ws_sets.append((wset, zt))
                        else:
                            ws_sets.append((wset, None))
                        cc = d * 4 + pq
                        col = c0 + pq
                        psS = [self.nextps(), self.nextps()]
                        for ri_ in range(2):
                            for g2 in range(2):
                                self.op("pe", lambda e, g2=g2, ri_=ri_: e.matmul(psS[ri_].t[:, 0:NCH], lhsT=wset[ri_ * 2 + g2].t[:, :],
                                                                                  rhs=U[pq * 2 + g2].t[:, :], start=(g2 == 0), stop=(g2 == 1)),
                                        r=[wset[ri_ * 2 + g2], U[pq * 2 + g2]], w=[psS[ri_]], sig=(g2 == 1))
                        hbre, hbim = HBre[cc], HBim[cc]
                        rho = mag8.t[:, col:col + 1]
                        L = NCH
                        Tr = TBre.t[:, pq * NCH:(pq + 1) * NCH]
                        Ti = TBim.t[:, pq * NCH:(pq + 1) * NCH]
                        bre = psS[0].t[:, 0:L] if d == 0 else self.rev(psS[0].t[:, 0:L])
                        bim = psS[1].t[:, 0:L] if d == 0 else self.rev(psS[1].t[:, 0:L])
                        TT(m1.t[:, 0:L], bre, Tr, ALU.mult, [psS[0], TBre], [m1])
                        TT(m4.t[:, 0:L], bre, Ti, ALU.mult, [psS[0], TBim], [m4])
                        TT(m2.t[:, 0:L], bim, Ti, ALU.mult, [psS[1], TBim], [m2])
                        TT(m3.t[:, 0:L], bim, Tr, ALU.mult, [psS[1], TBre], [m3])
                        TT(zre.t[:, 0:L], m1.t[:, 0:L], m2.t[:, 0:L], ALU.subtract, [m1, m2], [zre], eng="pool")
                        TT(zim.t[:, 0:L], m3.t[:, 0:L], m4.t[:, 0:L], ALU.add, [m3, m4], [zim], eng="pool")
                        pS = 0 if d == 0 else 64
                        TT(zre.t[:, pS:pS + 1], zre.t[:, pS:pS + 1], I8re.t[:, col:col + 1], ALU.add, [zre, I8re], [zre])
                        TT(zim.t[:, pS:pS + 1], zim.t[:, pS:pS + 1], I8im.t[:, col:col + 1], ALU.add, [zim, I8im], [zim])
                        RHv = tt1.t[:, 0:NCH]
                        dve(lambda e: e.tensor_scalar(out=RHv, in0=RHm[d].t[:, :], scalar1=rho, scalar2=1.0, op0=ALU.mult, op1=ALU.mult),
                            [RHm[d], mag8], [tt1])
                        dve(lambda e: e.tensor_tensor_scan(out=hsre.t[:, 0:L], data0=RHv, data1=zre.t[:, 0:L],
                                                           initial=0.0, op0=ALU.mult, op1=ALU.add), [tt1, zre], [hsre])
                        dve(lambda e: e.tensor_tensor_scan(out=hsim.t[:, 0:L], data0=RHv, data1=zim.t[:, 0:L],
                                                           initial=0.0, op0=ALU.mult, op1=ALU.add), [tt1, zim], [hsim])
                        TT(m1.t[:, 0:L], hsre.t[:, 0:L], Tr, ALU.mult, [hsre, TBre], [m1])
                        TT(m2.t[:, 0:L], hsim.t[:, 0:L], Ti, ALU.mult, [hsim, TBim], [m2])
                        TT(m3.t[:, 0:L], hsim.t[:, 0:L], Tr, ALU.mult, [hsim, TBre], [m3])
                        TT(m4.t[:, 0:L], hsre.t[:, 0:L], Ti, ALU.mult, [hsre, TBim], [m4])
                        ore = hbre.t[:, 0:L] if d == 0 else self.rev(hbre.t[:, 0:L])
                        oim = hbim.t[:, 0:L] if d == 0 else self.rev(hbim.t[:, 0:L])
                        TT(ore, m1.t[:, 0:L], m2.t[:, 0:L], ALU.add, [m1, m2], [hbre], eng="pool")
                        TT(oim, m3.t[:, 0:L], m4.t[:, 0:L], ALU.subtract, [m3, m4], [hbim], eng="pool")
                        for sq_ in (1, 2):
                            lp = (287 if sq_ == 1 else 319) if d == 0 else (63 if sq_ == 1 else 31)
                            fcol = ((sq_ - 1) * 2 + d) * 64 + (4 * kc + pq)
                            TT(FSre.t[:, fcol:fcol + 1], m1.t[:, lp:lp + 1], m2.t[:, lp:lp + 1], ALU.add, [m1, m2], [FSre])
                            TT(FSim.t[:, fcol:fcol + 1], m3.t[:, lp:lp + 1], m4.t[:, lp:lp + 1], ALU.subtract, [m3, m4], [FSim])
                    if d == 0:
                        factor(d, BBre, BBim, NPre, NPim, 0, 1)
                        for pq in range(4):
                            zt = ZT[(seti + pq) % 2]
                            half_copy([zt[0], zt[1]], FAre, pq, 1.0)
                            half_copy([zt[2], zt[3]], FAim, pq, -1.0)
                            ws_sets[pq] = (ws_sets[pq][0], zt)
                            self._s5_tgen_pending = None
                    if d == 0:
                        factor(d, CTre, CTim, PWre, PWim, 0, 1)
                    else:
                        factor(d, CTre, CTim, NPre, NPim, 0, 1)
                    for pq in range(4):
                        yt = YTn[pq % 2]
                        self.op("act", lambda e: e.activation(out=yt[0].t[:, :], in_=FAre.t[:, pq * 128:(pq + 1) * 128], func=AF.Identity), r=[FAre], w=[yt[0]])
                        self.op("act", lambda e: e.activation(out=yt[1].t[:, :], in_=FAim.t[:, pq * 128:(pq + 1) * 128], func=AF.Identity), r=[FAim], w=[yt[1]])
                        zt = ws_sets[pq][1]
                        for g2 in range(2):
                            gl = pq * 2 + g2
                            pT = self.nextps()
                            self.op("pe", lambda e: e.matmul(pT.t[:, 0:128], lhsT=zt[g2].t[:, :], rhs=yt[0].t[:, :], start=True, stop=False),
                                    r=[zt[g2], yt[0]], w=[pT], sig=False)
                            self.op("pe", lambda e: e.matmul(pT.t[:, 0:128], lhsT=zt[2 + g2].t[:, :], rhs=yt[1].t[:, :], start=False, stop=True),
                                    r=[zt[2 + g2], yt[1]], w=[pT])
                            if d == 0:
                                TT(Tacc[gl].t[:, :], pT.t[:, 0:128], V(maskF), ALU.mult, [pT, maskF], [Tacc[gl]])
                            else:
                                TT(tt1.t[:, 0:128], pT.t[:, 0:128], V(maskB), ALU.mult, [pT, maskB], [tt1])
                                TT(Tb[gl].t[:, :], Tacc[gl].t[:, :], tt1.t[:, 0:128], ALU.add, [Tacc[gl], tt1], [Tb[gl]], eng="pool")
                    if d == 0:
                        factor(d, CTre, CTim, PWre, PWim, 1, 1)
                    else:
                        factor(d, CTre, CTim, PWre, PWim, 8, -1)
                    for pq in range(4):
                        zo = ZO[d * 4 + pq]
                        half_copy([zo[0], zo[1]], FAre, pq, 1.0)
                        half_copy([zo[2], zo[3]], FAim, pq, -1.0)
                if kc + 1 < KC:
                    prep(kc + 1)
                for gl in range(8):
                    pq, g2 = gl // 2, gl % 2
                    pY = self.nextps()
                    self.op("pe", lambda e: e.matmul(pY.t[:, 0:NCH], lhsT=Tb[gl].t[:, :], rhs=U[gl].t[:, :], start=True, stop=False),
                            r=[Tb[gl], U[gl]], w=[pY], sig=False)
                    for d in range(2):
                        cc = d * 4 + pq
                        zo = ZO[cc]
                        col = d * 64 + 4 * kc + pq
                        pieces = []
                        for sq_ in range(3):
                            n0, L = SEQC[sq_]
                            if d == 0:
                                pieces.append((n0 + 1, L - 1, HBre[cc].t[:, n0:n0 + L - 1], HBim[cc].t[:, n0:n0 + L - 1], [HBre[cc]], [HBim[cc]]))
                            else:
                                pieces.append((n0, L - 1, HBre[cc].t[:, n0 + 1:n0 + L], HBim[cc].t[:, n0 + 1:n0 + L], [HBre[cc]], [HBim[cc]]))
                        ic = 0 if d == 0 else 255
                        pieces.append((ic, 1, H0bre.t[:, col:col + 1], H0bim.t[:, col:col + 1], [H0bre], [H0bim]))
                        for pi_, (o0, Ln, rre, rim, bre_, bim_) in enumerate(pieces):
                            lastmm = (d == 1 and pi_ == len(pieces) - 1)
                            self.op("pe", lambda e: e.matmul(pY.t[:, o0:o0 + Ln], lhsT=zo[g2].t[:, :], rhs=rre, start=False, stop=False),
                                    r=[zo[g2]] + bre_, w=[pY], sig=False)
                            self.op("pe", lambda e: e.matmul(pY.t[:, o0:o0 + Ln], lhsT=zo[2 + g2].t[:, :], rhs=rim, start=False, stop=lastmm),
                                    r=[zo[2 + g2]] + bim_, w=[pY], sig=lastmm)
                    self.op("act", lambda e: e.activation(out=Yhi[gl].t[:, :], in_=pY.t[:, 0:NCH], func=AF.Identity), r=[pY], w=[Yhi[gl]])
                    TT(Ylo[gl].t[:, :], pY.t[:, 0:NCH], Yhi[gl].t[:, :], ALU.subtract, [pY, Yhi[gl]], [Ylo[gl]])
                for j in range(8):
                    pU = self.nextps()
                    for gl in range(8):
                        self.op("pe", lambda e: e.matmul(pU.t[:, 0:NCH], lhsT=selw(j, gl), rhs=Yhi[gl].t[:, :],
                                                         start=(gl == 0), stop=False), r=[Sel, Yhi[gl]], w=[pU], sig=False)
                        self.op("pe", lambda e: e.matmul(pU.t[:, 0:NCH], lhsT=selw(j, gl), rhs=Ylo[gl].t[:, :],
                                                         start=False, stop=(gl == 7)), r=[Sel, Ylo[gl]], w=[pU], sig=(gl == 7))
                    self.op("act", lambda e: e.activation(out=YK.t[:, j:T:8], in_=pU.t[:, 0:NCH], func=AF.Identity), r=[pU], w=[YK])
                self.dma("sp", self.YT[kc], YK.t[:, :], r=[YK], w=self.YTb)
            for (FS, dst) in ((FSre, self.new_sre), (FSim, self.new_sim)):
                for hlf in range(2):
                    pst = self.nextps()
                    self.op("pe", lambda e, hlf=hlf, FS=FS: e.transpose(out=pst.t[:, 0:128], in_=FS.t[:, hlf * 128:(hlf + 1) * 128], identity=self.ident.t[:, :]),
                            r=[FS, self.ident], w=[pst])
                    self.op("act", lambda e: e.activation(out=fso.t[:, :], in_=pst.t[:, 0:128], func=AF.Identity), r=[pst], w=[fso])
                    self.dma("sp", dst[hlf * 128:(hlf + 1) * 128, :], fso.t[:, :], r=[fso], w=[self.OUTb])
            self.barrier()

    def phase_glu(self):
        l = 1
        C1 = 2.0 * 0.7978845608028654
        with ExitStack() as ph:
            xt = self.sb(ph, "gx", [128, KC * TW], F32)
            yt = self.sb(ph, "gy", [128, KC * TW], F32)
            GB = self.sb(ph, "gb", [128, KC * TW], BF16)
            wblk = [self.sb(ph, f"gw{i}", [128, KC * 512], BF16) for i in range(2)]
            ta = [self.sb(ph, f"gta{i}", [128, TW], F32) for i in range(2)]
            tb_ = [self.sb(ph, f"gtb{i}", [128, TW], F32) for i in range(2)]
            sg = [self.sb(ph, f"gsg{i}", [128, TW], F32) for i in range(2)]
            DS = self.sb(ph, "DS", [128, 16], F32)
            BG = self.sb(ph, "BG", [128, 16], F32)
            rst = self.sb(ph, "rst", [128, TW], F32)
            self.load_fm("ds", self.ssm_d.rearrange("(k p) -> k p", p=128), 16, DS.t[:, :], DS, self.INb)
            self.load_fm("bg", self.b_glu.rearrange("(k p) -> k p", p=128), 16, BG.t[:, :], BG, self.INb)
            wi = 0
            for tt in range(NT):
                n = 0 if tt < 4 else 1
                tok = slice(tt * TW, (tt + 1) * TW)
                self.dma("sp", xt.t[:, :].rearrange("p (k t) -> p k t", k=KC), self.xt_dram(tt), r=[self.XTb[tt]], w=[xt])
                self.dma("sp", yt.t[:, :].rearrange("p (k t) -> p k t", k=KC),
                         self.YT[:, :, tt * TW:(tt + 1) * TW].rearrange("k p t -> p k t"), r=[self.YTb[tt]], w=[yt])
                self.dma("sp", rst.t[:, :], self.RSD[:, tt * TW:(tt + 1) * TW], r=[self.RSDb], w=[rst])
                for kc in range(KC):
                    a, b = ta[kc % 2], tb_[kc % 2]
                    ks = slice(kc * TW, (kc + 1) * TW)
                    self.op("dve", lambda e: e.scalar_tensor_tensor(out=a.t[:, :], in0=xt.t[:, ks], scalar=self.av(l, 0, kc, n), in1=rst.t[:, :],
                                                                    op0=ALU.mult, op1=ALU.mult), r=[xt, rst, self.AV], w=[a])
                    self.op("act", lambda e: e.activation(out=a.t[:, :], in_=a.t[:, :], func=AF.Identity, bias=self.modv(l, 0, kc, n), scale=1.0),
                            r=[a, self.MOD], w=[a])
                    self.op("dve", lambda e: e.scalar_tensor_tensor(out=yt.t[:, ks], in0=a.t[:, :], scalar=DS.t[:, kc:kc + 1], in1=yt.t[:, ks],
                                                                    op0=ALU.mult, op1=ALU.add), r=[a, DS, yt], w=[yt])
                    self.op("dve", lambda e: e.tensor_tensor(out=b.t[:, :], in0=yt.t[:, ks], in1=yt.t[:, ks], op=ALU.mult), r=[yt], w=[b])
                    self.op("dve", lambda e: e.tensor_scalar(out=b.t[:, :], in0=b.t[:, :], scalar1=0.044715, scalar2=1.0, op0=ALU.mult, op1=ALU.add),
                            r=[b], w=[b])
                    self.op("dve", lambda e: e.tensor_tensor(out=b.t[:, :], in0=b.t[:, :], in1=yt.t[:, ks], op=ALU.mult), r=[b, yt], w=[b])
                    self.op("act", lambda e: e.activation(out=b.t[:, :], in_=b.t[:, :], func=AF.Sigmoid, scale=C1), r=[b], w=[b])
                    self.op("dve", lambda e: e.tensor_tensor(out=yt.t[:, ks], in0=yt.t[:, ks], in1=b.t[:, :], op=ALU.mult), r=[b, yt], w=[yt])
                    self.op("act", lambda e: e.activation(out=GB.t[:, ks], in_=yt.t[:, ks], func=AF.Identity), r=[yt], w=[GB])
                for mcb in range(4):
                    wb = wblk[wi % 2]
                    wi += 1
                    self.dma("pool", wb.t[:, :].rearrange("p (k m) -> p k m", k=KC),
                             self.w_glu[:, mcb * 512:(mcb + 1) * 512].rearrange("(k p) m -> p k m", p=128), r=[self.INb], w=[wb])
                    for m2 in range(4):
                        mc = mcb * 4 + m2
                        ms_ = slice(mc * TW, (mc + 1) * TW)
                        ps = self.nextps()
                        for kc in range(KC):
                            self.op("pe", lambda e, kc=kc: e.matmul(ps.t[:, :], lhsT=wb.t[:, kc * 512 + m2 * 128: kc * 512 + (m2 + 1) * 128],
                                                                  rhs=GB.t[:, kc * TW:(kc + 1) * TW], start=(kc == 0), stop=(kc == KC - 1)),
                                    r=[wb, GB], w=[ps], sig=(kc == KC - 1))
                        s_ = sg[mc % 2]
                        self.op("act", lambda e: e.activation(out=s_.t[:, :], in_=ps.t[:, :], func=AF.Sigmoid, bias=BG.t[:, mc:mc + 1], scale=1.0),
                                r=[ps, BG], w=[s_])
                        self.op("dve", lambda e: e.tensor_tensor(out=s_.t[:, :], in0=s_.t[:, :], in1=yt.t[:, ms_], op=ALU.mult), r=[s_, yt], w=[s_])
                        self.op("dve", lambda e: e.scalar_tensor_tensor(out=xt.t[:, ms_], in0=s_.t[:, :], scalar=self.modv(l, 2, mc, n), in1=xt.t[:, ms_],
                                                                        op0=ALU.mult, op1=ALU.add), r=[s_, xt, self.MOD], w=[xt])
                self.dma("sp", self.xt_dram(tt), xt.t[:, :].rearrange("p (k t) -> p k t", k=KC), r=[xt], w=[self.XTb[tt]])
            self.barrier()


def _rope_consts():
    half = 64
    inv = (10000.0 ** (-np.arange(0, half, 2, dtype=np.float32) / np.float32(half))).astype(np.float32)
    r, col = np.meshgrid(np.arange(32), np.arange(64), indexing="ij")
    r = r.reshape(-1).astype(np.float32)
    col = col.reshape(-1).astype(np.float32)
    ar = r[:, None] * inv
    ac = col[:, None] * inv
    ang = np.concatenate([ar, ar, ac, ac], axis=-1)
    cosT = np.ascontiguousarray(np.cos(ang).T.astype(np.float32))
    sinT = np.ascontiguousarray(np.sin(ang).T.astype(np.float32))
    P = np.zeros((128, 128), np.float32)
    for a in range(2):
        for i in range(32):
            P[a * 64 + i, a * 64 + 32 + i] = -1.0
            P[a * 64 + 32 + i, a * 64 + i] = 1.0
    return cosT, sinT, np.ascontiguousarray(P.T)


def _s5_consts():
    sel = np.zeros((128, 8, 240), np.float32)
    for a in range(8):
        for c in range(16):
            sel[a * 16 + c, a, 112 + c] = 1.0
    ii = np.arange(128) // 16
    maskf = (ii[None, :] >= ii[:, None]).astype(np.float32)
    maskb = (ii[:, None] >= ii[None, :]).astype(np.float32)
    return np.ascontiguousarray(sel.reshape(128, 8 * 240)), maskf, maskb


_SEL, _MASKF, _MASKB = _s5_consts()
_CACHE = {}


def _get_prog(debug=False, stop_after=None):
    key = (debug, stop_after)
    if key not in _CACHE:
        mk = MK(debug=debug, stop_after=stop_after)
        mk.build()
        _CACHE[key] = mk
    return _CACHE[key]


def make_in_maps(inp, cores):
    cosT, sinT, ropeT = _rope_consts()
    ident = np.eye(128, dtype=np.float32)
    f = lambda a: np.ascontiguousarray(np.asarray(a, dtype=np.float32))
    shared = {
        "w_mod": f(inp["w_mod"]), "b_mod": f(inp["b_mod"]), "norm_g": f(inp["norm_g"]),
        "w_qkv": f(inp["w_qkv"][0]), "lam_vecs": f(inp["lam_vecs"][0]), "subln_g": f(inp["subln_g"][0]),
        "w_o": f(inp["w_o"][0]),
        "ssm_a_re": f(inp["ssm_a_re"][0]), "ssm_a_im": f(inp["ssm_a_im"][0]), "ssm_log_step": f(inp["ssm_log_step"][0]),
        "ssm_b_re": f(inp["ssm_b_re"][0]), "ssm_b_im": f(inp["ssm_b_im"][0]),
        "ssm_c_re": f(inp["ssm_c_re"][0]), "ssm_c_im": f(inp["ssm_c_im"][0]),
        "ssm_d": f(inp["ssm_d"][0]), "w_glu": f(inp["w_glu"][0]), "b_glu": f(inp["b_glu"][0]),
        "w_up": f(inp["w_up"]), "conv_w": f(inp["conv_w"]), "conv_b": f(inp["conv_b"]), "w_down": f(inp["w_down"]),
        "final_g": f(inp["final_g"]),
        "c_ident": ident, "c_ropeT": ropeT, "c_cos": cosT, "c_sin": sinT,
        "c_sel": _SEL, "c_maskf": _MASKF, "c_maskb": _MASKB,
    }
    maps = []
    for c in cores:
        m = dict(shared)
        m["x_all"] = np.ascontiguousarray(np.concatenate(
            [f(inp["x_sample"][c]), f(inp["x_prompt"][2 * c]), f(inp["x_prompt"][2 * c + 1])], axis=0))
        m["cond"] = np.ascontiguousarray(np.stack([f(inp["c"][c]), f(inp["c_ctx"])], axis=0))
        m["cache_k"] = f(inp["cache_k"][c, 0]).reshape(256, 2048)
        m["cache_v"] = f(inp["cache_v"][c, 0]).reshape(256, 2048)
        m["st_re"] = f(inp["state_re"][c, 0])
        m["st_im"] = f(inp["state_im"][c, 0])
        maps.append(m)
    return maps


def kernel(**inputs):
    mk = _get_prog()
    cores = list(range(NCORES))
    maps = make_in_maps(inputs, cores)
    res = run_bass_kernel_spmd(mk.nc, maps, core_ids=cores)
    R = res.results
    y_prompt = np.zeros((16, 256, D), np.float32)
    y_sample = np.zeros((8, 2048, D), np.float32)
    nk = np.zeros((16, 1, 256, 8, 2, 128), np.float32)
    nv = np.zeros((16, 1, 256, 8, 256), np.float32)
    sre = np.zeros((16, 1, 2, 128, 64), np.float32)
    sim = np.zeros((16, 1, 2, 128, 64), np.float32)
    for c in cores:
        r = R[c]
        ya = np.asarray(r["y_all"])
        y_sample[c] = ya[0:2048]
        y_prompt[2 * c] = ya[2048:2304]
        y_prompt[2 * c + 1] = ya[2304:2560]
        k = np.asarray(r["new_k"]).reshape(2, 256, 8, 2, 128)
        v = np.asarray(r["new_v"]).reshape(2, 256, 8, 256)
        nk[2 * c:2 * c + 2, 0] = k
        nv[2 * c:2 * c + 2, 0] = v
        a = np.asarray(r["new_sre"]).reshape(2, 2, 64, 2, 64).reshape(2, 2, 128, 64)
        b = np.asarray(r["new_sim"]).reshape(2, 2, 64, 2, 64).reshape(2, 2, 128, 64)
        sre[2 * c:2 * c + 2, 0] = a
        sim[2 * c:2 * c + 2, 0] = b
    return (y_prompt, y_sample, nk, nv, sre, sim)
```
